# Optimizing a Trainium2 kernel written in Bass

```python
import math
import jax, jax.numpy as jnp
from jax import lax
import numpy as np

D_MODEL = 1024
BATCH = 8
SEQ = 2048
DEPTH = 2

GRID_W = 64
CTX_LEN = 256
HEAD_DIM = 64
A_HEADS = 4
B_HEADS = 8
B_KV_HEADS = 2
B_GROUP = B_HEADS // B_KV_HEADS
LRU_WIDTH = D_MODEL
LRU_BLOCKS = 8
LRU_BLOCK = LRU_WIDTH // LRU_BLOCKS
LRU_C = 8.0
LRU_CONV = 4
D_FF = 2816
FFN_CONV = 3
Q_BLOCK = 128
ROPE_BASE = 10000.0
EPS = 1e-6

A_QK = A_HEADS * 2 * HEAD_DIM
A_V = A_HEADS * 2 * HEAD_DIM
B_Q = B_HEADS * HEAD_DIM
B_KV = B_KV_HEADS * HEAD_DIM
ATTN_IN = 2 * A_QK + A_V + B_Q + 2 * B_KV
ATTN_OUT = A_V + B_Q
ATTN_SPLITS = [A_QK, 2 * A_QK, 2 * A_QK + A_V, 2 * A_QK + A_V + B_Q, 2 * A_QK + A_V + B_Q + B_KV]

kernel_name = 'hybrid_diffattn_gqa_rglru_convffn_prefix'


def rmsnorm(x, g):
    xf = x.astype(jnp.float32)
    y = xf * lax.rsqrt(jnp.mean(xf * xf, axis=-1, keepdims=True) + EPS)
    return (y * g.astype(jnp.float32)).astype(x.dtype)


def modulation(cvec, w_ada, b_ada):
    m = jax.nn.silu(cvec) @ w_ada + b_ada
    return [t[:, None, :] for t in jnp.split(m, 6, axis=-1)]


def modulate(x, g, shift, scale):
    return rmsnorm(x, g) * (1 + scale) + shift


def rope_tables(rows):
    pairs = HEAD_DIM // 4
    inv = ROPE_BASE ** (-jnp.arange(pairs, dtype=jnp.float32) / pairs)
    row = jnp.repeat(jnp.arange(rows, dtype=jnp.float32), GRID_W)
    col = jnp.tile(jnp.arange(GRID_W, dtype=jnp.float32), rows)
    ang = jnp.concatenate([row[:, None] * inv, col[:, None] * inv], axis=-1)
    return jnp.cos(ang), jnp.sin(ang)


def apply_rope(x, cos, sin):
    xp = x.reshape(x.shape[:-1] + (HEAD_DIM // 2, 2))
    x0, x1 = xp[..., 0], xp[..., 1]
    c = cos[None, :, None, :].astype(x.dtype)
    s = sin[None, :, None, :].astype(x.dtype)
    return jnp.stack([x0 * c - x1 * s, x0 * s + x1 * c], axis=-1).reshape(x.shape)


def dwconv(x, w, b, left):
    k = w.shape[0]
    y = lax.conv_general_dilated(x, w[:, None, :].astype(x.dtype), window_strides=(1,),
                                 padding=[(left, k - 1 - left)],
                                 dimension_numbers=('NWC', 'WIO', 'NWC'),
                                 feature_group_count=x.shape[-1])
    return y + b


def diff_attend(q1, q2, k1, k2, v, lam):
    scale = HEAD_DIM ** -0.5
    p1 = jax.nn.softmax(jnp.einsum('bqhd,bkhd->bhqk', q1, k1).astype(jnp.float32) * scale, axis=-1)
    p2 = jax.nn.softmax(jnp.einsum('bqhd,bkhd->bhqk', q2, k2).astype(jnp.float32) * scale, axis=-1)
    p = (p1 - lam * p2).astype(v.dtype)
    return jnp.einsum('bhqk,bkhe->bqhe', p, v)


def gqa_attend(q, k, v):
    b, nq = q.shape[:2]
    qg = q.reshape(b, nq, B_KV_HEADS, B_GROUP, HEAD_DIM)
    s = jnp.einsum('bqgrd,bkgd->bgrqk', qg, k).astype(jnp.float32) * (HEAD_DIM ** -0.5)
    p = jax.nn.softmax(s, axis=-1).astype(v.dtype)
    o = jnp.einsum('bgrqk,bkgd->bqgrd', p, v)
    return o.reshape(b, nq, B_Q)


def sweep_blocks(fn, *qs):
    b, n = qs[0].shape[:2]
    nb = n // Q_BLOCK
    blk = tuple(jnp.moveaxis(q.reshape((b, nb, Q_BLOCK) + q.shape[2:]), 1, 0) for q in qs)
    out = lax.map(lambda t: fn(*t), blk)
    return jnp.moveaxis(out, 0, 1).reshape((b, n) + out.shape[3:])


def attn_heads(h, w_in, q_norm, k_norm, rope):
    b, n, _ = h.shape
    qa, ka, va, qb, kb, vb = jnp.split(h @ w_in, ATTN_SPLITS, axis=-1)
    qa = qa.reshape(b, n, A_HEADS, 2, HEAD_DIM)
    ka = ka.reshape(b, n, A_HEADS, 2, HEAD_DIM)
    q1, q2, k1, k2 = qa[..., 0, :], qa[..., 1, :], ka[..., 0, :], ka[..., 1, :]
    va = va.reshape(b, n, A_HEADS, 2 * HEAD_DIM)
    qb = rmsnorm(qb.reshape(b, n, B_HEADS, HEAD_DIM), q_norm)
    kb = rmsnorm(kb.reshape(b, n, B_KV_HEADS, HEAD_DIM), k_norm)
    vb = vb.reshape(b, n, B_KV_HEADS, HEAD_DIM)
    if rope is not None:
        cos, sin = rope
        q1, q2, k1, k2, qb, kb = (apply_rope(t, cos, sin) for t in (q1, q2, k1, k2, qb, kb))
    return q1, q2, qb, k1, k2, va, kb, vb


def attn_mixer(h_lat, h_ctx, rope, layer_idx, ctx_out, w_in, lam_q1, lam_k1, lam_q2, lam_k2,
               subln, q_norm, k_norm, w_o):
    f32 = jnp.float32
    lam_init = 0.8 - 0.6 * math.exp(-0.3 * layer_idx)
    lam = (jnp.exp(jnp.sum(lam_q1.astype(f32) * lam_k1.astype(f32)))
           - jnp.exp(jnp.sum(lam_q2.astype(f32) * lam_k2.astype(f32))) + lam_init)
    lat = attn_heads(h_lat, w_in, q_norm, k_norm, rope)
    ctx = attn_heads(h_ctx, w_in, q_norm, k_norm, None)
    keys = tuple(jnp.concatenate([kl, kc], axis=1) for kl, kc in zip(lat[3:], ctx[3:]))

    def heads_out(q1, q2, qb, k1, k2, va, kb, vb):
        b, nq = q1.shape[:2]
        oa = rmsnorm(diff_attend(q1, q2, k1, k2, va, lam), subln) * (1 - lam_init)
        ob = gqa_attend(qb, kb, vb)
        return jnp.concatenate([oa.reshape(b, nq, A_V), ob], axis=-1)

    y_lat = sweep_blocks(lambda q1, q2, qb: heads_out(q1, q2, qb, *keys), *lat[:3]) @ w_o
    y_ctx = heads_out(*ctx) @ w_o if ctx_out else None
    return y_lat, y_ctx


def lru_coeffs(u, wa, ba, wx, bx, a_param):
    ub = u.reshape(u.shape[:2] + (LRU_BLOCKS, LRU_BLOCK))
    r = jax.nn.sigmoid(jnp.einsum('bnki,kij->bnkj', ub, wa).reshape(u.shape) + ba)
    i = jax.nn.sigmoid(jnp.einsum('bnki,kij->bnkj', ub, wx).reshape(u.shape) + bx)
    log_a = -LRU_C * r * jax.nn.softplus(-a_param)
    return jnp.exp(log_a), jnp.sqrt(-jnp.expm1(2 * log_a)) * (i * u)


def linear_scan(a, b, h0, reverse, keep_states):
    def step(h, ab):
        h = ab[0] * h + ab[1]
        return h, (h if keep_states else None)
    h_end, hs = lax.scan(step, h0, (jnp.swapaxes(a, 0, 1), jnp.swapaxes(b, 0, 1)), reverse=reverse)
    return (jnp.swapaxes(hs, 0, 1) if keep_states else None), h_end


def lru_mixer(h_lat, h_ctx, ctx_out, w_in, conv_w, conv_b, gate_a_w, gate_a_b, gate_x_w, gate_x_b,
              a_param, w_o):
    w_gate, w_rec = w_in[:, :LRU_WIDTH], w_in[:, LRU_WIDTH:]
    u_lat = dwconv(h_lat @ w_rec, conv_w, conv_b, LRU_CONV // 2)
    u_ctx = dwconv(h_ctx @ w_rec, conv_w, conv_b, LRU_CONV // 2)
    zeros = jnp.zeros((h_lat.shape[0], LRU_WIDTH), u_lat.dtype)

    def direction(d, reverse):
        a_c, b_c = lru_coeffs(u_ctx, gate_a_w[d], gate_a_b[d], gate_x_w[d], gate_x_b[d], a_param[d])
        hs_c, h_end = linear_scan(a_c, b_c, zeros, reverse, ctx_out)
        a_l, b_l = lru_coeffs(u_lat, gate_a_w[d], gate_a_b[d], gate_x_w[d], gate_x_b[d], a_param[d])
        hs_l, _ = linear_scan(a_l, b_l, h_end, reverse, True)
        return hs_l, hs_c

    lat_f, ctx_f = direction(0, False)
    lat_b, ctx_b = direction(1, True)
    y_lat = ((lat_f + lat_b) * jax.nn.gelu(h_lat @ w_gate)) @ w_o
    y_ctx = ((ctx_f + ctx_b) * jax.nn.gelu(h_ctx @ w_gate)) @ w_o if ctx_out else None
    return y_lat, y_ctx


def conv_ffn(h, w_up, conv_w, conv_b, w_down):
    u = dwconv(h @ w_up, conv_w, conv_b, FFN_CONV // 2)
    val, gate = jnp.split(u, [D_FF], axis=-1)
    return (jax.nn.silu(gate) * val) @ w_down


def setup_inputs(seed: int = 0) -> dict:
    key = jax.random.key(seed)
    ks = iter(jax.random.split(key, 64))
    D = D_MODEL

    def nrm(shape, s):
        return jax.random.normal(next(ks), shape, jnp.float32) * s

    def gain(n):
        return 1.0 + nrm((n,), 0.01)

    def lru_a_param():
        u = jax.random.uniform(next(ks), (2, LRU_WIDTH), jnp.float32, 0.9, 0.999)
        return -jnp.log(jnp.expm1(-jnp.log(u) / LRU_C))

    return {
        'x': nrm((BATCH, SEQ, D), 1.0),
        'c': nrm((BATCH, D), 1.0),
        'ctx': nrm((BATCH, CTX_LEN, D), 1.0),
        'c_ctx': nrm((D,), 1.0),
        'l0_ada_w': nrm((D, 6 * D), 0.5 * D ** -0.5),
        'l0_ada_b': nrm((6 * D,), 0.01),
        'l0_norm_mix': gain(D),
        'l0_norm_ffn': gain(D),
        'l0_w_in': nrm((D, ATTN_IN), D ** -0.5),
        'l0_lam_q1': nrm((HEAD_DIM,), 0.1),
        'l0_lam_k1': nrm((HEAD_DIM,), 0.1),
        'l0_lam_q2': nrm((HEAD_DIM,), 0.1),
        'l0_lam_k2': nrm((HEAD_DIM,), 0.1),
        'l0_subln': gain(2 * HEAD_DIM),
        'l0_q_norm': gain(HEAD_DIM),
        'l0_k_norm': gain(HEAD_DIM),
        'l0_w_o': nrm((ATTN_OUT, D), ATTN_OUT ** -0.5),
        'l0_ffn_up': nrm((D, 2 * D_FF), D ** -0.5),
        'l0_ffn_conv_w': nrm((FFN_CONV, 2 * D_FF), FFN_CONV ** -0.5),
        'l0_ffn_conv_b': nrm((2 * D_FF,), 0.01),
        'l0_ffn_down': nrm((D_FF, D), D_FF ** -0.5),
        'l1_ada_w': nrm((D, 6 * D), 0.5 * D ** -0.5),
        'l1_ada_b': nrm((6 * D,), 0.01),
        'l1_norm_mix': gain(D),
        'l1_norm_ffn': gain(D),
        'l1_w_in': nrm((D, 2 * LRU_WIDTH), D ** -0.5),
        'l1_conv_w': nrm((LRU_CONV, LRU_WIDTH), LRU_CONV ** -0.5),
        'l1_conv_b': nrm((LRU_WIDTH,), 0.01),
        'l1_gate_a_w': nrm((2, LRU_BLOCKS, LRU_BLOCK, LRU_BLOCK), LRU_BLOCK ** -0.5),
        'l1_gate_a_b': nrm((2, LRU_WIDTH), 0.01),
        'l1_gate_x_w': nrm((2, LRU_BLOCKS, LRU_BLOCK, LRU_BLOCK), LRU_BLOCK ** -0.5),
        'l1_gate_x_b': nrm((2, LRU_WIDTH), 0.01),
        'l1_a_param': lru_a_param(),
        'l1_w_o': nrm((LRU_WIDTH, D), LRU_WIDTH ** -0.5),
        'l1_ffn_up': nrm((D, 2 * D_FF), D ** -0.5),
        'l1_ffn_conv_w': nrm((FFN_CONV, 2 * D_FF), FFN_CONV ** -0.5),
        'l1_ffn_conv_b': nrm((2 * D_FF,), 0.01),
        'l1_ffn_down': nrm((D_FF, D), D_FF ** -0.5),
        'final_norm': gain(D),
    }


def reference(x, c, ctx, c_ctx,
              l0_ada_w, l0_ada_b, l0_norm_mix, l0_norm_ffn, l0_w_in, l0_lam_q1, l0_lam_k1, l0_lam_q2,
              l0_lam_k2, l0_subln, l0_q_norm, l0_k_norm, l0_w_o, l0_ffn_up, l0_ffn_conv_w, l0_ffn_conv_b,
              l0_ffn_down,
              l1_ada_w, l1_ada_b, l1_norm_mix, l1_norm_ffn, l1_w_in, l1_conv_w, l1_conv_b, l1_gate_a_w,
              l1_gate_a_b, l1_gate_x_w, l1_gate_x_b, l1_a_param, l1_w_o, l1_ffn_up, l1_ffn_conv_w,
              l1_ffn_conv_b, l1_ffn_down,
              final_norm):
    rows = x.shape[1] // GRID_W
    rope = rope_tables(rows)
    layers = (
        dict(ada_w=l0_ada_w, ada_b=l0_ada_b, norm_mix=l0_norm_mix, norm_ffn=l0_norm_ffn,
             ffn=(l0_ffn_up, l0_ffn_conv_w, l0_ffn_conv_b, l0_ffn_down),
             mix=dict(w_in=l0_w_in, lam_q1=l0_lam_q1, lam_k1=l0_lam_k1, lam_q2=l0_lam_q2,
                      lam_k2=l0_lam_k2, subln=l0_subln, q_norm=l0_q_norm, k_norm=l0_k_norm, w_o=l0_w_o)),
        dict(ada_w=l1_ada_w, ada_b=l1_ada_b, norm_mix=l1_norm_mix, norm_ffn=l1_norm_ffn,
             ffn=(l1_ffn_up, l1_ffn_conv_w, l1_ffn_conv_b, l1_ffn_down),
             mix=dict(w_in=l1_w_in, conv_w=l1_conv_w, conv_b=l1_conv_b, gate_a_w=l1_gate_a_w,
                      gate_a_b=l1_gate_a_b, gate_x_w=l1_gate_x_w, gate_x_b=l1_gate_x_b,
                      a_param=l1_a_param, w_o=l1_w_o)),
    )
    for l in range(DEPTH):
        p = layers[l]
        ctx_out = l < DEPTH - 1
        m_lat = modulation(c, p['ada_w'], p['ada_b'])
        m_ctx = modulation(c_ctx[None, :], p['ada_w'], p['ada_b'])
        h_lat = modulate(x, p['norm_mix'], m_lat[0], m_lat[1])
        h_ctx = modulate(ctx, p['norm_mix'], m_ctx[0], m_ctx[1])
        if l % 2 == 0:
            y_lat, y_ctx = attn_mixer(h_lat, h_ctx, rope, l, ctx_out, **p['mix'])
        else:
            y_lat, y_ctx = lru_mixer(h_lat, h_ctx, ctx_out, **p['mix'])
        x = x + m_lat[2] * y_lat
        x = x + m_lat[5] * conv_ffn(modulate(x, p['norm_ffn'], m_lat[3], m_lat[4]), *p['ffn'])
        if ctx_out:
            ctx = ctx + m_ctx[2] * y_ctx
            ctx = ctx + m_ctx[5] * conv_ffn(modulate(ctx, p['norm_ffn'], m_ctx[3], m_ctx[4]), *p['ffn'])
    return rmsnorm(x, final_norm)
```

```python
import math
import os
from contextlib import ExitStack
import numpy as np
import concourse.bass as bass
import concourse.mybir as mybir
from concourse.bass_utils import run_bass_kernel_spmd

F32 = mybir.dt.float32
BF16 = mybir.dt.bfloat16
AF = mybir.ActivationFunctionType
ALU = mybir.AluOpType
AX = mybir.AxisListType

D = 1024
KC = 8
NT = 18
T = 2304
DFF = 2816
NPAIR = 22
EPS = 1e-6


class Reg:
    __slots__ = ("w", "r")

    def __init__(self):
        self.w = None
        self.r = []


def regs(n):
    return [Reg() for _ in range(n)]


class Eng:
    def __init__(self, fw, e, name, is_dma=False):
        self.fw, self.e, self.name, self.is_dma = fw, e, name, is_dma
        self.count = 0
        self.waited = {}
        if not is_dma:
            self.sem = fw.nc.alloc_semaphore("s_" + name)
        else:
            self.nsem = 6
            self.sems = [fw.nc.alloc_semaphore("d_%s_%d" % (name, i)) for i in range(self.nsem)]

    def sem_val(self, idx):
        if not self.is_dma:
            return self.sem, idx
        i = idx - 1
        return self.sems[i % self.nsem], 16 * (i // self.nsem + 1)


class FW:
    def __init__(self, nc):
        self.nc = nc
        self.pe = Eng(self, nc.tensor, "pe")
        self.act = Eng(self, nc.scalar, "act")
        self.dve = Eng(self, nc.vector, "dve")
        self.pool = Eng(self, nc.gpsimd, "pool")
        self.sp = Eng(self, nc.sync, "sp", True)
        self.gq = Eng(self, nc.gpsimd, "gq", True)
        self.all = [self.pe, self.act, self.dve, self.pool, self.sp, self.gq]
        self.host = {"pe": self.pe, "act": self.act, "dve": self.dve, "pool": self.pool,
                     "sp": self.sp, "gq": self.pool}
        self.ninst = 0

    def _wait(self, eng, dep):
        de, di = dep
        host = self.host[eng.name]
        if de.is_dma:
            sem, val = de.sem_val(di)
            key = (de.name, (di - 1) % de.nsem)
        else:
            sem, val = de.sem, di
            key = de.name
        if host.waited.get(key, 0) >= val:
            return
        host.waited[key] = val
        eng.e.wait_ge(sem, val)

    def op(self, eng, fn, reads=(), writes=()):
        deps = []
        for r in reads:
            if r.w is not None:
                deps.append(r.w)
        for w in writes:
            if w.w is not None:
                deps.append(w.w)
            deps.extend(w.r)
        if eng.is_dma and eng.count >= eng.nsem:
            self._wait(eng, (eng, eng.count - eng.nsem + 1))
        seen = set()
        for d in deps:
            de, di = d
            k = (de.name, di)
            if k in seen:
                continue
            seen.add(k)
            if de is eng and not eng.is_dma:
                if eng is self.pe:
                    continue
                if not any((r.w is not None and r.w[0] is eng and r.w[1] == di) for r in reads):
                    continue
            self._wait(eng, d)
        inst = fn()
        eng.count += 1
        idx = eng.count
        sem, _ = eng.sem_val(idx)
        inst.then_inc(sem, 16 if eng.is_dma else 1)
        self.ninst += 1
        me = (eng, idx)
        for r in reads:
            r.r.append(me)
            if len(r.r) > 64:
                last = {}
                for (e2, i2) in r.r:
                    if e2.name not in last or last[e2.name][1] < i2:
                        last[e2.name] = (e2, i2)
                r.r = list(last.values())
        for w in writes:
            w.w = me
            w.r = []
        return inst

    def barrier(self):
        hosts = [self.pe, self.act, self.dve, self.pool, self.sp]
        for h in hosts:
            for e in self.all:
                if e.count == 0:
                    continue
                if e.is_dma:
                    for di in range(max(1, e.count - e.nsem + 1), e.count + 1):
                        self._wait(h, (e, di))
                elif e is not h:
                    self._wait(h, (e, e.count))


WEIGHT_SPECS = [
    ("l0_ada_w", (1024, 6144)), ("l0_ada_b", (6144,)), ("l0_norm_mix", (1024,)), ("l0_norm_ffn", (1024,)),
    ("l0_w_in", (1024, 2304)), ("l0_lam_q1", (64,)), ("l0_lam_k1", (64,)), ("l0_lam_q2", (64,)),
    ("l0_lam_k2", (64,)), ("l0_subln", (128,)), ("l0_q_norm", (64,)), ("l0_k_norm", (64,)),
    ("l0_w_o", (1024, 1024)), ("l0_ffn_up", (1024, 5632)), ("l0_ffn_conv_w", (3, 5632)),
    ("l0_ffn_conv_b", (5632,)), ("l0_ffn_down", (2816, 1024)),
    ("l1_ada_w", (1024, 6144)), ("l1_ada_b", (6144,)), ("l1_norm_mix", (1024,)), ("l1_norm_ffn", (1024,)),
    ("l1_w_in", (1024, 2048)), ("l1_conv_w", (4, 1024)), ("l1_conv_b", (1024,)),
    ("l1_gate_a_w", (2, 8, 128, 128)), ("l1_gate_a_b", (2, 1024)), ("l1_gate_x_w", (2, 8, 128, 128)),
    ("l1_gate_x_b", (2, 1024)), ("l1_a_param", (2, 1024)), ("l1_w_o", (1024, 1024)),
    ("l1_ffn_up", (1024, 5632)), ("l1_ffn_conv_w", (3, 5632)), ("l1_ffn_conv_b", (5632,)),
    ("l1_ffn_down", (2816, 1024)), ("final_norm", (1024,)),
]


def build(stop=99):
    nc = bass.Bass("TRN2", target_bir_lowering=False)
    fw = FW(nc)
    PE, ACT, DVE, POOL, SP, GQ = fw.pe, fw.act, fw.dve, fw.pool, fw.sp, fw.gq
    _uid = [0]

    def _sbt(name, shape, dt):
        _uid[0] += 1
        return nc.sbuf_tensor("%s_%d" % (name, _uid[0]), shape, dt)

    def _pst(name, shape, dt):
        _uid[0] += 1
        return nc.psum_tensor("%s_%d" % (name, _uid[0]), shape, dt)

    def din(name, shape):
        return nc.dram_tensor(name, list(shape), F32, kind="ExternalInput").ap()

    x_d = din("x", (2048, 1024)); ctx_d = din("ctx", (256, 1024)); c_d = din("c", (1024,)); cctx_d = din("c_ctx", (1024,))
    W = {n: din(n, s) for n, s in WEIGHT_SPECS}
    cos_d = din("rope_cos", (2048, 32)); sin_d = din("rope_sin", (2048, 32))
    out_d = nc.dram_tensor("out", [2048, 1024], F32, kind="ExternalOutput").ap()

    _rec = [None]

    def op(eng, f, r=(), w=()):
        if _rec[0] is not None:
            _rec[0].append((eng, f, list(r), list(w)))
            return None
        return fw.op(eng, f, reads=r, writes=w)

    def ve(eng):
        return nc.vector if eng is DVE else nc.gpsimd

    def mm(out, lhsT, rhs, st, sp_, r, w):
        op(PE, lambda: nc.tensor.matmul(out, lhsT, rhs, start=st, stop=sp_), r, w)

    def tr(out, in_, idt, r, w):
        op(PE, lambda: nc.tensor.transpose(out, in_, idt), r, w)

    def act(out, in_, func, r, w, **kw):
        op(ACT, lambda: nc.scalar.activation(out=out, in_=in_, func=func, **kw), r, w)

    def tt(eng, out, in0, in1, alu, r, w):
        op(eng, lambda: ve(eng).tensor_tensor(out=out, in0=in0, in1=in1, op=alu), r, w)

    def ts(eng, out, in0, s1, op0, r, w, s2=None, op1=None):
        if op1 is None:
            op(eng, lambda: ve(eng).tensor_scalar(out=out, in0=in0, scalar1=s1, scalar2=None, op0=op0), r, w)
        else:
            op(eng, lambda: ve(eng).tensor_scalar(out=out, in0=in0, scalar1=s1, scalar2=s2, op0=op0, op1=op1), r, w)

    def stt(out, in0, scalar, in1, op0, op1, r, w):
        op(DVE, lambda: nc.vector.scalar_tensor_tensor(out=out, in0=in0, scalar=scalar, in1=in1, op0=op0, op1=op1), r, w)

    def cp(eng, out, in_, r, w):
        if eng is ACT:
            op(ACT, lambda: nc.scalar.copy(out, in_), r, w)
        else:
            op(eng, lambda: ve(eng).tensor_copy(out=out, in_=in_), r, w)

    def recip(out, in_, r, w):
        op(DVE, lambda: nc.vector.reciprocal(out=out, in_=in_), r, w)

    def rsum(out, in_, r, w):
        op(DVE, lambda: nc.vector.reduce_sum(out=out, in_=in_, axis=AX.X), r, w)

    def memset(eng, ap, val, w):
        op(eng, lambda: ve(eng).memset(ap, val), (), w)

    def dma(q, out, in_, r, w):
        e = nc.sync if q is SP else nc.gpsimd
        op(q, lambda: e.dma_start(out=out, in_=in_), r, w)

    def bcast_mid(a, n):
        return bass.AP(a.tensor, a.offset, [a.ap[0], [0, n], *a.ap[1:]])

    def bcast_last(a, n):
        return bass.AP(a.tensor, a.offset, [*a.ap, [0, n]])

    sbt = nc.alloc_sbuf_tensor

    x_tok = sbt("x_tok", [128, NT, D], F32); Rx = regs(NT)
    hT = sbt("hT", [128, KC, T], BF16); RhT = regs(NT)
    identf = sbt("identf", [128, 128], F32); identb = sbt("identb", [128, 128], BF16); onesf = sbt("onesf", [128, 128], F32)
    Rc = Reg()
    NV = 640
    vecT = sbt("vecT", [128, NV], F32); Rvec = Reg()
    modT = sbt("modT", [128, 2, 48, 2], F32); Rmod = Reg()
    mgb = [sbt("mgb%d" % i, [128, D], F32) for i in range(2)]; Rmgb = regs(2)
    cs_tok = sbt("cs_tok", [128, 16, 32], F32); sn_tok = sbt("sn_tok", [128, 16, 32], F32); Rrope = Reg()
    Rjunk = Reg()
    ss = sbt("ss", [128, NT], F32); lnv = sbt("lnv", [128, NT], F32); rstd = sbt("rstd", [128, NT], F32); Rst = Reg()
    cst = sbt("cst", [128, 4], F32)
    scl = [sbt("scl%d" % i, [128, KC], F32) for i in range(2)]
    sft = [sbt("sft%d" % i, [128, KC], F32) for i in range(2)]
    Rsc = Reg()

    vec_list = [("c", c_d, 1024), ("c_ctx", cctx_d, 1024)]
    for l in (0, 1):
        vec_list += [("ada_b%d" % l, W["l%d_ada_b" % l], 6144), ("nmix%d" % l, W["l%d_norm_mix" % l], 1024),
                     ("nffn%d" % l, W["l%d_norm_ffn" % l], 1024),
                     ("fcw%d" % l, W["l%d_ffn_conv_w" % l].rearrange("a b -> (a b)"), 3 * 5632),
                     ("fcb%d" % l, W["l%d_ffn_conv_b" % l], 5632)]
    vec_list += [("l1cw", W["l1_conv_w"].rearrange("a b -> (a b)"), 4096), ("l1cb", W["l1_conv_b"], 1024),
                 ("gab", W["l1_gate_a_b"].rearrange("a b -> (a b)"), 2048),
                 ("gxb", W["l1_gate_x_b"].rearrange("a b -> (a b)"), 2048),
                 ("apar", W["l1_a_param"].rearrange("a b -> (a b)"), 2048)]
    voff = {}
    r0 = 0
    for name, ap_, n in vec_list:
        voff[name] = r0
        r0 += n // 128
    assert r0 <= NV

    with ExitStack() as _es:
        stg = _es.enter_context(_sbt("stg", [128, 5, 128], F32))
        adaw0 = _es.enter_context(_sbt("adaw0", [128, KC, 512], BF16))
        adaw1 = _es.enter_context(_sbt("adaw1", [128, KC, 512], BF16))
        scb = _es.enter_context(_sbt("scb", [128, KC, 2], BF16))
        sct = _es.enter_context(_sbt("sct", [128, 16], F32))
        pT0 = _es.enter_context(_pst("pT0", [128, 128], F32))
        modps = _es.enter_context(_pst("modps", [128, 96], F32))
        Rstg, Rscb, Rsct, RpT0, Rmp = Reg(), Reg(), Reg(), Reg(), Reg()
        adaw = [adaw0, adaw1]; Radaw = regs(2)
        for t in range(NT):
            src = ctx_d[t * 128:(t + 1) * 128, :] if t < 2 else x_d[(t - 2) * 128:(t - 1) * 128, :]
            dma(SP, x_tok[:, t, :], src, (), [Rx[t]])
        memset(POOL, identf[:], 0.0, [Rc])
        op(POOL, lambda: nc.gpsimd.affine_select(out=identf[:], in_=identf[:], compare_op=ALU.not_equal, fill=1.0,
                                                 base=0, pattern=[[-1, 128]], channel_multiplier=1), [Rc], [Rc])
        cp(POOL, identb[:], identf[:], [Rc], [Rc])
        memset(POOL, onesf[:], 1.0, [Rc])
        memset(POOL, cst[:, 0:1], EPS, [Rc])
        memset(POOL, cst[:, 1:2], 1.0, [Rc])
        memset(POOL, stg[:], 0.0, [Rstg])
        dma(SP, cs_tok[:], cos_d.rearrange("(i p) f -> p i f", p=128), (), [Rrope])
        dma(SP, sn_tok[:], sin_d.rearrange("(i p) f -> p i f", p=128), (), [Rrope])
        for name, ap_, n in vec_list:
            ra, rb = voff[name], voff[name] + n // 128
            r = ra
            while r < rb:
                s = r // 128
                e = min(rb, (s + 1) * 128)
                dma(SP, stg[r - s * 128:e - s * 128, s, :],
                    ap_[(r - ra) * 128:(e - ra) * 128].rearrange("(r p) -> r p", p=128), (), [Rstg])
                r = e
        for s in range(5):
            tr(pT0[:], stg[:, s, :], identf[:], [Rstg, Rc], [RpT0])
            cp(DVE, vecT[:, s * 128:(s + 1) * 128], pT0[:], [RpT0], [Rvec])
        act(sct[:], vecT[:, 0:16], AF.Exp, [Rvec], [Rsct], scale=-1.0)
        ts(DVE, sct[:], sct[:], 1.0, ALU.add, [Rsct], [Rsct])
        recip(sct[:], sct[:], [Rsct], [Rsct])
        tt(DVE, scb[:].rearrange("p k v -> p v k"), vecT[:, 0:16].rearrange("p (v k) -> p v k", v=2),
           sct[:].rearrange("p (v k) -> p v k", v=2), ALU.mult, [Rsct, Rvec], [Rscb])
        for l in (0, 1):
            wsrc = W["l%d_ada_w" % l].rearrange("(k p) n -> p k n", p=128)
            for blk in range(12):
                wb, Rw = adaw[blk % 2], Radaw[blk % 2]
                dma(GQ, wb[:], wsrc[:, :, blk * 512:(blk + 1) * 512], (), [Rw])
                for j in range(4):
                    col = (blk * 4 + j) * 2
                    for k in range(KC):
                        mm(modps[:, col:col + 2], wb[:, k, j * 128:(j + 1) * 128], scb[:, k, :], k == 0, k == KC - 1,
                           [Rw, Rscb], [Rmp])
            ab = vecT[:, voff["ada_b%d" % l]:voff["ada_b%d" % l] + 48]
            tt(DVE, modT[:, l, :, :], modps[:, 0:96].rearrange("p (j v) -> p j v", v=2), bcast_last(ab, 2), ALU.add,
               [Rmp, Rvec], [Rmod])
        fw.barrier()

    def prep_and_build(l, which, groups, pb, pst):
        gname = ("nmix%d" if which == 0 else "nffn%d") % l
        g = vecT[:, voff[gname]:voff[gname] + 8]
        vs = sorted(set(v for v, _ in groups))
        Rpb, Rdg = Reg(), regs(2)
        with ExitStack() as _es:
            dg0 = _es.enter_context(_sbt("dg0", [128, 128], F32))
            dg1 = _es.enter_context(_sbt("dg1", [128, 128], F32))
            xnb0 = _es.enter_context(_sbt("xnb0", [128, 4, D], BF16))
            xnb1 = _es.enter_context(_sbt("xnb1", [128, 4, D], BF16))
            junk = _es.enter_context(_sbt("junk", [128, D], BF16))
            dgs = [dg0, dg1]
            xnbs = [xnb0, xnb1]; Rxn = regs(2)
            for v in vs:
                s_scale = (3 * which + 1) * 8
                s_shift = (3 * which) * 8
                s_gate = (3 * which + 2) * 8
                stt(scl[v][:], modT[:, l, s_scale:s_scale + 8, v], 1.0, g, ALU.add, ALU.mult, [Rmod, Rvec], [Rsc])
                cp(DVE, sft[v][:], modT[:, l, s_shift:s_shift + 8, v], [Rmod], [Rsc])
                for j in range(8):
                    ts(DVE, dgs[j % 2][:], identf[:], modT[:, l, s_gate + j, v:v + 1], ALU.mult, [Rc, Rmod], [Rdg[j % 2]])
                    mm(pb[:, j * 128:(j + 1) * 128], onesf[:], dgs[j % 2][:], True, True, [Rc, Rdg[j % 2]], [Rpb])
                cp(DVE, mgb[v][:], pb[:, 0:D], [Rpb], [Rmgb[v]])
            tiles_all = [t for _, ts_ in groups for t in ts_]
            for t in tiles_all:
                act(junk[:], x_tok[:, t, :], AF.Square, [Rx[t]], [Rjunk, Rst], accum_out=ss[:, t:t + 1])
            t0, t1 = min(tiles_all), max(tiles_all) + 1
            act(lnv[:, t0:t1], ss[:, t0:t1], AF.Ln, [Rst], [Rst], scale=1.0 / D, bias=cst[:, 0:1])
            act(rstd[:, t0:t1], lnv[:, t0:t1], AF.Exp, [Rst], [Rst], scale=-0.5)
            Rps = regs(2)
            for gi, (v, tiles) in enumerate(groups):
                xn, Rn = xnbs[gi % 2], Rxn[gi % 2]
                for i, t in enumerate(tiles):
                    if i % 2:
                        act(xn[:, i, :], x_tok[:, t, :], AF.Identity, [Rx[t], Rst], [Rn], scale=rstd[:, t:t + 1])
                    else:
                        ts(DVE, xn[:, i, :], x_tok[:, t, :], rstd[:, t:t + 1], ALU.mult, [Rx[t], Rst], [Rn])
                n = len(tiles) * 128
                tok0 = tiles[0] * 128
                for half in range(2):
                    ps_, Rp = pst[half], Rps[half]
                    for ci in range(4):
                        c = half * 4 + ci
                        for i, t in enumerate(tiles):
                            tr(ps_[:, ci, i * 128:(i + 1) * 128], xn[:, i, c * 128:(c + 1) * 128], identb[:], [Rn, Rc], [Rp])
                        act(hT[:, c, tok0:tok0 + n], ps_[:, ci, 0:n], AF.Identity, [Rp, Rsc], [RhT[t] for t in tiles],
                            scale=scl[v][:, c:c + 1], bias=sft[v][:, c:c + 1])
            fw.barrier()

    def phase_hT(l, which, groups):
        with ExitStack() as _es:
            pb = _es.enter_context(_pst("pb", [128, D], F32))
            pst0 = _es.enter_context(_pst("pst0", [128, 4, 512], BF16))
            pst1 = _es.enter_context(_pst("pst1", [128, 4, 512], BF16))
            prep_and_build(l, which, groups, pb, [pst0, pst1])

    G_ALL = [(1, [0, 1])] + [(0, list(range(2 + 4 * i, 6 + 4 * i))) for i in range(4)]
    G_LAT = [(0, list(range(2 + 4 * i, 6 + 4 * i))) for i in range(4)]

    def resid_add(tq, v, lhsT, wmat, pwo, Rpwo, tmpb, Rtmp, rdeps):
        for ci, (c0, cw) in enumerate(((0, 384), (384, 384), (768, 256))):
            mm(pwo[:, 0:cw], lhsT, wmat[:, c0:c0 + cw], True, True, rdeps, [Rpwo])
            tb, Rt = tmpb[ci % 2], Rtmp[ci % 2]
            tt(DVE, tb[:, 0:cw], pwo[:, 0:cw], mgb[v][:, c0:c0 + cw], ALU.mult, [Rpwo, Rmgb[v]], [Rt])
            tt(POOL, x_tok[:, tq, c0:c0 + cw], x_tok[:, tq, c0:c0 + cw], tb[:, 0:cw], ALU.add, [Rt, Rx[tq]], [Rx[tq]])

    def phase_attn():
        w_in = W["l0_w_in"].rearrange("(k p) n -> p k n", p=128)
        w_o = W["l0_w_o"]
        NB = 2
        with ExitStack() as _es:
            wslab = _es.enter_context(_sbt("wslab", [128, NB, KC, 384], BF16))
            qkT = _es.enter_context(_sbt("qkT", [128, NB, 2, T], BF16))
            Va = _es.enter_context(_sbt("Va", [128, NB, NT, 129], BF16))
            kTb = _es.enter_context(_sbt("kTb", [128, T], BF16))
            Vb = _es.enter_context(_sbt("Vb", [128, NT, 2, 65], BF16))
            wou = _es.enter_context(_sbt("wou", [128, NB, D], BF16))
            stgq = _es.enter_context(_sbt("stgq", [128, 2, 256], BF16))
            pTb = _es.enter_context(_sbt("pTb", [128, 3, 1024], BF16))
            rt = _es.enter_context(_sbt("rt", [128, 2, 4, 128], F32))
            xs = _es.enter_context(_sbt("xs", [128, 2, 3, 128], F32))
            sm = _es.enter_context(_sbt("sm", [128, 4, 16], F32))
            fo = _es.enter_context(_sbt("fo", [128, 4, 3, 128], F32))
            ob = _es.enter_context(_sbt("ob", [128, 4, 128], BF16))
            oTs = _es.enter_context(_sbt("oTs", [128, 4, 128], BF16))
            accS = _es.enter_context(_sbt("accS", [128, 1, 4, 132], F32))
            tmpb_ = _es.enter_context(_sbt("tmpb", [128, 2, 512], F32))
            lamb = _es.enter_context(_sbt("lamb", [128, 4, 64], F32))
            lams = _es.enter_context(_sbt("lams", [128, 8], F32))
            nrmb = _es.enter_context(_sbt("nrmb", [128, 256], F32))
            pss0 = _es.enter_context(_pst("pss0", [128, 1024], F32))
            pss1 = _es.enter_context(_pst("pss1", [128, 1024], F32))
            accP = _es.enter_context(_pst("accP", [128, 4, 256], F32))
            bk6 = _es.enter_context(_pst("bk6", [128, 512], F32))
            bk7 = _es.enter_context(_pst("bk7", [128, 1024], BF16))
            pp = bk6[:, 0:384]
            pwo = bk6[:, 0:384]
            ptq = bk7[:, 0:256]
            pto = bk7[:, 256:512]
            Rws, Rqk, RVa, Rwou = regs(NB), regs(NB), regs(NB), regs(NB)
            RkTb, RVb, Rlam, Rnrm = Reg(), Reg(), Reg(), Reg()
            Rstq, RpT, Rrt, Rxs, Rsm, Rfo, Rob, RoT, Rtmp = regs(2), regs(3), regs(2), regs(2), regs(4), regs(4), regs(4), regs(4), regs(2)
            RaccS = regs(2)
            Rpss, Racc, Rpp, Rptq, Rpto = regs(2), regs(4), Reg(), Reg(), Reg()
            Rpwo = Rpp
            pss = [pss0, pss1]
            tmpb = [tmpb_[:, 0, :], tmpb_[:, 1, :]]
            for i, nm in enumerate(("l0_lam_q1", "l0_lam_k1", "l0_lam_q2", "l0_lam_k2")):
                dma(SP, lamb[:, i, :], W[nm].partition_broadcast(128), (), [Rlam])
            dma(SP, nrmb[:, 0:128], W["l0_subln"].partition_broadcast(128), (), [Rnrm])
            dma(SP, nrmb[:, 128:192], W["l0_q_norm"].partition_broadcast(128), (), [Rnrm])
            dma(SP, nrmb[:, 192:256], W["l0_k_norm"].partition_broadcast(128), (), [Rnrm])
            lam_init = 0.8 - 0.6 * math.exp(-0.3 * 0)
            ts(DVE, nrmb[:, 0:128], nrmb[:, 0:128], 1.0 - lam_init, ALU.mult, [Rnrm], [Rnrm])
            tt(DVE, lamb[:, 0, :], lamb[:, 0, :], lamb[:, 1, :], ALU.mult, [Rlam], [Rlam])
            tt(DVE, lamb[:, 2, :], lamb[:, 2, :], lamb[:, 3, :], ALU.mult, [Rlam], [Rlam])
            rsum(lams[:, 0:1], lamb[:, 0, :], [Rlam], [Rlam])
            rsum(lams[:, 1:2], lamb[:, 2, :], [Rlam], [Rlam])
            act(lams[:, 2:4], lams[:, 0:2], AF.Exp, [Rlam], [Rlam])
            tt(DVE, lams[:, 4:5], lams[:, 3:4], lams[:, 2:3], ALU.subtract, [Rlam], [Rlam])
            ts(DVE, lams[:, 5:6], lams[:, 4:5], -lam_init, ALU.add, [Rlam], [Rlam])
            for b in range(NB):
                memset(POOL, Va[:, b, :, 128:129], 1.0, [RVa[b]])
            memset(POOL, Vb[:, :, :, 64:65], 1.0, [RVb])
            nlam = lams[:, 5:6]
            subln_b = nrmb[:, 0:128]
            qn_b = nrmb[:, 128:192]
            kn_b = nrmb[:, 192:256]

            def rope(src, H, t, dst, si, rsrc):
                i = t - 2
                cs = bcast_mid(cs_tok[:, i, :], H); sn = bcast_mid(sn_tok[:, i, :], H)
                sv = src.rearrange("p (h i two) -> p h i two", h=H, two=2)
                dv = dst.rearrange("p (h i two) -> p h i two", h=H, two=2)
                x0, x1 = sv[:, :, :, 0], sv[:, :, :, 1]
                tmp = [rt[:, si, j, 0:H * 32].rearrange("p (h i) -> p h i", h=H) for j in range(4)]
                tt(DVE, tmp[0], x0, cs, ALU.mult, rsrc + [Rrope], [Rrt[si]])
                tt(DVE, tmp[1], x1, sn, ALU.mult, rsrc + [Rrope], [Rrt[si]])
                tt(DVE, tmp[2], x0, sn, ALU.mult, rsrc + [Rrope], [Rrt[si]])
                tt(DVE, tmp[3], x1, cs, ALU.mult, rsrc + [Rrope], [Rrt[si]])
                tt(POOL, dv[:, :, :, 0], tmp[0], tmp[1], ALU.subtract, [Rrt[si]], [Rstq[si]])
                tt(POOL, dv[:, :, :, 1], tmp[2], tmp[3], ALU.add, [Rrt[si]], [Rstq[si]])

            def qknorm(src, gain_b, si, Rp_):
                a, b2, c2 = xs[:, si, 0, :], xs[:, si, 1, :], xs[:, si, 2, :]
                cp(DVE, a, src, Rp_, [Rxs[si]])
                tt(POOL, b2, a, a, ALU.mult, [Rxs[si]], [Rxs[si]])
                rsum(sm[:, si, 0:2], b2.rearrange("p (h d) -> p h d", h=2), [Rxs[si]], [Rsm[si]])
                act(sm[:, si, 2:4], sm[:, si, 0:2], AF.Ln, [Rsm[si]], [Rsm[si]], scale=1.0 / 64, bias=cst[:, 0:1])
                act(sm[:, si, 4:6], sm[:, si, 2:4], AF.Exp, [Rsm[si]], [Rsm[si]], scale=-0.5)
                tt(DVE, b2.rearrange("p (h d) -> p h d", h=2), a.rearrange("p (h d) -> p h d", h=2),
                   bcast_last(sm[:, si, 4:6], 64), ALU.mult, [Rxs[si], Rsm[si]], [Rxs[si]])
                tt(POOL, c2.rearrange("p (h d) -> p h d", h=2), b2.rearrange("p (h d) -> p h d", h=2),
                   bcast_mid(gain_b, 2), ALU.mult, [Rxs[si], Rnrm], [Rxs[si]])
                return c2

            ppbuf = [accP[:, 0:2, :].rearrange("p a c -> p (a c)"), accP[:, 2:4, :].rearrange("p a c -> p (a c)")]
            Rppbuf = [[Racc[0], Racc[1]], [Racc[2], Racc[3]]]
            ptqs = [bk7[:, 0:256], bk7[:, 256:512]]
            ptos = [bk7[:, 512:640], bk7[:, 640:768]]
            Rbk7 = Reg()
            Rptqs, Rptos = [Rbk7, Rbk7], [Rbk7, Rbk7]
            Rbk6 = Reg()

            def load_slab(kind, idx, b, sb_):
                if kind == "diff":
                    cols = [(idx * 128, 128), (512 + idx * 128, 128), (1024 + idx * 128, 128)]
                elif kind == "kv":
                    cols = [(2048, 256)]
                else:
                    cols = [(1536 + 64 * idx, 64), (1536 + 64 * (idx + 4), 64)]
                o = 0
                for (c0, cw) in cols:
                    dma(GQ, wslab[:, sb_, :, o:o + cw], w_in[:, :, c0:c0 + cw], (), [Rws[sb_]])
                    o += cw
                if kind == "diff":
                    dma(GQ, wou[:, b, :], w_o[idx * 128:(idx + 1) * 128, :], (), [Rwou[b]])
                elif kind == "gq":
                    dma(GQ, wou[0:64, b, :], w_o[512 + 64 * idx:512 + 64 * idx + 64, :], (), [Rwou[b]])
                    dma(GQ, wou[64:128, b, :], w_o[512 + 64 * (idx + 4):512 + 64 * (idx + 4) + 64, :], (), [Rwou[b]])

            def project(kind, idx, b, sb_):
                if kind == "diff":
                    cols = [(idx * 128, 128), (512 + idx * 128, 128), (1024 + idx * 128, 128)]
                elif kind == "kv":
                    cols = [(2048, 256)]
                else:
                    cols = [(1536 + 64 * idx, 64), (1536 + 64 * (idx + 4), 64)]
                ncol = sum(cw for _, cw in cols)
                pending = []
                for t0_ in range(0, NT, 2):
                    pair = [t0_, t0_ + 1]
                    for t in pair:
                        si = t % 2
                        for k in range(KC):
                            mm(ppbuf[si][:, 0:ncol], hT[:, k, t * 128:(t + 1) * 128], wslab[:, sb_, k, 0:ncol], k == 0, k == KC - 1,
                               [RhT[t], Rws[sb_]], Rppbuf[si])
                    for pfn in pending:
                        pfn()
                    pending = []
                    recs = []
                    for t in pair:
                        si = t % 2
                        pp = ppbuf[si]
                        Rp = Rppbuf[si]
                        lat = t >= 2
                        ptq, Rptq = ptqs[si], Rptqs[si]
                        _rec[0] = []
                        if kind == "diff":
                            if lat:
                                rope(pp[:, 0:256], 4, t, stgq[:, si, 0:256], si, Rp)
                            else:
                                cp(DVE, stgq[:, si, 0:256], pp[:, 0:256], Rp, [Rstq[si]])
                            cp(DVE, Va[:, b, t, 0:128], pp[:, 256:384], Rp, [RVa[b]])

                            def pfn(t=t, si=si, ptq=ptq, Rptq=Rptq):
                                tr(ptq[:, 0:128], stgq[:, si, 0:128], identb[:], [Rstq[si], Rc], [Rptq])
                                tr(ptq[:, 128:256], stgq[:, si, 128:256], identb[:], [Rstq[si], Rc], [Rptq])
                                cp(DVE, qkT[:, b, :, t * 128:(t + 1) * 128], ptq[:, 0:256].rearrange("p (a q) -> p a q", a=2),
                                   [Rptq], [Rqk[b]])
                        elif kind == "kv":
                            xn_ = qknorm(pp[:, 0:128], kn_b, si, Rp)
                            cp(DVE, Vb[:, t, :, 0:64], pp[:, 128:256].rearrange("p (g d) -> p g d", g=2), Rp, [RVb])
                            if lat:
                                rope(xn_, 2, t, stgq[:, si, 0:128], si, [Rxs[si]])
                            else:
                                cp(POOL, stgq[:, si, 0:128], xn_, [Rxs[si]], [Rstq[si]])

                            def pfn(t=t, si=si, ptq=ptq, Rptq=Rptq):
                                tr(ptq[:, 0:128], stgq[:, si, 0:128], identb[:], [Rstq[si], Rc], [Rptq])
                                cp(DVE, kTb[:, t * 128:(t + 1) * 128], ptq[:, 0:128], [Rptq], [RkTb])
                        else:
                            xn_ = qknorm(pp[:, 0:128], qn_b, si, Rp)
                            if lat:
                                rope(xn_, 2, t, stgq[:, si, 0:128], si, [Rxs[si]])
                            else:
                                cp(POOL, stgq[:, si, 0:128], xn_, [Rxs[si]], [Rstq[si]])

                            def pfn(t=t, si=si, ptq=ptq, Rptq=Rptq):
                                tr(ptq[:, 0:128], stgq[:, si, 0:128], identb[:], [Rstq[si], Rc], [Rptq])
                                cp(DVE, qkT[:, b, 0, t * 128:(t + 1) * 128], ptq[:, 0:128], [Rptq], [Rqk[b]])
                        recs.append(_rec[0])
                        _rec[0] = None
                        pending.append(pfn)
                    for k_ in range(max(len(r_) for r_ in recs)):
                        for r_ in recs:
                            if k_ < len(r_):
                                e_, f_, rr_, ww_ = r_[k_]
                                fw.op(e_, f_, reads=rr_, writes=ww_)
                for pfn in pending:
                    pfn()

            def attend(kind, idx, b):
                if kind == "diff":
                    dv1 = 129
                    kT_ap = lambda s, kt: qkT[s * 64:(s + 1) * 64, b, 1, kt * 128:(kt + 1) * 128]
                    v_ap = lambda s, kt: Va[:, b, kt, 0:129]
                    Rk, Rv = Rqk[b], RVa[b]
                else:
                    dv1 = 65
                    kT_ap = lambda s, kt: kTb[s * 64:(s + 1) * 64, kt * 128:(kt + 1) * 128]
                    v_ap = lambda s, kt: Vb[:, kt, s, 0:65]
                    Rk, Rv = RkTb, RVb
                blocks = [(0, [0, 1])] + [(256 + 256 * i, list(range(NT))) for i in range(8)]
                G = []
                for bi_, (q0, ktiles) in enumerate(blocks):
                    for g0 in range(0, len(ktiles), 2):
                        G.append((bi_, q0, ktiles[g0:g0 + 2], g0 == 0, g0 + 2 >= len(ktiles)))
                n_g = len(G)

                def QK(i):
                    bi_, q0, grp, first, last = G[i]
                    pb_, Rp = pss[i % 2], Rpss[i % 2]
                    for gi, kt in enumerate(grp):
                        for s in range(2):
                            o_ = s * 512 + gi * 256
                            mm(pb_[:, o_:o_ + 256], kT_ap(s, kt), qkT[s * 64:(s + 1) * 64, b, 0, q0:q0 + 256], True, True,
                               [Rk, Rqk[b]], [Rp])

                def EXP(i):
                    grp = G[i][2]
                    n = len(grp) * 256
                    src = pss[i % 2][:, :].rearrange("p (s c) -> p s c", s=2)[:, :, 0:n]
                    dst = pTb[:, i % 3, :].rearrange("p (s c) -> p s c", s=2)[:, :, 0:n]
                    act(dst, src, AF.Exp, [Rpss[i % 2]], [RpT[i % 3]], scale=0.125)

                def PV(i):
                    bi_, q0, grp, first, last = G[i]
                    for gi, kt in enumerate(grp):
                        for s in range(2):
                            for qi in range(2):
                                ai = s * 2 + qi
                                o_ = s * 512 + gi * 256 + qi * 128
                                mm(accP[:, ai, 0:dv1], pTb[:, i % 3, o_:o_ + 128], v_ap(s, kt),
                                   first and gi == 0 and qi == 0, last and gi == len(grp) - 1, [RpT[i % 3], Rv], [Racc[ai]])

                def FIN_A(bi_, q0):
                    fb = bi_ % 2
                    fa = 0
                    cp(DVE, accS[:, fb * 0, 0:2, 0:dv1], accP[:, 0:2, 0:dv1], [Racc[0], Racc[1]], [RaccS[0]])
                    cp(DVE, accS[:, 0, 2:4, 0:dv1], accP[:, 2:4, 0:dv1], [Racc[2], Racc[3]], [RaccS[0]])
                    for qi in range(2):
                        sl = fb * 2 + qi
                        if kind == "diff":
                            a0, a1 = accS[:, 0, 0 + qi, :], accS[:, 0, 2 + qi, :]
                            recip(sm[:, sl, 8:9], a0[:, 128:129], [RaccS[0]], [Rsm[sl]])
                            recip(sm[:, sl, 9:10], a1[:, 128:129], [RaccS[0]], [Rsm[sl]])
                            tt(DVE, sm[:, sl, 10:11], sm[:, sl, 9:10], nlam, ALU.mult, [Rsm[sl], Rlam], [Rsm[sl]])
                            ts(DVE, fo[:, sl, 0, :], a0[:, 0:128], sm[:, sl, 8:9], ALU.mult, [RaccS[0], Rsm[sl]], [Rfo[sl]])
                            stt(fo[:, sl, 1, :], a1[:, 0:128], sm[:, sl, 10:11], fo[:, sl, 0, :], ALU.mult, ALU.add,
                                [RaccS[0], Rsm[sl], Rfo[sl]], [Rfo[sl]])
                            tt(POOL, fo[:, sl, 2, :], fo[:, sl, 1, :], fo[:, sl, 1, :], ALU.mult, [Rfo[sl]], [Rfo[sl]])
                            rsum(sm[:, sl, 11:12], fo[:, sl, 2, :], [Rfo[sl]], [Rsm[sl]])
                        else:
                            for s in range(2):
                                a_ = accS[:, 0, s * 2 + qi, :]
                                recip(sm[:, sl, 8 + s:9 + s], a_[:, 64:65], [RaccS[0]], [Rsm[sl]])
                                ts(DVE, ob[:, sl, s * 64:(s + 1) * 64], a_[:, 0:64], sm[:, sl, 8 + s:9 + s], ALU.mult,
                                   [RaccS[0], Rsm[sl]], [Rob[sl]])

                def FIN_A2(bi_, q0):
                    if kind != "diff":
                        return
                    fb = bi_ % 2
                    for qi in range(2):
                        sl = fb * 2 + qi
                        act(sm[:, sl, 12:13], sm[:, sl, 11:12], AF.Ln, [Rsm[sl]], [Rsm[sl]], scale=1.0 / 128, bias=cst[:, 0:1])
                        act(sm[:, sl, 13:14], sm[:, sl, 12:13], AF.Exp, [Rsm[sl]], [Rsm[sl]], scale=-0.5)
                        stt(ob[:, sl, :], fo[:, sl, 1, :], sm[:, sl, 13:14], subln_b, ALU.mult, ALU.mult,
                            [Rfo[sl], Rsm[sl], Rnrm], [Rob[sl]])

                def FIN_B0(bi_, q0):
                    fb = bi_ % 2
                    for qi in range(2):
                        sl = fb * 2 + qi
                        pto, Rpto = ptos[qi], Rptos[qi]
                        tr(pto[:, 0:128], ob[:, sl, :], identb[:], [Rob[sl], Rc], [Rpto])
                    for qi in range(2):
                        sl = fb * 2 + qi
                        pto, Rpto = ptos[qi], Rptos[qi]
                        cp(DVE, oTs[:, sl, :], pto[:, 0:128], [Rpto], [RoT[sl]])

                def FIN_Bk(bi_, q0, kk):
                    fb = bi_ % 2
                    qi, j = kk // 2, kk % 2
                    sl = fb * 2 + qi
                    tq = q0 // 128 + qi
                    v = 1 if tq < 2 else 0
                    h = j % 2
                    c0 = j * 512
                    mm(bk6[:, 0:512], oTs[:, sl, :], wou[:, b, c0:c0 + 512], True, True, [RoT[sl], Rwou[b]], [Rbk6])
                    tt(DVE, tmpb[h][:, 0:512], bk6[:, 0:512], mgb[v][:, c0:c0 + 512], ALU.mult, [Rbk6, Rmgb[v]], [Rtmp[h]])
                    tt(POOL, x_tok[:, tq, c0:c0 + 512], x_tok[:, tq, c0:c0 + 512], tmpb[h][:, 0:512], ALU.add,
                       [Rtmp[h], Rx[tq]], [Rx[tq]])

                def run_stage(st_, b2_, q2_):
                    if st_ == 0:
                        FIN_A2(b2_, q2_)
                    elif st_ == 1:
                        FIN_B0(b2_, q2_)
                    else:
                        FIN_Bk(b2_, q2_, st_ - 2)

                sched = []
                QK(0)
                if n_g > 1:
                    QK(1)
                for i in range(n_g):
                    EXP(i)
                    if i + 2 < n_g:
                        QK(i + 2)
                    PV(i)
                    bi_, q0, grp, first, last = G[i]
                    if last:
                        FIN_A(bi_, q0)
                        sched.append((i + 4, 0, bi_, q0))
                        sched.append((i + 6, 1, bi_, q0))
                        for kk in range(4):
                            sched.append((i + 7 + kk, 2 + kk, bi_, q0))
                        sched.sort()
                    while sched and sched[0][0] <= i:
                        _, st_, b2_, q2_ = sched.pop(0)
                        run_stage(st_, b2_, q2_)
                for (_, st_, b2_, q2_) in sched:
                    run_stage(st_, b2_, q2_)

            units = [("diff", h) for h in range(4)] + [("kv", 0)] + [("gq", j) for j in range(4)]
            plan = []
            bi = 0
            for pos, (kind, idx) in enumerate(units):
                if kind == "kv":
                    plan.append((kind, idx, bi % NB, pos % 2))
                else:
                    plan.append((kind, idx, bi % NB, pos % 2))
                    bi += 1
            load_slab(*plan[0])
            for pos, (kind, idx, b, sb_) in enumerate(plan):
                project(kind, idx, b, sb_)
                if pos + 1 < len(plan):
                    load_slab(*plan[pos + 1])
                if kind != "kv":
                    attend(kind, idx, b)
            fw.barrier()

    def phase_ffn(l, with_ctx):
        w_up = W["l%d_ffn_up" % l].rearrange("(k p) n -> p k n", p=128)
        w_dn = W["l%d_ffn_down" % l]
        cw0 = voff["fcw%d" % l]; cb0 = voff["fcb%d" % l]
        groups = [list(range(0, 4)), list(range(4, 8)), list(range(8, 12)), list(range(12, 16)), list(range(16, 19)), list(range(19, 22))]
        blocks = []
        if with_ctx:
            blocks.append((0, 256, 0, 256, True, True))
        s_ = 256
        for nt_ in (3, 3, 3, 3, 3, 1):
            blocks.append((s_, s_ + nt_ * 128, 256, T))
            s_ += nt_ * 128
        blocks = [(b_[0], b_[1], b_[2], b_[3]) for b_ in blocks]
        with ExitStack() as _es:
            wup = _es.enter_context(_sbt("wup", [128, 2, KC, 4, 256], BF16))
            wdn = _es.enter_context(_sbt("wdn", [128, 2, 4, D], BF16))
            actT = _es.enter_context(_sbt("actT", [128, 2, 4, 384], BF16))
            cvg = _es.enter_context(_sbt("cvg", [128, 2, 2, 384], F32))
            sgt = _es.enter_context(_sbt("sgt", [128, 2, 384], F32))
            tmpf = _es.enter_context(_sbt("tmpf", [128, 2, 512], F32))
            pu0 = _es.enter_context(_pst("pu0", [128, 2, 512], F32))
            pu1 = _es.enter_context(_pst("pu1", [128, 2, 512], F32))
            pwof = _es.enter_context(_pst("pwof", [128, 512], F32))
            pwof2 = _es.enter_context(_pst("pwof2", [128, 512], F32))
            Rwup, Rwdn, RaT, Rcv2, Rsg, Rtmp, Rpu2 = regs(2), regs(2), regs(2), [regs(2), regs(2)], regs(2), regs(2), [regs(2), regs(2)]
            Rpwo = Reg()
            pus = [pu0, pu1]
            tmpb = [tmpf[:, 0, :], tmpf[:, 1, :]]
            it = 0
            blk_ctr = [0]
            pend_list = []
            pwoH = [pwof[:, 0:512], pwof2[:, 0:512]]
            RpwoH = regs(2)
            def load_group(gi_):
                wb_ = gi_ % 2
                for pi, j in enumerate(groups[gi_]):
                    dma(GQ, wup[:, wb_, :, pi, 0:128], w_up[:, :, j * 128:(j + 1) * 128], (), [Rwup[wb_]])
                    dma(GQ, wup[:, wb_, :, pi, 128:256], w_up[:, :, DFF + j * 128:DFF + (j + 1) * 128], (), [Rwup[wb_]])
                    dma(GQ, wdn[:, wb_, pi, :], w_dn[j * 128:(j + 1) * 128, :], (), [Rwdn[wb_]])

            load_group(0)
            for gi, pairs in enumerate(groups):
                wb = gi % 2
                for bi_, (st, en, seg0, seg1) in enumerate(blocks):
                    in0 = max(seg0, st - 1); in1 = min(seg1, en + 1)
                    n_in = in1 - in0; n = en - st
                    L = st - in0
                    Rr = in1 - en
                    ab = blk_ctr[0] % 2
                    blk_ctr[0] += 1
                    tiles = list(range(st // 128, en // 128))
                    for pi, j in enumerate(pairs):
                        pu, Rp2 = pus[it % 2], Rpu2[it % 2]
                        cb_ = it % 2
                        it += 1
                        for half in range(2):
                            for k in range(KC):
                                mm(pu[:, half, 0:n_in], wup[:, wb, k, pi, half * 128:(half + 1) * 128], hT[:, k, in0:in1],
                                   k == 0, k == KC - 1, [Rwup[wb]] + [RhT[t] for t in range(in0 // 128, (in1 - 1) // 128 + 1)], [Rp2[half]])
                        for half in range(2):
                            fch = j + half * NPAIR
                            w0 = vecT[:, cw0 + fch:cw0 + fch + 1]
                            w1 = vecT[:, cw0 + 44 + fch:cw0 + 44 + fch + 1]
                            w2 = vecT[:, cw0 + 88 + fch:cw0 + 88 + fch + 1]
                            bb = vecT[:, cb0 + fch:cb0 + fch + 1]
                            c_ = cvg[:, cb_, half, :]
                            Rp = Rp2[half]
                            Rcvh = Rcv2[cb_][half]
                            act(c_[:, 0:n], pu[:, half, L:L + n], AF.Identity, [Rp, Rvec], [Rcvh], scale=w1, bias=bb)
                            if L == 1:
                                stt(c_[:, 0:n], pu[:, half, 0:n], w0, c_[:, 0:n], ALU.mult, ALU.add, [Rp, Rvec, Rcvh], [Rcvh])
                            else:
                                stt(c_[:, 1:n], pu[:, half, 0:n - 1], w0, c_[:, 1:n], ALU.mult, ALU.add, [Rp, Rvec, Rcvh], [Rcvh])
                            if Rr == 1:
                                stt(c_[:, 0:n], pu[:, half, L + 1:L + 1 + n], w2, c_[:, 0:n], ALU.mult, ALU.add, [Rp, Rvec, Rcvh], [Rcvh])
                            else:
                                stt(c_[:, 0:n - 1], pu[:, half, L + 1:L + n], w2, c_[:, 0:n - 1], ALU.mult, ALU.add, [Rp, Rvec, Rcvh], [Rcvh])
                        act(sgt[:, cb_, 0:n], cvg[:, cb_, 1, 0:n], AF.Silu, [Rcv2[cb_][1]], [Rsg[cb_]])
                        tt(DVE, actT[:, ab, pi, 0:n], sgt[:, cb_, 0:n], cvg[:, cb_, 0, 0:n], ALU.mult, [Rsg[cb_], Rcv2[cb_][0]], [RaT[ab]])
                        if pend_list:
                            pend_list.pop(0)()
                    def DOWN(ti, tq, ab=ab, wb=wb, npair=len(pairs)):
                        if True:
                            v = 1 if tq < 2 else 0
                            for jj in range(2):
                                h = jj % 2
                                c0 = jj * 512
                                for pi in range(npair):
                                    mm(pwoH[h], actT[:, ab, pi, ti * 128:(ti + 1) * 128], wdn[:, wb, pi, c0:c0 + 512],
                                       pi == 0, pi == npair - 1, [RaT[ab], Rwdn[wb]], [RpwoH[h]])
                                tt(DVE, tmpb[h][:, 0:512], pwoH[h], mgb[v][:, c0:c0 + 512], ALU.mult, [RpwoH[h], Rmgb[v]], [Rtmp[h]])
                                tt(POOL, x_tok[:, tq, c0:c0 + 512], x_tok[:, tq, c0:c0 + 512], tmpb[h][:, 0:512], ALU.add,
                                   [Rtmp[h], Rx[tq]], [Rx[tq]])
                    while pend_list:
                        pend_list.pop(0)()
                    for ti, tq in enumerate(tiles):
                        pend_list.append(lambda ti=ti, tq=tq, DOWN=DOWN: DOWN(ti, tq))
                    if bi_ == 0 and gi + 1 < len(groups):
                        load_group(gi + 1)
            while pend_list:
                pend_list.pop(0)()
            fw.barrier()

    def phase_lru():
        w_in = W["l1_w_in"].rearrange("(k p) n -> p k n", p=128)
        w_o = W["l1_w_o"]
        PIECES = [(0, 256), (256, 1280), (1280, T)]
        with ExitStack() as _es:
            wl = _es.enter_context(_sbt("wl", [128, 2, KC, 256], BF16))
            gw = _es.enter_context(_sbt("gw", [128, 2, 4, 128], BF16))
            woc = _es.enter_context(_sbt("woc", [128, 2, D], BF16))
            upre = _es.enter_context(_sbt("upre", [128, T], F32))
            uu = _es.enter_context(_sbt("uu", [128, T], F32))
            ub = _es.enter_context(_sbt("ub", [128, T], BF16))
            hsf = _es.enter_context(_sbt("hsf", [128, 2048], F32))
            pa = _es.enter_context(_sbt("pa", [128, 2, 1024], F32))
            pq = _es.enter_context(_sbt("pq", [128, 2, 1024], F32))
            hsb = _es.enter_context(_sbt("hsb", [128, 2, 1024], F32))
            gt = _es.enter_context(_sbt("gt", [128, 3, 512], F32))
            gb = _es.enter_context(_sbt("gb", [128, 2048], BF16))
            yb = _es.enter_context(_sbt("yb", [128, 2048], BF16))
            lv = _es.enter_context(_sbt("lv", [128, 64], F32))
            carry = _es.enter_context(_sbt("carry", [128, 4], F32))
            pu = _es.enter_context(_pst("pu", [128, 2, 512], F32))
            pg = _es.enter_context(_pst("pg", [128, 2, 512], F32))
            pwol = _es.enter_context(_pst("pwol", [128, 512], F32))
            pwol2 = _es.enter_context(_pst("pwol2", [128, 512], F32))
            Rwl, Rgw, Rwoc = regs(2), regs(2), regs(2)
            Rupre, Ruu, Rub, Rhsf, Rpa, Rpq, Rhsb, Rgt, Rgb, Ryb, Rlv, Rcar = (Reg() for _ in range(12))
            Rtmp = regs(2); Rpu, Rpg = regs(2), regs(2); Rpwo = Reg()
            Rpa2, Rpq2, Rhsb2 = regs(2), regs(2), regs(2)
            pcnt = [0]
            pr_ = upre
            ap0 = voff["apar"]
            act(lv[:, 0:16], vecT[:, ap0:ap0 + 16], AF.Exp, [Rvec], [Rlv], scale=-1.0)
            act(lv[:, 0:16], lv[:, 0:16], AF.Ln, [Rlv], [Rlv], scale=1.0, bias=cst[:, 1:2])
            ts(DVE, lv[:, 16:32], lv[:, 0:16], -16.0, ALU.mult, [Rlv], [Rlv])
            ts(DVE, lv[:, 0:16], lv[:, 0:16], -8.0, ALU.mult, [Rlv], [Rlv])
            ts(DVE, lv[:, 32:48], vecT[:, voff["gab"]:voff["gab"] + 16], -1.0, ALU.mult, [Rvec], [Rlv])
            ts(DVE, lv[:, 48:64], vecT[:, voff["gxb"]:voff["gxb"] + 16], -1.0, ALU.mult, [Rvec], [Rlv])
            cw0 = voff["l1cw"]; cb0 = voff["l1cb"]
            blocks5 = [(0, 256)] + [(256 + 512 * i, 256 + 512 * (i + 1)) for i in range(4)]
            def load_chunk(c_):
                b_ = c_ % 2
                dma(GQ, wl[:, b_, :, 0:128], w_in[:, :, D + c_ * 128:D + (c_ + 1) * 128], (), [Rwl[b_]])
                dma(GQ, wl[:, b_, :, 128:256], w_in[:, :, c_ * 128:(c_ + 1) * 128], (), [Rwl[b_]])
                for d_ in range(2):
                    dma(GQ, gw[:, b_, d_ * 2 + 0, :], W["l1_gate_a_w"][d_, c_], (), [Rgw[b_]])
                    dma(GQ, gw[:, b_, d_ * 2 + 1, :], W["l1_gate_x_w"][d_, c_], (), [Rgw[b_]])
                dma(GQ, woc[:, b_, :], w_o[c_ * 128:(c_ + 1) * 128, :], (), [Rwoc[b_]])

            Rpw2 = regs(2)
            load_chunk(0)
            for c in range(KC):
                b = c % 2
                if c + 1 < KC:
                    load_chunk(c + 1)
                tt(DVE, woc[:, b, :], woc[:, b, :], mgb[0][:], ALU.mult, [Rwoc[b], Rmgb[0]], [Rwoc[b]])
                for bi_, (s0, s1) in enumerate(blocks5):
                    n = s1 - s0
                    hr = [RhT[t] for t in range(s0 // 128, s1 // 128)]
                    for k in range(KC):
                        mm(pu[:, bi_ % 2, 0:n], wl[:, b, k, 0:128], hT[:, k, s0:s1], k == 0, k == KC - 1, [Rwl[b]] + hr, [Rpu[bi_ % 2]])
                    cp(ACT, upre[:, s0:s1], pu[:, bi_ % 2, 0:n], [Rpu[bi_ % 2]], [Rupre])
                    if s0 >= 256:
                        for k in range(KC):
                            mm(pg[:, bi_ % 2, 0:n], wl[:, b, k, 128:256], hT[:, k, s0:s1], k == 0, k == KC - 1, [Rwl[b]] + hr, [Rpg[bi_ % 2]])
                        xg, t1, t2 = gt[:, 0, 0:n], gt[:, 1, 0:n], gt[:, 2, 0:n]
                        cp(ACT, xg, pg[:, bi_ % 2, 0:n], [Rpg[bi_ % 2]], [Rgt])
                        tt(POOL, t1, xg, xg, ALU.mult, [Rgt], [Rgt])
                        ts(DVE, t1, t1, 0.044715, ALU.mult, [Rgt], [Rgt], s2=1.0, op1=ALU.add)
                        tt(DVE, t1, t1, xg, ALU.mult, [Rgt], [Rgt])
                        act(t2, t1, AF.Sigmoid, [Rgt], [Rgt], scale=1.5957691216057308)
                        tt(POOL, gb[:, s0 - 256:s1 - 256], xg, t2, ALU.mult, [Rgt], [Rgb])
                wv = [vecT[:, cw0 + j * 8 + c:cw0 + j * 8 + c + 1] for j in range(4)]
                bv = vecT[:, cb0 + c:cb0 + c + 1]
                for (g0, g1) in ((0, 256), (256, T)):
                    act(uu[:, g0:g1], upre[:, g0:g1], AF.Identity, [Rupre, Rvec], [Ruu], scale=wv[2], bias=bv)
                    stt(uu[:, g0 + 2:g1], upre[:, g0:g1 - 2], wv[0], uu[:, g0 + 2:g1], ALU.mult, ALU.add, [Rupre, Rvec, Ruu], [Ruu])
                    stt(uu[:, g0 + 1:g1], upre[:, g0:g1 - 1], wv[1], uu[:, g0 + 1:g1], ALU.mult, ALU.add, [Rupre, Rvec, Ruu], [Ruu])
                    stt(uu[:, g0:g1 - 1], upre[:, g0 + 1:g1], wv[3], uu[:, g0:g1 - 1], ALU.mult, ALU.add, [Rupre, Rvec, Ruu], [Ruu])
                cp(ACT, ub[:], uu[:], [Ruu], [Rub])
                for d in range(2):
                    order = PIECES if d == 0 else [PIECES[0], PIECES[2], PIECES[1]]
                    nsp = lv[:, d * 8 + c:d * 8 + c + 1]
                    nsp2 = lv[:, 16 + d * 8 + c:16 + d * 8 + c + 1]
                    gab_ = vecT[:, voff["gab"] + d * 8 + c:voff["gab"] + d * 8 + c + 1]
                    gxb_ = vecT[:, voff["gxb"] + d * 8 + c:voff["gxb"] + d * 8 + c + 1]
                    for pi_, (p0, p1) in enumerate(order):
                        np_ = p1 - p0
                        rS, iS = pr_[:, 0:np_], pr_[:, 1024:1024 + np_]
                        for s0 in range(p0, p1, 512):
                            s1 = min(p1, s0 + 512); n = s1 - s0; o0 = s0 - p0
                            mm(pu[:, 0, 0:n], gw[:, b, d * 2 + 0, :], ub[:, s0:s1], True, True, [Rgw[b], Rub], [Rpu[0]])
                            mm(pu[:, 1, 0:n], gw[:, b, d * 2 + 1, :], ub[:, s0:s1], True, True, [Rgw[b], Rub], [Rpu[1]])
                            act(rS[:, o0:o0 + n], pu[:, 0, 0:n], AF.Sigmoid, [Rpu[0], Rvec], [Rupre], scale=1.0, bias=gab_)
                            act(iS[:, o0:o0 + n], pu[:, 1, 0:n], AF.Sigmoid, [Rpu[1], Rvec], [Rupre], scale=1.0, bias=gxb_)
                        pb_ = pcnt[0] % 2
                        pcnt[0] += 1
                        A_, Q_ = pa[:, pb_, 0:np_], pq[:, pb_, 0:np_]
                        Rpa, Rpq, Rhsb = Rpa2[pb_], Rpq2[pb_], Rhsb2[pb_]
                        act(A_, rS, AF.Exp, [Rupre, Rlv], [Rpa], scale=nsp)
                        tt(DVE, Q_, A_, A_, ALU.mult, [Rpa], [Rpq])
                        act(Q_, Q_, AF.Ln, [Rpq], [Rpq], scale=-1.0, bias=cst[:, 1:2])
                        act(Q_, Q_, AF.Exp, [Rpq], [Rpq], scale=0.5)
                        tt(DVE, iS, iS, uu[:, p0:p1], ALU.mult, [Rupre, Ruu], [Rupre])
                        tt(DVE, Q_, Q_, iS, ALU.mult, [Rpq, Rupre], [Rpq])
                        if d == 0:
                            if p0 == 0:
                                dst = hsb[:, pb_, 0:np_]; Rd = Rhsb
                            else:
                                dst = hsf[:, p0 - 256:p1 - 256]; Rd = Rhsf
                            init = 0.0 if pi_ == 0 else carry[:, 0:1]
                            op(DVE, lambda: nc.vector.tensor_tensor_scan(out=dst, data0=A_, data1=Q_, initial=init,
                                                                        op0=ALU.mult, op1=ALU.add), [Rpa, Rpq, Rcar], [Rd])
                            cp(DVE, carry[:, 0:1], dst[:, np_ - 1:np_], [Rd], [Rcar])
                        else:
                            dst = hsb[:, pb_, 0:np_]; Rd = Rhsb
                            init = 0.0 if pi_ == 0 else carry[:, 1:2]
                            op(DVE, lambda: nc.vector.tensor_tensor_scan(out=dst[:, ::-1], data0=A_[:, ::-1], data1=Q_[:, ::-1],
                                                                        initial=init, op0=ALU.mult, op1=ALU.add),
                               [Rpa, Rpq, Rcar], [Rd])
                            cp(DVE, carry[:, 1:2], dst[:, 0:1], [Rd], [Rcar])
                            if p0 >= 256:
                                l0_, l1_ = p0 - 256, p1 - 256
                                tt(DVE, dst, dst, hsf[:, l0_:l1_], ALU.add, [Rd, Rhsf], [Rd])
                                tt(POOL, yb[:, l0_:l1_], dst, gb[:, l0_:l1_], ALU.mult, [Rd, Rgb], [Ryb])
                pws, Rpws = [pwol, pwol2], Rpw2
                for tl in range(16):
                    tq = 2 + tl
                    for half in range(2):
                        c0 = half * 512
                        mm(pws[half][:, 0:512], yb[:, tl * 128:(tl + 1) * 128], woc[:, b, c0:c0 + 512], True, True,
                           [Ryb, Rwoc[b]], [Rpws[half]])
                        tt(DVE, x_tok[:, tq, c0:c0 + 512], pws[half][:, 0:512], x_tok[:, tq, c0:c0 + 512], ALU.add,
                           [Rpws[half], Rx[tq]], [Rx[tq]])
            fw.barrier()

    def phase_final(tiles_src=None):
        with ExitStack() as _es:
            fnb = _es.enter_context(_sbt("fnb", [128, D], F32))
            osb = _es.enter_context(_sbt("osb", [128, 2, D], F32))
            junk = _es.enter_context(_sbt("junk", [128, D], BF16))
            Rfn, Ros = Reg(), regs(2)
            dma(SP, fnb[:], W["final_norm"].partition_broadcast(128), (), [Rfn])
            for t in range(2, NT):
                act(junk[:], x_tok[:, t, :], AF.Square, [Rx[t]], [Rjunk, Rst], accum_out=ss[:, t:t + 1])
            act(lnv[:, 2:NT], ss[:, 2:NT], AF.Ln, [Rst], [Rst], scale=1.0 / D, bias=cst[:, 0:1])
            act(rstd[:, 2:NT], lnv[:, 2:NT], AF.Exp, [Rst], [Rst], scale=-0.5)
            for t in range(2, NT):
                o = osb[:, t % 2, :]
                stt(o, x_tok[:, t, :], rstd[:, t:t + 1], fnb[:], ALU.mult, ALU.mult, [Rx[t], Rst, Rfn], [Ros[t % 2]])
                dma(SP, out_d[(t - 2) * 128:(t - 1) * 128, :], o, [Ros[t % 2]], ())
            fw.barrier()

    def dump_x():
        print("COUNTS", {e.name: e.count for e in fw.all}, fw.ninst)
        for t in range(2, NT):
            dma(SP, out_d[(t - 2) * 128:(t - 1) * 128, :], x_tok[:, t, :], [Rx[t]], ())
        fw.barrier()

    phase_hT(0, 0, G_ALL)
    if stop == 1:
        dump_x(); return nc
    phase_attn()
    if stop == 2:
        dump_x(); return nc
    phase_hT(0, 1, G_ALL)
    phase_ffn(0, True)
    if stop == 3:
        dump_x(); return nc
    phase_hT(1, 0, G_ALL)
    phase_lru()
    if stop == 4:
        dump_x(); return nc
    phase_hT(1, 1, G_LAT)
    phase_ffn(1, False)
    if stop == 5:
        dump_x(); return nc
    phase_final()
    return nc


def _rope_tables():
    pairs = 16
    inv = (10000.0 ** (-np.arange(pairs, dtype=np.float32) / pairs)).astype(np.float32)
    row = np.repeat(np.arange(32, dtype=np.float32), 64)
    col = np.tile(np.arange(64, dtype=np.float32), 32)
    ang = np.concatenate([row[:, None] * inv, col[:, None] * inv], axis=-1).astype(np.float32)
    return np.cos(ang).astype(np.float32), np.sin(ang).astype(np.float32)


def kernel(_stop=99, **inputs):
    nc = build(_stop)
    cos, sin = _rope_tables()
    shared = {n: np.ascontiguousarray(np.asarray(inputs[n], dtype=np.float32)) for n, _ in WEIGHT_SPECS}
    shared["c_ctx"] = np.ascontiguousarray(np.asarray(inputs["c_ctx"], dtype=np.float32))
    shared["rope_cos"] = cos
    shared["rope_sin"] = sin
    x = np.asarray(inputs["x"], dtype=np.float32)
    c = np.asarray(inputs["c"], dtype=np.float32)
    ctx = np.asarray(inputs["ctx"], dtype=np.float32)
    in_maps = []
    for b in range(8):
        m = dict(shared)
        m["x"] = np.ascontiguousarray(x[b]); m["ctx"] = np.ascontiguousarray(ctx[b]); m["c"] = np.ascontiguousarray(c[b])
        in_maps.append(m)
    res = run_bass_kernel_spmd(nc, in_maps, core_ids=list(range(8)))
    return np.stack([np.asarray(r["out"], dtype=np.float32) for r in res.results], axis=0)
```

```python
import math
import os
from contextlib import ExitStack
import numpy as np
import concourse.bass as bass
import concourse.mybir as mybir
from concourse.bass_utils import run_bass_kernel_spmd

F32 = mybir.dt.float32
BF16 = mybir.dt.bfloat16
AF = mybir.ActivationFunctionType
ALU = mybir.AluOpType
AX = mybir.AxisListType

D = 1024
KC = 8
NT = 18
T = 2304
DFF = 2816
NPAIR = 22
EPS = 1e-6


class Reg:
    __slots__ = ("w", "r")

    def __init__(self):
        self.w = None
        self.r = []


def regs(n):
    return [Reg() for _ in range(n)]


class Eng:
    def __init__(self, fw, e, name, is_dma=False):
        self.fw, self.e, self.name, self.is_dma = fw, e, name, is_dma
        self.count = 0
        self.waited = {}
        if not is_dma:
            self.sem = fw.nc.alloc_semaphore("s_" + name)
        else:
            self.nsem = 6
            self.sems = [fw.nc.alloc_semaphore("d_%s_%d" % (name, i)) for i in range(self.nsem)]

    def sem_val(self, idx):
        if not self.is_dma:
            return self.sem, idx
        i = idx - 1
        return self.sems[i % self.nsem], 16 * (i // self.nsem + 1)


class FW:
    def __init__(self, nc):
        self.nc = nc
        self.pe = Eng(self, nc.tensor, "pe")
        self.act = Eng(self, nc.scalar, "act")
        self.dve = Eng(self, nc.vector, "dve")
        self.pool = Eng(self, nc.gpsimd, "pool")
        self.sp = Eng(self, nc.sync, "sp", True)
        self.gq = Eng(self, nc.gpsimd, "gq", True)
        self.all = [self.pe, self.act, self.dve, self.pool, self.sp, self.gq]
        self.host = {"pe": self.pe, "act": self.act, "dve": self.dve, "pool": self.pool,
                     "sp": self.sp, "gq": self.pool}
        self.ninst = 0

    def _wait(self, eng, dep):
        de, di = dep
        host = self.host[eng.name]
        if de.is_dma:
            sem, val = de.sem_val(di)
            key = (de.name, (di - 1) % de.nsem)
        else:
            sem, val = de.sem, di
            key = de.name
        if host.waited.get(key, 0) >= val:
            return
        host.waited[key] = val
        eng.e.wait_ge(sem, val)

    def op(self, eng, fn, reads=(), writes=()):
        deps = []
        for r in reads:
            if r.w is not None:
                deps.append(r.w)
        for w in writes:
            if w.w is not None:
                deps.append(w.w)
            deps.extend(w.r)
        if eng.is_dma and eng.count >= eng.nsem:
            self._wait(eng, (eng, eng.count - eng.nsem + 1))
        seen = set()
        for d in deps:
            de, di = d
            k = (de.name, di)
            if k in seen:
                continue
            seen.add(k)
            if de is eng and not eng.is_dma:
                if eng is self.pe:
                    continue
                if not any((r.w is not None and r.w[0] is eng and r.w[1] == di) for r in reads):
                    continue
            self._wait(eng, d)
        inst = fn()
        eng.count += 1
        idx = eng.count
        sem, _ = eng.sem_val(idx)
        inst.then_inc(sem, 16 if eng.is_dma else 1)
        self.ninst += 1
        me = (eng, idx)
        for r in reads:
            r.r.append(me)
            if len(r.r) > 64:
                last = {}
                for (e2, i2) in r.r:
                    if e2.name not in last or last[e2.name][1] < i2:
                        last[e2.name] = (e2, i2)
                r.r = list(last.values())
        for w in writes:
            w.w = me
            w.r = []
        return inst

    def barrier(self):
        hosts = [self.pe, self.act, self.dve, self.pool, self.sp]
        for h in hosts:
            for e in self.all:
                if e.count == 0:
                    continue
                if e.is_dma:
                    for di in range(max(1, e.count - e.nsem + 1), e.count + 1):
                        self._wait(h, (e, di))
                elif e is not h:
                    self._wait(h, (e, e.count))


WEIGHT_SPECS = [
    ("l0_ada_w", (1024, 6144)), ("l0_ada_b", (6144,)), ("l0_norm_mix", (1024,)), ("l0_norm_ffn", (1024,)),
    ("l0_w_in", (1024, 2304)), ("l0_lam_q1", (64,)), ("l0_lam_k1", (64,)), ("l0_lam_q2", (64,)),
    ("l0_lam_k2", (64,)), ("l0_subln", (128,)), ("l0_q_norm", (64,)), ("l0_k_norm", (64,)),
    ("l0_w_o", (1024, 1024)), ("l0_ffn_up", (1024, 5632)), ("l0_ffn_conv_w", (3, 5632)),
    ("l0_ffn_conv_b", (5632,)), ("l0_ffn_down", (2816, 1024)),
    ("l1_ada_w", (1024, 6144)), ("l1_ada_b", (6144,)), ("l1_norm_mix", (1024,)), ("l1_norm_ffn", (1024,)),
    ("l1_w_in", (1024, 2048)), ("l1_conv_w", (4, 1024)), ("l1_conv_b", (1024,)),
    ("l1_gate_a_w", (2, 8, 128, 128)), ("l1_gate_a_b", (2, 1024)), ("l1_gate_x_w", (2, 8, 128, 128)),
    ("l1_gate_x_b", (2, 1024)), ("l1_a_param", (2, 1024)), ("l1_w_o", (1024, 1024)),
    ("l1_ffn_up", (1024, 5632)), ("l1_ffn_conv_w", (3, 5632)), ("l1_ffn_conv_b", (5632,)),
    ("l1_ffn_down", (2816, 1024)), ("final_norm", (1024,)),
]


def build(stop=99):
    nc = bass.Bass("TRN2", target_bir_lowering=False)
    fw = FW(nc)
    PE, ACT, DVE, POOL, SP, GQ = fw.pe, fw.act, fw.dve, fw.pool, fw.sp, fw.gq
    _uid = [0]

    def _sbt(name, shape, dt):
        _uid[0] += 1
        return nc.sbuf_tensor("%s_%d" % (name, _uid[0]), shape, dt)

    def _pst(name, shape, dt):
        _uid[0] += 1
        return nc.psum_tensor("%s_%d" % (name, _uid[0]), shape, dt)

    def din(name, shape):
        return nc.dram_tensor(name, list(shape), F32, kind="ExternalInput").ap()

    x_d = din("x", (2048, 1024)); ctx_d = din("ctx", (256, 1024)); c_d = din("c", (1024,)); cctx_d = din("c_ctx", (1024,))
    W = {n: din(n, s) for n, s in WEIGHT_SPECS}
    cos_d = din("rope_cos", (2048, 32)); sin_d = din("rope_sin", (2048, 32))
    out_d = nc.dram_tensor("out", [2048, 1024], F32, kind="ExternalOutput").ap()

    _rec = [None]

    def op(eng, f, r=(), w=()):
        if _rec[0] is not None:
            _rec[0].append((eng, f, list(r), list(w)))
            return None
        return fw.op(eng, f, reads=r, writes=w)

    def ve(eng):
        return nc.vector if eng is DVE else nc.gpsimd

    def mm(out, lhsT, rhs, st, sp_, r, w):
        op(PE, lambda: nc.tensor.matmul(out, lhsT, rhs, start=st, stop=sp_), r, w)

    def tr(out, in_, idt, r, w):
        op(PE, lambda: nc.tensor.transpose(out, in_, idt), r, w)

    def act(out, in_, func, r, w, **kw):
        op(ACT, lambda: nc.scalar.activation(out=out, in_=in_, func=func, **kw), r, w)

    def tt(eng, out, in0, in1, alu, r, w):
        op(eng, lambda: ve(eng).tensor_tensor(out=out, in0=in0, in1=in1, op=alu), r, w)

    def ts(eng, out, in0, s1, op0, r, w, s2=None, op1=None):
        if op1 is None:
            op(eng, lambda: ve(eng).tensor_scalar(out=out, in0=in0, scalar1=s1, scalar2=None, op0=op0), r, w)
        else:
            op(eng, lambda: ve(eng).tensor_scalar(out=out, in0=in0, scalar1=s1, scalar2=s2, op0=op0, op1=op1), r, w)

    def stt(out, in0, scalar, in1, op0, op1, r, w):
        op(DVE, lambda: nc.vector.scalar_tensor_tensor(out=out, in0=in0, scalar=scalar, in1=in1, op0=op0, op1=op1), r, w)

    def cp(eng, out, in_, r, w):
        if eng is ACT:
            op(ACT, lambda: nc.scalar.copy(out, in_), r, w)
        else:
            op(eng, lambda: ve(eng).tensor_copy(out=out, in_=in_), r, w)

    def recip(out, in_, r, w):
        op(DVE, lambda: nc.vector.reciprocal(out=out, in_=in_), r, w)

    def rsum(out, in_, r, w):
        op(DVE, lambda: nc.vector.reduce_sum(out=out, in_=in_, axis=AX.X), r, w)

    def memset(eng, ap, val, w):
        op(eng, lambda: ve(eng).memset(ap, val), (), w)

    def dma(q, out, in_, r, w):
        e = nc.sync if q is SP else nc.gpsimd
        op(q, lambda: e.dma_start(out=out, in_=in_), r, w)

    def bcast_mid(a, n):
        return bass.AP(a.tensor, a.offset, [a.ap[0], [0, n], *a.ap[1:]])

    def bcast_last(a, n):
        return bass.AP(a.tensor, a.offset, [*a.ap, [0, n]])

    sbt = nc.alloc_sbuf_tensor

    x_tok = sbt("x_tok", [128, NT, D], F32); Rx = regs(NT)
    hT = sbt("hT", [128, KC, T], BF16); RhT = regs(NT)
    identf = sbt("identf", [128, 128], F32); identb = sbt("identb", [128, 128], BF16); onesf = sbt("onesf", [128, 128], F32)
    Rc = Reg()
    NV = 640
    vecT = sbt("vecT", [128, NV], F32); Rvec = Reg()
    modT = sbt("modT", [128, 2, 48, 2], F32); Rmod = Reg()
    mgb = [sbt("mgb%d" % i, [128, D], F32) for i in range(2)]; Rmgb = regs(2)
    cs_tok = sbt("cs_tok", [128, 16, 32], F32); sn_tok = sbt("sn_tok", [128, 16, 32], F32); Rrope = Reg()
    Rjunk = Reg()
    ss = sbt("ss", [128, NT], F32); lnv = sbt("lnv", [128, NT], F32); rstd = sbt("rstd", [128, NT], F32); Rst = Reg()
    cst = sbt("cst", [128, 4], F32)
    scl = [sbt("scl%d" % i, [128, KC], F32) for i in range(2)]
    sft = [sbt("sft%d" % i, [128, KC], F32) for i in range(2)]
    Rsc = Reg()

    vec_list = [("c", c_d, 1024), ("c_ctx", cctx_d, 1024)]
    for l in (0, 1):
        vec_list += [("ada_b%d" % l, W["l%d_ada_b" % l], 6144), ("nmix%d" % l, W["l%d_norm_mix" % l], 1024),
                     ("nffn%d" % l, W["l%d_norm_ffn" % l], 1024),
                     ("fcw%d" % l, W["l%d_ffn_conv_w" % l].rearrange("a b -> (a b)"), 3 * 5632),
                     ("fcb%d" % l, W["l%d_ffn_conv_b" % l], 5632)]
    vec_list += [("l1cw", W["l1_conv_w"].rearrange("a b -> (a b)"), 4096), ("l1cb", W["l1_conv_b"], 1024),
                 ("gab", W["l1_gate_a_b"].rearrange("a b -> (a b)"), 2048),
                 ("gxb", W["l1_gate_x_b"].rearrange("a b -> (a b)"), 2048),
                 ("apar", W["l1_a_param"].rearrange("a b -> (a b)"), 2048)]
    voff = {}
    r0 = 0
    for name, ap_, n in vec_list:
        voff[name] = r0
        r0 += n // 128
    assert r0 <= NV

    with ExitStack() as _es:
        stg = _es.enter_context(_sbt("stg", [128, 5, 128], F32))
        adaw0 = _es.enter_context(_sbt("adaw0", [128, KC, 512], BF16))
        adaw1 = _es.enter_context(_sbt("adaw1", [128, KC, 512], BF16))
        scb = _es.enter_context(_sbt("scb", [128, KC, 2], BF16))
        sct = _es.enter_context(_sbt("sct", [128, 16], F32))
        pT0 = _es.enter_context(_pst("pT0", [128, 128], F32))
        modps = _es.enter_context(_pst("modps", [128, 96], F32))
        Rstg, Rscb, Rsct, RpT0, Rmp = Reg(), Reg(), Reg(), Reg(), Reg()
        adaw = [adaw0, adaw1]; Radaw = regs(2)
        for t in range(NT):
            src = ctx_d[t * 128:(t + 1) * 128, :] if t < 2 else x_d[(t - 2) * 128:(t - 1) * 128, :]
            dma(SP, x_tok[:, t, :], src, (), [Rx[t]])
        memset(POOL, identf[:], 0.0, [Rc])
        op(POOL, lambda: nc.gpsimd.affine_select(out=identf[:], in_=identf[:], compare_op=ALU.not_equal, fill=1.0,
                                                 base=0, pattern=[[-1, 128]], channel_multiplier=1), [Rc], [Rc])
        cp(POOL, identb[:], identf[:], [Rc], [Rc])
        memset(POOL, onesf[:], 1.0, [Rc])
        memset(POOL, cst[:, 0:1], EPS, [Rc])
        memset(POOL, cst[:, 1:2], 1.0, [Rc])
        memset(POOL, stg[:], 0.0, [Rstg])
        dma(SP, cs_tok[:], cos_d.rearrange("(i p) f -> p i f", p=128), (), [Rrope])
        dma(SP, sn_tok[:], sin_d.rearrange("(i p) f -> p i f", p=128), (), [Rrope])
        for name, ap_, n in vec_list:
            ra, rb = voff[name], voff[name] + n // 128
            r = ra
            while r < rb:
                s = r // 128
                e = min(rb, (s + 1) * 128)
                dma(SP, stg[r - s * 128:e - s * 128, s, :],
                    ap_[(r - ra) * 128:(e - ra) * 128].rearrange("(r p) -> r p", p=128), (), [Rstg])
                r = e
        for s in range(5):
            tr(pT0[:], stg[:, s, :], identf[:], [Rstg, Rc], [RpT0])
            cp(DVE, vecT[:, s * 128:(s + 1) * 128], pT0[:], [RpT0], [Rvec])
        act(sct[:], vecT[:, 0:16], AF.Exp, [Rvec], [Rsct], scale=-1.0)
        ts(DVE, sct[:], sct[:], 1.0, ALU.add, [Rsct], [Rsct])
        recip(sct[:], sct[:], [Rsct], [Rsct])
        tt(DVE, scb[:].rearrange("p k v -> p v k"), vecT[:, 0:16].rearrange("p (v k) -> p v k", v=2),
           sct[:].rearrange("p (v k) -> p v k", v=2), ALU.mult, [Rsct, Rvec], [Rscb])
        for l in (0, 1):
            wsrc = W["l%d_ada_w" % l].rearrange("(k p) n -> p k n", p=128)
            for blk in range(12):
                wb, Rw = adaw[blk % 2], Radaw[blk % 2]
                dma(GQ, wb[:], wsrc[:, :, blk * 512:(blk + 1) * 512], (), [Rw])
                for j in range(4):
                    col = (blk * 4 + j) * 2
                    for k in range(KC):
                        mm(modps[:, col:col + 2], wb[:, k, j * 128:(j + 1) * 128], scb[:, k, :], k == 0, k == KC - 1,
                           [Rw, Rscb], [Rmp])
            ab = vecT[:, voff["ada_b%d" % l]:voff["ada_b%d" % l] + 48]
            tt(DVE, modT[:, l, :, :], modps[:, 0:96].rearrange("p (j v) -> p j v", v=2), bcast_last(ab, 2), ALU.add,
               [Rmp, Rvec], [Rmod])
        fw.barrier()

    def prep_and_build(l, which, groups, pb, pst):
        gname = ("nmix%d" if which == 0 else "nffn%d") % l
        g = vecT[:, voff[gname]:voff[gname] + 8]
        vs = sorted(set(v for v, _ in groups))
        Rpb, Rdg = Reg(), regs(2)
        with ExitStack() as _es:
            dg0 = _es.enter_context(_sbt("dg0", [128, 128], F32))
            dg1 = _es.enter_context(_sbt("dg1", [128, 128], F32))
            xnb0 = _es.enter_context(_sbt("xnb0", [128, 4, D], BF16))
            xnb1 = _es.enter_context(_sbt("xnb1", [128, 4, D], BF16))
            junk = _es.enter_context(_sbt("junk", [128, D], BF16))
            dgs = [dg0, dg1]
            xnbs = [xnb0, xnb1]; Rxn = regs(2)
            for v in vs:
                s_scale = (3 * which + 1) * 8
                s_shift = (3 * which) * 8
                s_gate = (3 * which + 2) * 8
                stt(scl[v][:], modT[:, l, s_scale:s_scale + 8, v], 1.0, g, ALU.add, ALU.mult, [Rmod, Rvec], [Rsc])
                cp(DVE, sft[v][:], modT[:, l, s_shift:s_shift + 8, v], [Rmod], [Rsc])
                for j in range(8):
                    ts(DVE, dgs[j % 2][:], identf[:], modT[:, l, s_gate + j, v:v + 1], ALU.mult, [Rc, Rmod], [Rdg[j % 2]])
                    mm(pb[:, j * 128:(j + 1) * 128], onesf[:], dgs[j % 2][:], True, True, [Rc, Rdg[j % 2]], [Rpb])
                cp(DVE, mgb[v][:], pb[:, 0:D], [Rpb], [Rmgb[v]])
            tiles_all = [t for _, ts_ in groups for t in ts_]
            for t in tiles_all:
                act(junk[:], x_tok[:, t, :], AF.Square, [Rx[t]], [Rjunk, Rst], accum_out=ss[:, t:t + 1])
            t0, t1 = min(tiles_all), max(tiles_all) + 1
            act(lnv[:, t0:t1], ss[:, t0:t1], AF.Ln, [Rst], [Rst], scale=1.0 / D, bias=cst[:, 0:1])
            act(rstd[:, t0:t1], lnv[:, t0:t1], AF.Exp, [Rst], [Rst], scale=-0.5)
            Rps = regs(2)
            for gi, (v, tiles) in enumerate(groups):
                xn, Rn = xnbs[gi % 2], Rxn[gi % 2]
                for i, t in enumerate(tiles):
                    if i % 2:
                        act(xn[:, i, :], x_tok[:, t, :], AF.Identity, [Rx[t], Rst], [Rn], scale=rstd[:, t:t + 1])
                    else:
                        ts(DVE, xn[:, i, :], x_tok[:, t, :], rstd[:, t:t + 1], ALU.mult, [Rx[t], Rst], [Rn])
                n = len(tiles) * 128
                tok0 = tiles[0] * 128
                for half in range(2):
                    ps_, Rp = pst[half], Rps[half]
                    for ci in range(4):
                        c = half * 4 + ci
                        for i, t in enumerate(tiles):
                            tr(ps_[:, ci, i * 128:(i + 1) * 128], xn[:, i, c * 128:(c + 1) * 128], identb[:], [Rn, Rc], [Rp])
                        act(hT[:, c, tok0:tok0 + n], ps_[:, ci, 0:n], AF.Identity, [Rp, Rsc], [RhT[t] for t in tiles],
                            scale=scl[v][:, c:c + 1], bias=sft[v][:, c:c + 1])
            fw.barrier()

    def phase_hT(l, which, groups):
        with ExitStack() as _es:
            pb = _es.enter_context(_pst("pb", [128, D], F32))
            pst0 = _es.enter_context(_pst("pst0", [128, 4, 512], BF16))
            pst1 = _es.enter_context(_pst("pst1", [128, 4, 512], BF16))
            prep_and_build(l, which, groups, pb, [pst0, pst1])

    G_ALL = [(1, [0, 1])] + [(0, list(range(2 + 4 * i, 6 + 4 * i))) for i in range(4)]
    G_LAT = [(0, list(range(2 + 4 * i, 6 + 4 * i))) for i in range(4)]

    def resid_add(tq, v, lhsT, wmat, pwo, Rpwo, tmpb, Rtmp, rdeps):
        for ci, (c0, cw) in enumerate(((0, 384), (384, 384), (768, 256))):
            mm(pwo[:, 0:cw], lhsT, wmat[:, c0:c0 + cw], True, True, rdeps, [Rpwo])
            tb, Rt = tmpb[ci % 2], Rtmp[ci % 2]
            tt(DVE, tb[:, 0:cw], pwo[:, 0:cw], mgb[v][:, c0:c0 + cw], ALU.mult, [Rpwo, Rmgb[v]], [Rt])
            tt(POOL, x_tok[:, tq, c0:c0 + cw], x_tok[:, tq, c0:c0 + cw], tb[:, 0:cw], ALU.add, [Rt, Rx[tq]], [Rx[tq]])

    def phase_attn():
        w_in = W["l0_w_in"].rearrange("(k p) n -> p k n", p=128)
        w_o = W["l0_w_o"]
        NB = 2
        with ExitStack() as _es:
            wslab = _es.enter_context(_sbt("wslab", [128, NB, KC, 384], BF16))
            qkT = _es.enter_context(_sbt("qkT", [128, NB, 2, T], BF16))
            Va = _es.enter_context(_sbt("Va", [128, NB, NT, 129], BF16))
            kTb = _es.enter_context(_sbt("kTb", [128, T], BF16))
            Vb = _es.enter_context(_sbt("Vb", [128, NT, 2, 65], BF16))
            wou = _es.enter_context(_sbt("wou", [128, NB, D], BF16))
            stgq = _es.enter_context(_sbt("stgq", [128, 2, 256], BF16))
            pTb = _es.enter_context(_sbt("pTb", [128, 3, 1024], BF16))
            rt = _es.enter_context(_sbt("rt", [128, 2, 4, 128], F32))
            xs = _es.enter_context(_sbt("xs", [128, 2, 3, 128], F32))
            sm = _es.enter_context(_sbt("sm", [128, 4, 16], F32))
            fo = _es.enter_context(_sbt("fo", [128, 4, 3, 128], F32))
            ob = _es.enter_context(_sbt("ob", [128, 4, 128], BF16))
            oTs = _es.enter_context(_sbt("oTs", [128, 4, 128], BF16))
            accS = _es.enter_context(_sbt("accS", [128, 1, 4, 132], F32))
            tmpb_ = _es.enter_context(_sbt("tmpb", [128, 2, 512], F32))
            lamb = _es.enter_context(_sbt("lamb", [128, 4, 64], F32))
            lams = _es.enter_context(_sbt("lams", [128, 8], F32))
            nrmb = _es.enter_context(_sbt("nrmb", [128, 256], F32))
            pss0 = _es.enter_context(_pst("pss0", [128, 1024], F32))
            pss1 = _es.enter_context(_pst("pss1", [128, 1024], F32))
            accP = _es.enter_context(_pst("accP", [128, 4, 256], F32))
            bk6 = _es.enter_context(_pst("bk6", [128, 512], F32))
            bk7 = _es.enter_context(_pst("bk7", [128, 1024], BF16))
            pp = bk6[:, 0:384]
            pwo = bk6[:, 0:384]
            ptq = bk7[:, 0:256]
            pto = bk7[:, 256:512]
            Rws, Rqk, RVa, Rwou = regs(NB), regs(NB), regs(NB), regs(NB)
            RkTb, RVb, Rlam, Rnrm = Reg(), Reg(), Reg(), Reg()
            Rstq, RpT, Rrt, Rxs, Rsm, Rfo, Rob, RoT, Rtmp = regs(2), regs(3), regs(2), regs(2), regs(4), regs(4), regs(4), regs(4), regs(2)
            RaccS = regs(2)
            Rpss, Racc, Rpp, Rptq, Rpto = regs(2), regs(4), Reg(), Reg(), Reg()
            Rpwo = Rpp
            pss = [pss0, pss1]
            tmpb = [tmpb_[:, 0, :], tmpb_[:, 1, :]]
            for i, nm in enumerate(("l0_lam_q1", "l0_lam_k1", "l0_lam_q2", "l0_lam_k2")):
                dma(SP, lamb[:, i, :], W[nm].partition_broadcast(128), (), [Rlam])
            dma(SP, nrmb[:, 0:128], W["l0_subln"].partition_broadcast(128), (), [Rnrm])
            dma(SP, nrmb[:, 128:192], W["l0_q_norm"].partition_broadcast(128), (), [Rnrm])
            dma(SP, nrmb[:, 192:256], W["l0_k_norm"].partition_broadcast(128), (), [Rnrm])
            lam_init = 0.8 - 0.6 * math.exp(-0.3 * 0)
            ts(DVE, nrmb[:, 0:128], nrmb[:, 0:128], 1.0 - lam_init, ALU.mult, [Rnrm], [Rnrm])
            tt(DVE, lamb[:, 0, :], lamb[:, 0, :], lamb[:, 1, :], ALU.mult, [Rlam], [Rlam])
            tt(DVE, lamb[:, 2, :], lamb[:, 2, :], lamb[:, 3, :], ALU.mult, [Rlam], [Rlam])
            rsum(lams[:, 0:1], lamb[:, 0, :], [Rlam], [Rlam])
            rsum(lams[:, 1:2], lamb[:, 2, :], [Rlam], [Rlam])
            act(lams[:, 2:4], lams[:, 0:2], AF.Exp, [Rlam], [Rlam])
            tt(DVE, lams[:, 4:5], lams[:, 3:4], lams[:, 2:3], ALU.subtract, [Rlam], [Rlam])
            ts(DVE, lams[:, 5:6], lams[:, 4:5], -lam_init, ALU.add, [Rlam], [Rlam])
            for b in range(NB):
                memset(POOL, Va[:, b, :, 128:129], 1.0, [RVa[b]])
            memset(POOL, Vb[:, :, :, 64:65], 1.0, [RVb])
            nlam = lams[:, 5:6]
            subln_b = nrmb[:, 0:128]
            qn_b = nrmb[:, 128:192]
            kn_b = nrmb[:, 192:256]

            def rope(src, H, t, dst, si, rsrc):
                i = t - 2
                cs = bcast_mid(cs_tok[:, i, :], H); sn = bcast_mid(sn_tok[:, i, :], H)
                sv = src.rearrange("p (h i two) -> p h i two", h=H, two=2)
                dv = dst.rearrange("p (h i two) -> p h i two", h=H, two=2)
                x0, x1 = sv[:, :, :, 0], sv[:, :, :, 1]
                tmp = [rt[:, si, j, 0:H * 32].rearrange("p (h i) -> p h i", h=H) for j in range(4)]
                tt(DVE, tmp[0], x0, cs, ALU.mult, rsrc + [Rrope], [Rrt[si]])
                tt(DVE, tmp[1], x1, sn, ALU.mult, rsrc + [Rrope], [Rrt[si]])
                tt(DVE, tmp[2], x0, sn, ALU.mult, rsrc + [Rrope], [Rrt[si]])
                tt(DVE, tmp[3], x1, cs, ALU.mult, rsrc + [Rrope], [Rrt[si]])
                tt(POOL, dv[:, :, :, 0], tmp[0], tmp[1], ALU.subtract, [Rrt[si]], [Rstq[si]])
                tt(POOL, dv[:, :, :, 1], tmp[2], tmp[3], ALU.add, [Rrt[si]], [Rstq[si]])

            def qknorm(src, gain_b, si, Rp_):
                a, b2, c2 = xs[:, si, 0, :], xs[:, si, 1, :], xs[:, si, 2, :]
                cp(DVE, a, src, Rp_, [Rxs[si]])
                tt(POOL, b2, a, a, ALU.mult, [Rxs[si]], [Rxs[si]])
                rsum(sm[:, si, 0:2], b2.rearrange("p (h d) -> p h d", h=2), [Rxs[si]], [Rsm[si]])
                act(sm[:, si, 2:4], sm[:, si, 0:2], AF.Ln, [Rsm[si]], [Rsm[si]], scale=1.0 / 64, bias=cst[:, 0:1])
                act(sm[:, si, 4:6], sm[:, si, 2:4], AF.Exp, [Rsm[si]], [Rsm[si]], scale=-0.5)
                tt(DVE, b2.rearrange("p (h d) -> p h d", h=2), a.rearrange("p (h d) -> p h d", h=2),
                   bcast_last(sm[:, si, 4:6], 64), ALU.mult, [Rxs[si], Rsm[si]], [Rxs[si]])
                tt(POOL, c2.rearrange("p (h d) -> p h d", h=2), b2.rearrange("p (h d) -> p h d", h=2),
                   bcast_mid(gain_b, 2), ALU.mult, [Rxs[si], Rnrm], [Rxs[si]])
                return c2

            ppbuf = [accP[:, 0:2, :].rearrange("p a c -> p (a c)"), accP[:, 2:4, :].rearrange("p a c -> p (a c)")]
            Rppbuf = [[Racc[0], Racc[1]], [Racc[2], Racc[3]]]
            ptqs = [bk7[:, 0:256], bk7[:, 256:512]]
            ptos = [bk7[:, 512:640], bk7[:, 640:768]]
            Rbk7 = Reg()
            Rptqs, Rptos = [Rbk7, Rbk7], [Rbk7, Rbk7]
            Rbk6 = Reg()

            def load_slab(kind, idx, b, sb_):
                if kind == "diff":
                    cols = [(idx * 128, 128), (512 + idx * 128, 128), (1024 + idx * 128, 128)]
                elif kind == "kv":
                    cols = [(2048, 256)]
                else:
                    cols = [(1536 + 64 * idx, 64), (1536 + 64 * (idx + 4), 64)]
                o = 0
                for (c0, cw) in cols:
                    dma(GQ, wslab[:, sb_, :, o:o + cw], w_in[:, :, c0:c0 + cw], (), [Rws[sb_]])
                    o += cw
                if kind == "diff":
                    dma(GQ, wou[:, b, :], w_o[idx * 128:(idx + 1) * 128, :], (), [Rwou[b]])
                elif kind == "gq":
                    dma(GQ, wou[0:64, b, :], w_o[512 + 64 * idx:512 + 64 * idx + 64, :], (), [Rwou[b]])
                    dma(GQ, wou[64:128, b, :], w_o[512 + 64 * (idx + 4):512 + 64 * (idx + 4) + 64, :], (), [Rwou[b]])

            def project(kind, idx, b, sb_):
                if kind == "diff":
                    cols = [(idx * 128, 128), (512 + idx * 128, 128), (1024 + idx * 128, 128)]
                elif kind == "kv":
                    cols = [(2048, 256)]
                else:
                    cols = [(1536 + 64 * idx, 64), (1536 + 64 * (idx + 4), 64)]
                ncol = sum(cw for _, cw in cols)
                pending = []
                for t0_ in range(0, NT, 2):
                    pair = [t0_, t0_ + 1]
                    for t in pair:
                        si = t % 2
                        for k in range(KC):
                            mm(ppbuf[si][:, 0:ncol], hT[:, k, t * 128:(t + 1) * 128], wslab[:, sb_, k, 0:ncol], k == 0, k == KC - 1,
                               [RhT[t], Rws[sb_]], Rppbuf[si])
                    for pfn in pending:
                        pfn()
                    pending = []
                    recs = []
                    for t in pair:
                        si = t % 2
                        pp = ppbuf[si]
                        Rp = Rppbuf[si]
                        lat = t >= 2
                        ptq, Rptq = ptqs[si], Rptqs[si]
                        _rec[0] = []
                        if kind == "diff":
                            if lat:
                                rope(pp[:, 0:256], 4, t, stgq[:, si, 0:256], si, Rp)
                            else:
                                cp(DVE, stgq[:, si, 0:256], pp[:, 0:256], Rp, [Rstq[si]])
                            cp(DVE, Va[:, b, t, 0:128], pp[:, 256:384], Rp, [RVa[b]])

                            def pfn(t=t, si=si, ptq=ptq, Rptq=Rptq):
                                tr(ptq[:, 0:128], stgq[:, si, 0:128], identb[:], [Rstq[si], Rc], [Rptq])
                                tr(ptq[:, 128:256], stgq[:, si, 128:256], identb[:], [Rstq[si], Rc], [Rptq])
                                cp(DVE, qkT[:, b, :, t * 128:(t + 1) * 128], ptq[:, 0:256].rearrange("p (a q) -> p a q", a=2),
                                   [Rptq], [Rqk[b]])
                        elif kind == "kv":
                            xn_ = qknorm(pp[:, 0:128], kn_b, si, Rp)
                            cp(DVE, Vb[:, t, :, 0:64], pp[:, 128:256].rearrange("p (g d) -> p g d", g=2), Rp, [RVb])
                            if lat:
                                rope(xn_, 2, t, stgq[:, si, 0:128], si, [Rxs[si]])
                            else:
                                cp(POOL, stgq[:, si, 0:128], xn_, [Rxs[si]], [Rstq[si]])

                            def pfn(t=t, si=si, ptq=ptq, Rptq=Rptq):
                                tr(ptq[:, 0:128], stgq[:, si, 0:128], identb[:], [Rstq[si], Rc], [Rptq])
                                cp(DVE, kTb[:, t * 128:(t + 1) * 128], ptq[:, 0:128], [Rptq], [RkTb])
                        else:
                            xn_ = qknorm(pp[:, 0:128], qn_b, si, Rp)
                            if lat:
                                rope(xn_, 2, t, stgq[:, si, 0:128], si, [Rxs[si]])
                            else:
                                cp(POOL, stgq[:, si, 0:128], xn_, [Rxs[si]], [Rstq[si]])

                            def pfn(t=t, si=si, ptq=ptq, Rptq=Rptq):
                                tr(ptq[:, 0:128], stgq[:, si, 0:128], identb[:], [Rstq[si], Rc], [Rptq])
                                cp(DVE, qkT[:, b, 0, t * 128:(t + 1) * 128], ptq[:, 0:128], [Rptq], [Rqk[b]])
                        recs.append(_rec[0])
                        _rec[0] = None
                        pending.append(pfn)
                    for k_ in range(max(len(r_) for r_ in recs)):
                        for r_ in recs:
                            if k_ < len(r_):
                                e_, f_, rr_, ww_ = r_[k_]
                                fw.op(e_, f_, reads=rr_, writes=ww_)
                for pfn in pending:
                    pfn()

            def attend(kind, idx, b):
                if kind == "diff":
                    dv1 = 129
                    kT_ap = lambda s, kt: qkT[s * 64:(s + 1) * 64, b, 1, kt * 128:(kt + 1) * 128]
                    v_ap = lambda s, kt: Va[:, b, kt, 0:129]
                    Rk, Rv = Rqk[b], RVa[b]
                else:
                    dv1 = 65
                    kT_ap = lambda s, kt: kTb[s * 64:(s + 1) * 64, kt * 128:(kt + 1) * 128]
                    v_ap = lambda s, kt: Vb[:, kt, s, 0:65]
                    Rk, Rv = RkTb, RVb
                blocks = [(0, [0, 1])] + [(256 + 256 * i, list(range(NT))) for i in range(8)]
                G = []
                for bi_, (q0, ktiles) in enumerate(blocks):
                    for g0 in range(0, len(ktiles), 2):
                        G.append((bi_, q0, ktiles[g0:g0 + 2], g0 == 0, g0 + 2 >= len(ktiles)))
                n_g = len(G)

                def QK(i):
                    bi_, q0, grp, first, last = G[i]
                    pb_, Rp = pss[i % 2], Rpss[i % 2]
                    for gi, kt in enumerate(grp):
                        for s in range(2):
                            o_ = s * 512 + gi * 256
                            mm(pb_[:, o_:o_ + 256], kT_ap(s, kt), qkT[s * 64:(s + 1) * 64, b, 0, q0:q0 + 256], True, True,
                               [Rk, Rqk[b]], [Rp])

                def EXP(i):
                    grp = G[i][2]
                    n = len(grp) * 256
                    src = pss[i % 2][:, :].rearrange("p (s c) -> p s c", s=2)[:, :, 0:n]
                    dst = pTb[:, i % 3, :].rearrange("p (s c) -> p s c", s=2)[:, :, 0:n]
                    act(dst, src, AF.Exp, [Rpss[i % 2]], [RpT[i % 3]], scale=0.125)

                def PV(i):
                    bi_, q0, grp, first, last = G[i]
                    for gi, kt in enumerate(grp):
                        for s in range(2):
                            for qi in range(2):
                                ai = s * 2 + qi
                                o_ = s * 512 + gi * 256 + qi * 128
                                mm(accP[:, ai, 0:dv1], pTb[:, i % 3, o_:o_ + 128], v_ap(s, kt),
                                   first and gi == 0 and qi == 0, last and gi == len(grp) - 1, [RpT[i % 3], Rv], [Racc[ai]])

                def FIN_A(bi_, q0):
                    fb = bi_ % 2
                    fa = 0
                    cp(DVE, accS[:, fb * 0, 0:2, 0:dv1], accP[:, 0:2, 0:dv1], [Racc[0], Racc[1]], [RaccS[0]])
                    cp(DVE, accS[:, 0, 2:4, 0:dv1], accP[:, 2:4, 0:dv1], [Racc[2], Racc[3]], [RaccS[0]])
                    for qi in range(2):
                        sl = fb * 2 + qi
                        if kind == "diff":
                            a0, a1 = accS[:, 0, 0 + qi, :], accS[:, 0, 2 + qi, :]
                            recip(sm[:, sl, 8:9], a0[:, 128:129], [RaccS[0]], [Rsm[sl]])
                            recip(sm[:, sl, 9:10], a1[:, 128:129], [RaccS[0]], [Rsm[sl]])
                            tt(DVE, sm[:, sl, 10:11], sm[:, sl, 9:10], nlam, ALU.mult, [Rsm[sl], Rlam], [Rsm[sl]])
                            ts(DVE, fo[:, sl, 0, :], a0[:, 0:128], sm[:, sl, 8:9], ALU.mult, [RaccS[0], Rsm[sl]], [Rfo[sl]])
                            stt(fo[:, sl, 1, :], a1[:, 0:128], sm[:, sl, 10:11], fo[:, sl, 0, :], ALU.mult, ALU.add,
                                [RaccS[0], Rsm[sl], Rfo[sl]], [Rfo[sl]])
                            tt(POOL, fo[:, sl, 2, :], fo[:, sl, 1, :], fo[:, sl, 1, :], ALU.mult, [Rfo[sl]], [Rfo[sl]])
                            rsum(sm[:, sl, 11:12], fo[:, sl, 2, :], [Rfo[sl]], [Rsm[sl]])
                        else:
                            for s in range(2):
                                a_ = accS[:, 0, s * 2 + qi, :]
                                recip(sm[:, sl, 8 + s:9 + s], a_[:, 64:65], [RaccS[0]], [Rsm[sl]])
                                ts(DVE, ob[:, sl, s * 64:(s + 1) * 64], a_[:, 0:64], sm[:, sl, 8 + s:9 + s], ALU.mult,
                                   [RaccS[0], Rsm[sl]], [Rob[sl]])

                def FIN_A2(bi_, q0):
                    if kind != "diff":
                        return
                    fb = bi_ % 2
                    for qi in range(2):
                        sl = fb * 2 + qi
                        act(sm[:, sl, 12:13], sm[:, sl, 11:12], AF.Ln, [Rsm[sl]], [Rsm[sl]], scale=1.0 / 128, bias=cst[:, 0:1])
                        act(sm[:, sl, 13:14], sm[:, sl, 12:13], AF.Exp, [Rsm[sl]], [Rsm[sl]], scale=-0.5)
                        stt(ob[:, sl, :], fo[:, sl, 1, :], sm[:, sl, 13:14], subln_b, ALU.mult, ALU.mult,
                            [Rfo[sl], Rsm[sl], Rnrm], [Rob[sl]])

                def FIN_B0(bi_, q0):
                    fb = bi_ % 2
                    for qi in range(2):
                        sl = fb * 2 + qi
                        pto, Rpto = ptos[qi], Rptos[qi]
                        tr(pto[:, 0:128], ob[:, sl, :], identb[:], [Rob[sl], Rc], [Rpto])
                    for qi in range(2):
                        sl = fb * 2 + qi
                        pto, Rpto = ptos[qi], Rptos[qi]
                        cp(DVE, oTs[:, sl, :], pto[:, 0:128], [Rpto], [RoT[sl]])

                def FIN_Bk(bi_, q0, kk):
                    fb = bi_ % 2
                    qi, j = kk // 2, kk % 2
                    sl = fb * 2 + qi
                    tq = q0 // 128 + qi
                    v = 1 if tq < 2 else 0
                    h = j % 2
                    c0 = j * 512
                    mm(bk6[:, 0:512], oTs[:, sl, :], wou[:, b, c0:c0 + 512], True, True, [RoT[sl], Rwou[b]], [Rbk6])
                    tt(DVE, tmpb[h][:, 0:512], bk6[:, 0:512], mgb[v][:, c0:c0 + 512], ALU.mult, [Rbk6, Rmgb[v]], [Rtmp[h]])
                    tt(POOL, x_tok[:, tq, c0:c0 + 512], x_tok[:, tq, c0:c0 + 512], tmpb[h][:, 0:512], ALU.add,
                       [Rtmp[h], Rx[tq]], [Rx[tq]])

                def run_stage(st_, b2_, q2_):
                    if st_ == 0:
                        FIN_A2(b2_, q2_)
                    elif st_ == 1:
                        FIN_B0(b2_, q2_)
                    else:
                        FIN_Bk(b2_, q2_, st_ - 2)

                sched = []
                QK(0)
                if n_g > 1:
                    QK(1)
                for i in range(n_g):
                    EXP(i)
                    if i + 2 < n_g:
                        QK(i + 2)
                    PV(i)
                    bi_, q0, grp, first, last = G[i]
                    if last:
                        FIN_A(bi_, q0)
                        sched.append((i + 4, 0, bi_, q0))
                        sched.append((i + 6, 1, bi_, q0))
                        for kk in range(4):
                            sched.append((i + 7 + kk, 2 + kk, bi_, q0))
                        sched.sort()
                    while sched and sched[0][0] <= i:
                        _, st_, b2_, q2_ = sched.pop(0)
                        run_stage(st_, b2_, q2_)
                for (_, st_, b2_, q2_) in sched:
                    run_stage(st_, b2_, q2_)

            units = [("diff", h) for h in range(4)] + [("kv", 0)] + [("gq", j) for j in range(4)]
            plan = []
            bi = 0
            for pos, (kind, idx) in enumerate(units):
                if kind == "kv":
                    plan.append((kind, idx, bi % NB, pos % 2))
                else:
                    plan.append((kind, idx, bi % NB, pos % 2))
                    bi += 1
            load_slab(*plan[0])
            for pos, (kind, idx, b, sb_) in enumerate(plan):
                project(kind, idx, b, sb_)
                if pos + 1 < len(plan):
                    load_slab(*plan[pos + 1])
                if kind != "kv":
                    attend(kind, idx, b)
            fw.barrier()

    def phase_ffn(l, with_ctx):
        w_up = W["l%d_ffn_up" % l].rearrange("(k p) n -> p k n", p=128)
        w_dn = W["l%d_ffn_down" % l]
        cw0 = voff["fcw%d" % l]; cb0 = voff["fcb%d" % l]
        groups = [list(range(0, 4)), list(range(4, 8)), list(range(8, 12)), list(range(12, 16)), list(range(16, 19)), list(range(19, 22))]
        blocks = []
        if with_ctx:
            blocks.append((0, 256, 0, 256, True, True))
        s_ = 256
        for nt_ in (3, 3, 3, 3, 3, 1):
            blocks.append((s_, s_ + nt_ * 128, 256, T))
            s_ += nt_ * 128
        blocks = [(b_[0], b_[1], b_[2], b_[3]) for b_ in blocks]
        with ExitStack() as _es:
            wup = _es.enter_context(_sbt("wup", [128, 2, KC, 4, 256], BF16))
            wdn = _es.enter_context(_sbt("wdn", [128, 2, 4, D], BF16))
            actT = _es.enter_context(_sbt("actT", [128, 2, 4, 384], BF16))
            cvg = _es.enter_context(_sbt("cvg", [128, 2, 2, 384], F32))
            sgt = _es.enter_context(_sbt("sgt", [128, 2, 384], F32))
            tmpf = _es.enter_context(_sbt("tmpf", [128, 2, 512], F32))
            pu0 = _es.enter_context(_pst("pu0", [128, 2, 512], F32))
            pu1 = _es.enter_context(_pst("pu1", [128, 2, 512], F32))
            pwof = _es.enter_context(_pst("pwof", [128, 512], F32))
            pwof2 = _es.enter_context(_pst("pwof2", [128, 512], F32))
            Rwup, Rwdn, RaT, Rcv2, Rsg, Rtmp, Rpu2 = regs(2), regs(2), regs(2), [regs(2), regs(2)], regs(2), regs(2), [regs(2), regs(2)]
            Rpwo = Reg()
            pus = [pu0, pu1]
            tmpb = [tmpf[:, 0, :], tmpf[:, 1, :]]
            it = 0
            blk_ctr = [0]
            pend_list = []
            pwoH = [pwof[:, 0:512], pwof2[:, 0:512]]
            RpwoH = regs(2)
            def load_group(gi_):
                wb_ = gi_ % 2
                for pi, j in enumerate(groups[gi_]):
                    dma(GQ, wup[:, wb_, :, pi, 0:128], w_up[:, :, j * 128:(j + 1) * 128], (), [Rwup[wb_]])
                    dma(GQ, wup[:, wb_, :, pi, 128:256], w_up[:, :, DFF + j * 128:DFF + (j + 1) * 128], (), [Rwup[wb_]])
                    dma(GQ, wdn[:, wb_, pi, :], w_dn[j * 128:(j + 1) * 128, :], (), [Rwdn[wb_]])

            load_group(0)
            for gi, pairs in enumerate(groups):
                wb = gi % 2
                for bi_, (st, en, seg0, seg1) in enumerate(blocks):
                    in0 = max(seg0, st - 1); in1 = min(seg1, en + 1)
                    n_in = in1 - in0; n = en - st
                    L = st - in0
                    Rr = in1 - en
                    ab = blk_ctr[0] % 2
                    blk_ctr[0] += 1
                    tiles = list(range(st // 128, en // 128))
                    for pi, j in enumerate(pairs):
                        pu, Rp2 = pus[it % 2], Rpu2[it % 2]
                        cb_ = it % 2
                        it += 1
                        for half in range(2):
                            for k in range(KC):
                                mm(pu[:, half, 0:n_in], wup[:, wb, k, pi, half * 128:(half + 1) * 128], hT[:, k, in0:in1],
                                   k == 0, k == KC - 1, [Rwup[wb]] + [RhT[t] for t in range(in0 // 128, (in1 - 1) // 128 + 1)], [Rp2[half]])
                        for half in range(2):
                            fch = j + half * NPAIR
                            w0 = vecT[:, cw0 + fch:cw0 + fch + 1]
                            w1 = vecT[:, cw0 + 44 + fch:cw0 + 44 + fch + 1]
                            w2 = vecT[:, cw0 + 88 + fch:cw0 + 88 + fch + 1]
                            bb = vecT[:, cb0 + fch:cb0 + fch + 1]
                            c_ = cvg[:, cb_, half, :]
                            Rp = Rp2[half]
                            Rcvh = Rcv2[cb_][half]
                            act(c_[:, 0:n], pu[:, half, L:L + n], AF.Identity, [Rp, Rvec], [Rcvh], scale=w1, bias=bb)
                            if L == 1:
                                stt(c_[:, 0:n], pu[:, half, 0:n], w0, c_[:, 0:n], ALU.mult, ALU.add, [Rp, Rvec, Rcvh], [Rcvh])
                            else:
                                stt(c_[:, 1:n], pu[:, half, 0:n - 1], w0, c_[:, 1:n], ALU.mult, ALU.add, [Rp, Rvec, Rcvh], [Rcvh])
                            if Rr == 1:
                                stt(c_[:, 0:n], pu[:, half, L + 1:L + 1 + n], w2, c_[:, 0:n], ALU.mult, ALU.add, [Rp, Rvec, Rcvh], [Rcvh])
                            else:
                                stt(c_[:, 0:n - 1], pu[:, half, L + 1:L + n], w2, c_[:, 0:n - 1], ALU.mult, ALU.add, [Rp, Rvec, Rcvh], [Rcvh])
                        act(sgt[:, cb_, 0:n], cvg[:, cb_, 1, 0:n], AF.Silu, [Rcv2[cb_][1]], [Rsg[cb_]])
                        tt(DVE, actT[:, ab, pi, 0:n], sgt[:, cb_, 0:n], cvg[:, cb_, 0, 0:n], ALU.mult, [Rsg[cb_], Rcv2[cb_][0]], [RaT[ab]])
                        if pend_list:
                            pend_list.pop(0)()
                    def DOWN(ti, tq, ab=ab, wb=wb, npair=len(pairs)):
                        if True:
                            v = 1 if tq < 2 else 0
                            for jj in range(2):
                                h = jj % 2
                                c0 = jj * 512
                                for pi in range(npair):
                                    mm(pwoH[h], actT[:, ab, pi, ti * 128:(ti + 1) * 128], wdn[:, wb, pi, c0:c0 + 512],
                                       pi == 0, pi == npair - 1, [RaT[ab], Rwdn[wb]], [RpwoH[h]])
                                tt(DVE, tmpb[h][:, 0:512], pwoH[h], mgb[v][:, c0:c0 + 512], ALU.mult, [RpwoH[h], Rmgb[v]], [Rtmp[h]])
                                tt(POOL, x_tok[:, tq, c0:c0 + 512], x_tok[:, tq, c0:c0 + 512], tmpb[h][:, 0:512], ALU.add,
                                   [Rtmp[h], Rx[tq]], [Rx[tq]])
                    while pend_list:
                        pend_list.pop(0)()
                    for ti, tq in enumerate(tiles):
                        pend_list.append(lambda ti=ti, tq=tq, DOWN=DOWN: DOWN(ti, tq))
                    if bi_ == 0 and gi + 1 < len(groups):
                        load_group(gi + 1)
            while pend_list:
                pend_list.pop(0)()
            fw.barrier()

    def phase_lru():
        w_in = W["l1_w_in"].rearrange("(k p) n -> p k n", p=128)
        w_o = W["l1_w_o"]
        PIECES = [(0, 256), (256, 1280), (1280, T)]
        with ExitStack() as _es:
            wl = _es.enter_context(_sbt("wl", [128, 2, KC, 256], BF16))
            gw = _es.enter_context(_sbt("gw", [128, 2, 4, 128], BF16))
            woc = _es.enter_context(_sbt("woc", [128, 2, D], BF16))
            upre = _es.enter_context(_sbt("upre", [128, T], F32))
            uu = _es.enter_context(_sbt("uu", [128, T], F32))
            ub = _es.enter_context(_sbt("ub", [128, T], BF16))
            hsf = _es.enter_context(_sbt("hsf", [128, 2048], F32))
            pa = _es.enter_context(_sbt("pa", [128, 2, 1024], F32))
            pq = _es.enter_context(_sbt("pq", [128, 2, 1024], F32))
            hsb = _es.enter_context(_sbt("hsb", [128, 2, 1024], F32))
            gt = _es.enter_context(_sbt("gt", [128, 3, 512], F32))
            gb = _es.enter_context(_sbt("gb", [128, 2048], BF16))
            yb = _es.enter_context(_sbt("yb", [128, 2048], BF16))
            lv = _es.enter_context(_sbt("lv", [128, 64], F32))
            carry = _es.enter_context(_sbt("carry", [128, 4], F32))
            pu = _es.enter_context(_pst("pu", [128, 2, 512], F32))
            pg = _es.enter_context(_pst("pg", [128, 2, 512], F32))
            pwol = _es.enter_context(_pst("pwol", [128, 512], F32))
            pwol2 = _es.enter_context(_pst("pwol2", [128, 512], F32))
            Rwl, Rgw, Rwoc = regs(2), regs(2), regs(2)
            Rupre, Ruu, Rub, Rhsf, Rpa, Rpq, Rhsb, Rgt, Rgb, Ryb, Rlv, Rcar = (Reg() for _ in range(12))
            Rtmp = regs(2); Rpu, Rpg = regs(2), regs(2); Rpwo = Reg()
            Rpa2, Rpq2, Rhsb2 = regs(2), regs(2), regs(2)
            pcnt = [0]
            pr_ = upre
            ap0 = voff["apar"]
            act(lv[:, 0:16], vecT[:, ap0:ap0 + 16], AF.Exp, [Rvec], [Rlv], scale=-1.0)
            act(lv[:, 0:16], lv[:, 0:16], AF.Ln, [Rlv], [Rlv], scale=1.0, bias=cst[:, 1:2])
            ts(DVE, lv[:, 16:32], lv[:, 0:16], -16.0, ALU.mult, [Rlv], [Rlv])
            ts(DVE, lv[:, 0:16], lv[:, 0:16], -8.0, ALU.mult, [Rlv], [Rlv])
            ts(DVE, lv[:, 32:48], vecT[:, voff["gab"]:voff["gab"] + 16], -1.0, ALU.mult, [Rvec], [Rlv])
            ts(DVE, lv[:, 48:64], vecT[:, voff["gxb"]:voff["gxb"] + 16], -1.0, ALU.mult, [Rvec], [Rlv])
            cw0 = voff["l1cw"]; cb0 = voff["l1cb"]
            blocks5 = [(0, 256)] + [(256 + 512 * i, 256 + 512 * (i + 1)) for i in range(4)]
            def load_chunk(c_):
                b_ = c_ % 2
                dma(GQ, wl[:, b_, :, 0:128], w_in[:, :, D + c_ * 128:D + (c_ + 1) * 128], (), [Rwl[b_]])
                dma(GQ, wl[:, b_, :, 128:256], w_in[:, :, c_ * 128:(c_ + 1) * 128], (), [Rwl[b_]])
                for d_ in range(2):
                    dma(GQ, gw[:, b_, d_ * 2 + 0, :], W["l1_gate_a_w"][d_, c_], (), [Rgw[b_]])
                    dma(GQ, gw[:, b_, d_ * 2 + 1, :], W["l1_gate_x_w"][d_, c_], (), [Rgw[b_]])
                dma(GQ, woc[:, b_, :], w_o[c_ * 128:(c_ + 1) * 128, :], (), [Rwoc[b_]])

            Rpw2 = regs(2)
            rpend = []

            def drain(k):
                for _ in range(min(k, len(rpend))):
                    rpend.pop(0)()

            load_chunk(0)
            for c in range(KC):
                b = c % 2
                tt(DVE, woc[:, b, :], woc[:, b, :], mgb[0][:], ALU.mult, [Rwoc[b], Rmgb[0]], [Rwoc[b]])
                for bi_, (s0, s1) in enumerate(blocks5):
                    n = s1 - s0
                    hr = [RhT[t] for t in range(s0 // 128, s1 // 128)]
                    for k in range(KC):
                        mm(pu[:, bi_ % 2, 0:n], wl[:, b, k, 0:128], hT[:, k, s0:s1], k == 0, k == KC - 1, [Rwl[b]] + hr, [Rpu[bi_ % 2]])
                    cp(ACT, upre[:, s0:s1], pu[:, bi_ % 2, 0:n], [Rpu[bi_ % 2]], [Rupre])
                    if s0 >= 256:
                        for k in range(KC):
                            mm(pg[:, bi_ % 2, 0:n], wl[:, b, k, 128:256], hT[:, k, s0:s1], k == 0, k == KC - 1, [Rwl[b]] + hr, [Rpg[bi_ % 2]])
                        xg, t1, t2 = gt[:, 0, 0:n], gt[:, 1, 0:n], gt[:, 2, 0:n]
                        cp(ACT, xg, pg[:, bi_ % 2, 0:n], [Rpg[bi_ % 2]], [Rgt])
                        tt(POOL, t1, xg, xg, ALU.mult, [Rgt], [Rgt])
                        ts(DVE, t1, t1, 0.044715, ALU.mult, [Rgt], [Rgt], s2=1.0, op1=ALU.add)
                        tt(DVE, t1, t1, xg, ALU.mult, [Rgt], [Rgt])
                        act(t2, t1, AF.Sigmoid, [Rgt], [Rgt], scale=1.5957691216057308)
                        tt(POOL, gb[:, s0 - 256:s1 - 256], xg, t2, ALU.mult, [Rgt], [Rgb])
                drain(6)
                wv = [vecT[:, cw0 + j * 8 + c:cw0 + j * 8 + c + 1] for j in range(4)]
                bv = vecT[:, cb0 + c:cb0 + c + 1]
                for (g0, g1) in ((0, 256), (256, T)):
                    act(uu[:, g0:g1], upre[:, g0:g1], AF.Identity, [Rupre, Rvec], [Ruu], scale=wv[2], bias=bv)
                    stt(uu[:, g0 + 2:g1], upre[:, g0:g1 - 2], wv[0], uu[:, g0 + 2:g1], ALU.mult, ALU.add, [Rupre, Rvec, Ruu], [Ruu])
                    stt(uu[:, g0 + 1:g1], upre[:, g0:g1 - 1], wv[1], uu[:, g0 + 1:g1], ALU.mult, ALU.add, [Rupre, Rvec, Ruu], [Ruu])
                    stt(uu[:, g0:g1 - 1], upre[:, g0 + 1:g1], wv[3], uu[:, g0:g1 - 1], ALU.mult, ALU.add, [Rupre, Rvec, Ruu], [Ruu])
                cp(ACT, ub[:], uu[:], [Ruu], [Rub])
                drain(6)
                for d in range(2):
                    order = PIECES if d == 0 else [PIECES[0], PIECES[2], PIECES[1]]
                    nsp = lv[:, d * 8 + c:d * 8 + c + 1]
                    nsp2 = lv[:, 16 + d * 8 + c:16 + d * 8 + c + 1]
                    gab_ = vecT[:, voff["gab"] + d * 8 + c:voff["gab"] + d * 8 + c + 1]
                    gxb_ = vecT[:, voff["gxb"] + d * 8 + c:voff["gxb"] + d * 8 + c + 1]
                    for pi_, (p0, p1) in enumerate(order):
                        np_ = p1 - p0
                        rS, iS = pr_[:, 0:np_], pr_[:, 1024:1024 + np_]
                        for s0 in range(p0, p1, 512):
                            s1 = min(p1, s0 + 512); n = s1 - s0; o0 = s0 - p0
                            mm(pu[:, 0, 0:n], gw[:, b, d * 2 + 0, :], ub[:, s0:s1], True, True, [Rgw[b], Rub], [Rpu[0]])
                            mm(pu[:, 1, 0:n], gw[:, b, d * 2 + 1, :], ub[:, s0:s1], True, True, [Rgw[b], Rub], [Rpu[1]])
                            act(rS[:, o0:o0 + n], pu[:, 0, 0:n], AF.Sigmoid, [Rpu[0], Rvec], [Rupre], scale=1.0, bias=gab_)
                            act(iS[:, o0:o0 + n], pu[:, 1, 0:n], AF.Sigmoid, [Rpu[1], Rvec], [Rupre], scale=1.0, bias=gxb_)
                        pb_ = pcnt[0] % 2
                        pcnt[0] += 1
                        A_, Q_ = pa[:, pb_, 0:np_], pq[:, pb_, 0:np_]
                        Rpa, Rpq, Rhsb = Rpa2[pb_], Rpq2[pb_], Rhsb2[pb_]
                        act(A_, rS, AF.Exp, [Rupre, Rlv], [Rpa], scale=nsp)
                        act(Q_, rS, AF.Exp, [Rupre, Rlv], [Rpq], scale=nsp2)
                        act(Q_, Q_, AF.Ln, [Rpq], [Rpq], scale=-1.0, bias=cst[:, 1:2])
                        act(Q_, Q_, AF.Exp, [Rpq], [Rpq], scale=0.5)
                        tt(DVE, iS, iS, uu[:, p0:p1], ALU.mult, [Rupre, Ruu], [Rupre])
                        tt(DVE, Q_, Q_, iS, ALU.mult, [Rpq, Rupre], [Rpq])
                        if d == 0:
                            if p0 == 0:
                                dst = hsb[:, pb_, 0:np_]; Rd = Rhsb
                            else:
                                dst = hsf[:, p0 - 256:p1 - 256]; Rd = Rhsf
                            init = 0.0 if pi_ == 0 else carry[:, 0:1]
                            op(DVE, lambda: nc.vector.tensor_tensor_scan(out=dst, data0=A_, data1=Q_, initial=init,
                                                                        op0=ALU.mult, op1=ALU.add), [Rpa, Rpq, Rcar], [Rd])
                            cp(DVE, carry[:, 0:1], dst[:, np_ - 1:np_], [Rd], [Rcar])
                        else:
                            dst = hsb[:, pb_, 0:np_]; Rd = Rhsb
                            init = 0.0 if pi_ == 0 else carry[:, 1:2]
                            op(DVE, lambda: nc.vector.tensor_tensor_scan(out=dst[:, ::-1], data0=A_[:, ::-1], data1=Q_[:, ::-1],
                                                                        initial=init, op0=ALU.mult, op1=ALU.add),
                               [Rpa, Rpq, Rcar], [Rd])
                            cp(DVE, carry[:, 1:2], dst[:, 0:1], [Rd], [Rcar])
                            if p0 >= 256:
                                l0_, l1_ = p0 - 256, p1 - 256
                                tt(DVE, dst, dst, hsf[:, l0_:l1_], ALU.add, [Rd, Rhsf], [Rd])
                                tt(POOL, yb[:, l0_:l1_], dst, gb[:, l0_:l1_], ALU.mult, [Rd, Rgb], [Ryb])
                        if d == 0:
                            drain(6)
                        elif pi_ == 0:
                            drain(len(rpend))
                            if c + 1 < KC:
                                load_chunk(c + 1)
                pws, Rpws = [pwol, pwol2], Rpw2

                def _rstep(tl, half, b=b):
                    tq = 2 + tl
                    c0 = half * 512
                    mm(pws[half][:, 0:512], yb[:, tl * 128:(tl + 1) * 128], woc[:, b, c0:c0 + 512], True, True,
                       [Ryb, Rwoc[b]], [Rpws[half]])
                    tt(DVE, x_tok[:, tq, c0:c0 + 512], pws[half][:, 0:512], x_tok[:, tq, c0:c0 + 512], ALU.add,
                       [Rpws[half], Rx[tq]], [Rx[tq]])
                for tl in range(16):
                    for half in range(2):
                        rpend.append(lambda tl=tl, half=half, f=_rstep: f(tl, half))
            drain(len(rpend))
            fw.barrier()

    def phase_final(tiles_src=None):
        with ExitStack() as _es:
            fnb = _es.enter_context(_sbt("fnb", [128, D], F32))
            osb = _es.enter_context(_sbt("osb", [128, 2, D], F32))
            junk = _es.enter_context(_sbt("junk", [128, D], BF16))
            Rfn, Ros = Reg(), regs(2)
            dma(SP, fnb[:], W["final_norm"].partition_broadcast(128), (), [Rfn])
            for t in range(2, NT):
                act(junk[:], x_tok[:, t, :], AF.Square, [Rx[t]], [Rjunk, Rst], accum_out=ss[:, t:t + 1])
            act(lnv[:, 2:NT], ss[:, 2:NT], AF.Ln, [Rst], [Rst], scale=1.0 / D, bias=cst[:, 0:1])
            act(rstd[:, 2:NT], lnv[:, 2:NT], AF.Exp, [Rst], [Rst], scale=-0.5)
            for t in range(2, NT):
                o = osb[:, t % 2, :]
                stt(o, x_tok[:, t, :], rstd[:, t:t + 1], fnb[:], ALU.mult, ALU.mult, [Rx[t], Rst, Rfn], [Ros[t % 2]])
                dma(SP, out_d[(t - 2) * 128:(t - 1) * 128, :], o, [Ros[t % 2]], ())
            fw.barrier()

    def dump_x():
        print("COUNTS", {e.name: e.count for e in fw.all}, fw.ninst)
        for t in range(2, NT):
            dma(SP, out_d[(t - 2) * 128:(t - 1) * 128, :], x_tok[:, t, :], [Rx[t]], ())
        fw.barrier()

    phase_hT(0, 0, G_ALL)
    if stop == 1:
        dump_x(); return nc
    phase_attn()
    if stop == 2:
        dump_x(); return nc
    phase_hT(0, 1, G_ALL)
    phase_ffn(0, True)
    if stop == 3:
        dump_x(); return nc
    phase_hT(1, 0, G_ALL)
    phase_lru()
    if stop == 4:
        dump_x(); return nc
    phase_hT(1, 1, G_LAT)
    phase_ffn(1, False)
    if stop == 5:
        dump_x(); return nc
    phase_final()
    return nc


def _rope_tables():
    pairs = 16
    inv = (10000.0 ** (-np.arange(pairs, dtype=np.float32) / pairs)).astype(np.float32)
    row = np.repeat(np.arange(32, dtype=np.float32), 64)
    col = np.tile(np.arange(64, dtype=np.float32), 32)
    ang = np.concatenate([row[:, None] * inv, col[:, None] * inv], axis=-1).astype(np.float32)
    return np.cos(ang).astype(np.float32), np.sin(ang).astype(np.float32)


def kernel(_stop=99, **inputs):
    nc = build(_stop)
    cos, sin = _rope_tables()
    shared = {n: np.ascontiguousarray(np.asarray(inputs[n], dtype=np.float32)) for n, _ in WEIGHT_SPECS}
    shared["c_ctx"] = np.ascontiguousarray(np.asarray(inputs["c_ctx"], dtype=np.float32))
    shared["rope_cos"] = cos
    shared["rope_sin"] = sin
    x = np.asarray(inputs["x"], dtype=np.float32)
    c = np.asarray(inputs["c"], dtype=np.float32)
    ctx = np.asarray(inputs["ctx"], dtype=np.float32)
    in_maps = []
    for b in range(8):
        m = dict(shared)
        m["x"] = np.ascontiguousarray(x[b]); m["ctx"] = np.ascontiguousarray(ctx[b]); m["c"] = np.ascontiguousarray(c[b])
        in_maps.append(m)
    res = run_bass_kernel_spmd(nc, in_maps, core_ids=list(range(8)))
    return np.stack([np.asarray(r["out"], dtype=np.float32) for r in res.results], axis=0)
```

```python
import math
import os
from contextlib import ExitStack
import numpy as np
import concourse.bass as bass
import concourse.mybir as mybir
from concourse.bass_utils import run_bass_kernel_spmd

F32 = mybir.dt.float32
BF16 = mybir.dt.bfloat16
AF = mybir.ActivationFunctionType
ALU = mybir.AluOpType
AX = mybir.AxisListType

D = 1024
KC = 8
NT = 18
T = 2304
DFF = 2816
NPAIR = 22
EPS = 1e-6


class Reg:
    __slots__ = ("w", "r")

    def __init__(self):
        self.w = None
        self.r = []


def regs(n):
    return [Reg() for _ in range(n)]


class Eng:
    def __init__(self, fw, e, name, is_dma=False):
        self.fw, self.e, self.name, self.is_dma = fw, e, name, is_dma
        self.count = 0
        self.waited = {}
        if not is_dma:
            self.sem = fw.nc.alloc_semaphore("s_" + name)
        else:
            self.nsem = 6
            self.sems = [fw.nc.alloc_semaphore("d_%s_%d" % (name, i)) for i in range(self.nsem)]

    def sem_val(self, idx):
        if not self.is_dma:
            return self.sem, idx
        i = idx - 1
        return self.sems[i % self.nsem], 16 * (i // self.nsem + 1)


class FW:
    def __init__(self, nc):
        self.nc = nc
        self.pe = Eng(self, nc.tensor, "pe")
        self.act = Eng(self, nc.scalar, "act")
        self.dve = Eng(self, nc.vector, "dve")
        self.pool = Eng(self, nc.gpsimd, "pool")
        self.sp = Eng(self, nc.sync, "sp", True)
        self.gq = Eng(self, nc.gpsimd, "gq", True)
        self.all = [self.pe, self.act, self.dve, self.pool, self.sp, self.gq]
        self.host = {"pe": self.pe, "act": self.act, "dve": self.dve, "pool": self.pool,
                     "sp": self.sp, "gq": self.pool}
        self.ninst = 0

    def _wait(self, eng, dep):
        de, di = dep
        host = self.host[eng.name]
        if de.is_dma:
            sem, val = de.sem_val(di)
            key = (de.name, (di - 1) % de.nsem)
        else:
            sem, val = de.sem, di
            key = de.name
        if host.waited.get(key, 0) >= val:
            return
        host.waited[key] = val
        eng.e.wait_ge(sem, val)

    def op(self, eng, fn, reads=(), writes=()):
        deps = []
        for r in reads:
            if r.w is not None:
                deps.append(r.w)
        for w in writes:
            if w.w is not None:
                deps.append(w.w)
            deps.extend(w.r)
        if eng.is_dma and eng.count >= eng.nsem:
            self._wait(eng, (eng, eng.count - eng.nsem + 1))
        seen = set()
        for d in deps:
            de, di = d
            k = (de.name, di)
            if k in seen:
                continue
            seen.add(k)
            if de is eng and not eng.is_dma:
                if eng is self.pe:
                    continue
                if not any((r.w is not None and r.w[0] is eng and r.w[1] == di) for r in reads):
                    continue
            self._wait(eng, d)
        inst = fn()
        eng.count += 1
        idx = eng.count
        sem, _ = eng.sem_val(idx)
        inst.then_inc(sem, 16 if eng.is_dma else 1)
        self.ninst += 1
        me = (eng, idx)
        for r in reads:
            r.r.append(me)
            if len(r.r) > 64:
                last = {}
                for (e2, i2) in r.r:
                    if e2.name not in last or last[e2.name][1] < i2:
                        last[e2.name] = (e2, i2)
                r.r = list(last.values())
        for w in writes:
            w.w = me
            w.r = []
        return inst

    def barrier(self):
        hosts = [self.pe, self.act, self.dve, self.pool, self.sp]
        for h in hosts:
            for e in self.all:
                if e.count == 0:
                    continue
                if e.is_dma:
                    for di in range(max(1, e.count - e.nsem + 1), e.count + 1):
                        self._wait(h, (e, di))
                elif e is not h:
                    self._wait(h, (e, e.count))


WEIGHT_SPECS = [
    ("l0_ada_w", (1024, 6144)), ("l0_ada_b", (6144,)), ("l0_norm_mix", (1024,)), ("l0_norm_ffn", (1024,)),
    ("l0_w_in", (1024, 2304)), ("l0_lam_q1", (64,)), ("l0_lam_k1", (64,)), ("l0_lam_q2", (64,)),
    ("l0_lam_k2", (64,)), ("l0_subln", (128,)), ("l0_q_norm", (64,)), ("l0_k_norm", (64,)),
    ("l0_w_o", (1024, 1024)), ("l0_ffn_up", (1024, 5632)), ("l0_ffn_conv_w", (3, 5632)),
    ("l0_ffn_conv_b", (5632,)), ("l0_ffn_down", (2816, 1024)),
    ("l1_ada_w", (1024, 6144)), ("l1_ada_b", (6144,)), ("l1_norm_mix", (1024,)), ("l1_norm_ffn", (1024,)),
    ("l1_w_in", (1024, 2048)), ("l1_conv_w", (4, 1024)), ("l1_conv_b", (1024,)),
    ("l1_gate_a_w", (2, 8, 128, 128)), ("l1_gate_a_b", (2, 1024)), ("l1_gate_x_w", (2, 8, 128, 128)),
    ("l1_gate_x_b", (2, 1024)), ("l1_a_param", (2, 1024)), ("l1_w_o", (1024, 1024)),
    ("l1_ffn_up", (1024, 5632)), ("l1_ffn_conv_w", (3, 5632)), ("l1_ffn_conv_b", (5632,)),
    ("l1_ffn_down", (2816, 1024)), ("final_norm", (1024,)),
]


def build(stop=99):
    nc = bass.Bass("TRN2", target_bir_lowering=False)
    fw = FW(nc)
    PE, ACT, DVE, POOL, SP, GQ = fw.pe, fw.act, fw.dve, fw.pool, fw.sp, fw.gq
    _uid = [0]

    def _sbt(name, shape, dt):
        _uid[0] += 1
        return nc.sbuf_tensor("%s_%d" % (name, _uid[0]), shape, dt)

    def _pst(name, shape, dt):
        _uid[0] += 1
        return nc.psum_tensor("%s_%d" % (name, _uid[0]), shape, dt)

    def din(name, shape):
        return nc.dram_tensor(name, list(shape), F32, kind="ExternalInput").ap()

    x_d = din("x", (2048, 1024)); ctx_d = din("ctx", (256, 1024)); c_d = din("c", (1024,)); cctx_d = din("c_ctx", (1024,))
    W = {n: din(n, s) for n, s in WEIGHT_SPECS}
    cos_d = din("rope_cos", (2048, 32)); sin_d = din("rope_sin", (2048, 32))
    out_d = nc.dram_tensor("out", [2048, 1024], F32, kind="ExternalOutput").ap()

    _rec = [None]

    def op(eng, f, r=(), w=()):
        if _rec[0] is not None:
            _rec[0].append((eng, f, list(r), list(w)))
            return None
        return fw.op(eng, f, reads=r, writes=w)

    def ve(eng):
        return nc.vector if eng is DVE else nc.gpsimd

    def mm(out, lhsT, rhs, st, sp_, r, w):
        op(PE, lambda: nc.tensor.matmul(out, lhsT, rhs, start=st, stop=sp_), r, w)

    def tr(out, in_, idt, r, w):
        op(PE, lambda: nc.tensor.transpose(out, in_, idt), r, w)

    def act(out, in_, func, r, w, **kw):
        op(ACT, lambda: nc.scalar.activation(out=out, in_=in_, func=func, **kw), r, w)

    def tt(eng, out, in0, in1, alu, r, w):
        op(eng, lambda: ve(eng).tensor_tensor(out=out, in0=in0, in1=in1, op=alu), r, w)

    def ts(eng, out, in0, s1, op0, r, w, s2=None, op1=None):
        if op1 is None:
            op(eng, lambda: ve(eng).tensor_scalar(out=out, in0=in0, scalar1=s1, scalar2=None, op0=op0), r, w)
        else:
            op(eng, lambda: ve(eng).tensor_scalar(out=out, in0=in0, scalar1=s1, scalar2=s2, op0=op0, op1=op1), r, w)

    def stt(out, in0, scalar, in1, op0, op1, r, w):
        op(DVE, lambda: nc.vector.scalar_tensor_tensor(out=out, in0=in0, scalar=scalar, in1=in1, op0=op0, op1=op1), r, w)

    def cp(eng, out, in_, r, w):
        if eng is ACT:
            op(ACT, lambda: nc.scalar.copy(out, in_), r, w)
        else:
            op(eng, lambda: ve(eng).tensor_copy(out=out, in_=in_), r, w)

    def recip(out, in_, r, w):
        op(DVE, lambda: nc.vector.reciprocal(out=out, in_=in_), r, w)

    def rsum(out, in_, r, w):
        op(DVE, lambda: nc.vector.reduce_sum(out=out, in_=in_, axis=AX.X), r, w)

    def memset(eng, ap, val, w):
        op(eng, lambda: ve(eng).memset(ap, val), (), w)

    def dma(q, out, in_, r, w):
        e = nc.sync if q is SP else nc.gpsimd
        op(q, lambda: e.dma_start(out=out, in_=in_), r, w)

    def bcast_mid(a, n):
        return bass.AP(a.tensor, a.offset, [a.ap[0], [0, n], *a.ap[1:]])

    def bcast_last(a, n):
        return bass.AP(a.tensor, a.offset, [*a.ap, [0, n]])

    sbt = nc.alloc_sbuf_tensor

    x_tok = sbt("x_tok", [128, NT, D], F32); Rx = regs(NT)
    hT = sbt("hT", [128, KC, T], BF16); RhT = regs(NT)
    identf = sbt("identf", [128, 128], F32); identb = sbt("identb", [128, 128], BF16); onesf = sbt("onesf", [128, 128], F32)
    Rc = Reg()
    NV = 640
    vecT = sbt("vecT", [128, NV], F32); Rvec = Reg()
    modT = sbt("modT", [128, 2, 48, 2], F32); Rmod = Reg()
    mgb = [sbt("mgb%d" % i, [128, D], F32) for i in range(2)]; Rmgb = regs(2)
    cs_tok = sbt("cs_tok", [128, 16, 32], F32); sn_tok = sbt("sn_tok", [128, 16, 32], F32); Rrope = Reg()
    Rjunk = Reg()
    ss = sbt("ss", [128, NT], F32); lnv = sbt("lnv", [128, NT], F32); rstd = sbt("rstd", [128, NT], F32); Rst = Reg()
    cst = sbt("cst", [128, 4], F32)
    scl = [sbt("scl%d" % i, [128, KC], F32) for i in range(2)]
    sft = [sbt("sft%d" % i, [128, KC], F32) for i in range(2)]
    Rsc = Reg()

    vec_list = [("c", c_d, 1024), ("c_ctx", cctx_d, 1024)]
    for l in (0, 1):
        vec_list += [("ada_b%d" % l, W["l%d_ada_b" % l], 6144), ("nmix%d" % l, W["l%d_norm_mix" % l], 1024),
                     ("nffn%d" % l, W["l%d_norm_ffn" % l], 1024),
                     ("fcw%d" % l, W["l%d_ffn_conv_w" % l].rearrange("a b -> (a b)"), 3 * 5632),
                     ("fcb%d" % l, W["l%d_ffn_conv_b" % l], 5632)]
    vec_list += [("l1cw", W["l1_conv_w"].rearrange("a b -> (a b)"), 4096), ("l1cb", W["l1_conv_b"], 1024),
                 ("gab", W["l1_gate_a_b"].rearrange("a b -> (a b)"), 2048),
                 ("gxb", W["l1_gate_x_b"].rearrange("a b -> (a b)"), 2048),
                 ("apar", W["l1_a_param"].rearrange("a b -> (a b)"), 2048)]
    voff = {}
    r0 = 0
    for name, ap_, n in vec_list:
        voff[name] = r0
        r0 += n // 128
    assert r0 <= NV

    with ExitStack() as _es:
        stg = _es.enter_context(_sbt("stg", [128, 5, 128], F32))
        adaw0 = _es.enter_context(_sbt("adaw0", [128, KC, 512], BF16))
        adaw1 = _es.enter_context(_sbt("adaw1", [128, KC, 512], BF16))
        scb = _es.enter_context(_sbt("scb", [128, KC, 2], BF16))
        sct = _es.enter_context(_sbt("sct", [128, 16], F32))
        pT0 = _es.enter_context(_pst("pT0", [128, 128], F32))
        modps = _es.enter_context(_pst("modps", [128, 96], F32))
        Rstg, Rscb, Rsct, RpT0, Rmp = Reg(), Reg(), Reg(), Reg(), Reg()
        adaw = [adaw0, adaw1]; Radaw = regs(2)
        for t in range(NT):
            src = ctx_d[t * 128:(t + 1) * 128, :] if t < 2 else x_d[(t - 2) * 128:(t - 1) * 128, :]
            dma(SP, x_tok[:, t, :], src, (), [Rx[t]])
        memset(POOL, identf[:], 0.0, [Rc])
        op(POOL, lambda: nc.gpsimd.affine_select(out=identf[:], in_=identf[:], compare_op=ALU.not_equal, fill=1.0,
                                                 base=0, pattern=[[-1, 128]], channel_multiplier=1), [Rc], [Rc])
        cp(POOL, identb[:], identf[:], [Rc], [Rc])
        memset(POOL, onesf[:], 1.0, [Rc])
        memset(POOL, cst[:, 0:1], EPS, [Rc])
        memset(POOL, cst[:, 1:2], 1.0, [Rc])
        memset(POOL, stg[:], 0.0, [Rstg])
        dma(SP, cs_tok[:], cos_d.rearrange("(i p) f -> p i f", p=128), (), [Rrope])
        dma(SP, sn_tok[:], sin_d.rearrange("(i p) f -> p i f", p=128), (), [Rrope])
        for name, ap_, n in vec_list:
            ra, rb = voff[name], voff[name] + n // 128
            r = ra
            while r < rb:
                s = r // 128
                e = min(rb, (s + 1) * 128)
                dma(SP, stg[r - s * 128:e - s * 128, s, :],
                    ap_[(r - ra) * 128:(e - ra) * 128].rearrange("(r p) -> r p", p=128), (), [Rstg])
                r = e
        for s in range(5):
            tr(pT0[:], stg[:, s, :], identf[:], [Rstg, Rc], [RpT0])
            cp(DVE, vecT[:, s * 128:(s + 1) * 128], pT0[:], [RpT0], [Rvec])
        act(sct[:], vecT[:, 0:16], AF.Exp, [Rvec], [Rsct], scale=-1.0)
        ts(DVE, sct[:], sct[:], 1.0, ALU.add, [Rsct], [Rsct])
        recip(sct[:], sct[:], [Rsct], [Rsct])
        tt(DVE, scb[:].rearrange("p k v -> p v k"), vecT[:, 0:16].rearrange("p (v k) -> p v k", v=2),
           sct[:].rearrange("p (v k) -> p v k", v=2), ALU.mult, [Rsct, Rvec], [Rscb])
        for l in (0, 1):
            wsrc = W["l%d_ada_w" % l].rearrange("(k p) n -> p k n", p=128)
            for blk in range(12):
                wb, Rw = adaw[blk % 2], Radaw[blk % 2]
                dma(GQ, wb[:], wsrc[:, :, blk * 512:(blk + 1) * 512], (), [Rw])
                for j in range(4):
                    col = (blk * 4 + j) * 2
                    for k in range(KC):
                        mm(modps[:, col:col + 2], wb[:, k, j * 128:(j + 1) * 128], scb[:, k, :], k == 0, k == KC - 1,
                           [Rw, Rscb], [Rmp])
            ab = vecT[:, voff["ada_b%d" % l]:voff["ada_b%d" % l] + 48]
            tt(DVE, modT[:, l, :, :], modps[:, 0:96].rearrange("p (j v) -> p j v", v=2), bcast_last(ab, 2), ALU.add,
               [Rmp, Rvec], [Rmod])
        fw.barrier()

    def prep_and_build(l, which, groups, pb, pst):
        gname = ("nmix%d" if which == 0 else "nffn%d") % l
        g = vecT[:, voff[gname]:voff[gname] + 8]
        vs = sorted(set(v for v, _ in groups))
        Rpb, Rdg = Reg(), regs(2)
        with ExitStack() as _es:
            dg0 = _es.enter_context(_sbt("dg0", [128, 128], F32))
            dg1 = _es.enter_context(_sbt("dg1", [128, 128], F32))
            xnb0 = _es.enter_context(_sbt("xnb0", [128, 4, D], BF16))
            xnb1 = _es.enter_context(_sbt("xnb1", [128, 4, D], BF16))
            junk = _es.enter_context(_sbt("junk", [128, D], BF16))
            dgs = [dg0, dg1]
            xnbs = [xnb0, xnb1]; Rxn = regs(2)
            for v in vs:
                s_scale = (3 * which + 1) * 8
                s_shift = (3 * which) * 8
                s_gate = (3 * which + 2) * 8
                stt(scl[v][:], modT[:, l, s_scale:s_scale + 8, v], 1.0, g, ALU.add, ALU.mult, [Rmod, Rvec], [Rsc])
                cp(DVE, sft[v][:], modT[:, l, s_shift:s_shift + 8, v], [Rmod], [Rsc])
                for j in range(8):
                    ts(DVE, dgs[j % 2][:], identf[:], modT[:, l, s_gate + j, v:v + 1], ALU.mult, [Rc, Rmod], [Rdg[j % 2]])
                    mm(pb[:, j * 128:(j + 1) * 128], onesf[:], dgs[j % 2][:], True, True, [Rc, Rdg[j % 2]], [Rpb])
                cp(DVE, mgb[v][:], pb[:, 0:D], [Rpb], [Rmgb[v]])
            tiles_all = [t for _, ts_ in groups for t in ts_]
            for t in tiles_all:
                act(junk[:], x_tok[:, t, :], AF.Square, [Rx[t]], [Rjunk, Rst], accum_out=ss[:, t:t + 1])
            t0, t1 = min(tiles_all), max(tiles_all) + 1
            act(lnv[:, t0:t1], ss[:, t0:t1], AF.Ln, [Rst], [Rst], scale=1.0 / D, bias=cst[:, 0:1])
            act(rstd[:, t0:t1], lnv[:, t0:t1], AF.Exp, [Rst], [Rst], scale=-0.5)
            Rps = regs(2)
            for gi, (v, tiles) in enumerate(groups):
                xn, Rn = xnbs[gi % 2], Rxn[gi % 2]
                for i, t in enumerate(tiles):
                    if i % 2:
                        act(xn[:, i, :], x_tok[:, t, :], AF.Identity, [Rx[t], Rst], [Rn], scale=rstd[:, t:t + 1])
                    else:
                        ts(DVE, xn[:, i, :], x_tok[:, t, :], rstd[:, t:t + 1], ALU.mult, [Rx[t], Rst], [Rn])
                n = len(tiles) * 128
                tok0 = tiles[0] * 128
                for half in range(2):
                    ps_, Rp = pst[half], Rps[half]
                    for ci in range(4):
                        c = half * 4 + ci
                        for i, t in enumerate(tiles):
                            tr(ps_[:, ci, i * 128:(i + 1) * 128], xn[:, i, c * 128:(c + 1) * 128], identb[:], [Rn, Rc], [Rp])
                        act(hT[:, c, tok0:tok0 + n], ps_[:, ci, 0:n], AF.Identity, [Rp, Rsc], [RhT[t] for t in tiles],
                            scale=scl[v][:, c:c + 1], bias=sft[v][:, c:c + 1])
            fw.barrier()

    def phase_hT(l, which, groups):
        with ExitStack() as _es:
            pb = _es.enter_context(_pst("pb", [128, D], F32))
            pst0 = _es.enter_context(_pst("pst0", [128, 4, 512], BF16))
            pst1 = _es.enter_context(_pst("pst1", [128, 4, 512], BF16))
            prep_and_build(l, which, groups, pb, [pst0, pst1])

    G_ALL = [(1, [0, 1])] + [(0, list(range(2 + 4 * i, 6 + 4 * i))) for i in range(4)]
    G_LAT = [(0, list(range(2 + 4 * i, 6 + 4 * i))) for i in range(4)]

    def resid_add(tq, v, lhsT, wmat, pwo, Rpwo, tmpb, Rtmp, rdeps):
        for ci, (c0, cw) in enumerate(((0, 384), (384, 384), (768, 256))):
            mm(pwo[:, 0:cw], lhsT, wmat[:, c0:c0 + cw], True, True, rdeps, [Rpwo])
            tb, Rt = tmpb[ci % 2], Rtmp[ci % 2]
            tt(DVE, tb[:, 0:cw], pwo[:, 0:cw], mgb[v][:, c0:c0 + cw], ALU.mult, [Rpwo, Rmgb[v]], [Rt])
            tt(POOL, x_tok[:, tq, c0:c0 + cw], x_tok[:, tq, c0:c0 + cw], tb[:, 0:cw], ALU.add, [Rt, Rx[tq]], [Rx[tq]])

    def phase_attn():
        w_in = W["l0_w_in"].rearrange("(k p) n -> p k n", p=128)
        w_o = W["l0_w_o"]
        NB = 2
        with ExitStack() as _es:
            wslab = _es.enter_context(_sbt("wslab", [128, NB, KC, 384], BF16))
            qkT = _es.enter_context(_sbt("qkT", [128, NB, 2, T], BF16))
            Va = _es.enter_context(_sbt("Va", [128, NB, NT, 129], BF16))
            kTb = _es.enter_context(_sbt("kTb", [128, T], BF16))
            Vb = _es.enter_context(_sbt("Vb", [128, NT, 2, 65], BF16))
            wou = _es.enter_context(_sbt("wou", [128, NB, D], BF16))
            stgq = _es.enter_context(_sbt("stgq", [128, 2, 256], BF16))
            pTb = _es.enter_context(_sbt("pTb", [128, 3, 1024], BF16))
            rt = _es.enter_context(_sbt("rt", [128, 2, 4, 128], F32))
            xs = _es.enter_context(_sbt("xs", [128, 2, 3, 128], F32))
            sm = _es.enter_context(_sbt("sm", [128, 4, 16], F32))
            fo = _es.enter_context(_sbt("fo", [128, 4, 3, 128], F32))
            ob = _es.enter_context(_sbt("ob", [128, 4, 128], BF16))
            oTs = _es.enter_context(_sbt("oTs", [128, 4, 128], BF16))
            accS = _es.enter_context(_sbt("accS", [128, 1, 4, 132], F32))
            tmpb_ = _es.enter_context(_sbt("tmpb", [128, 2, 512], F32))
            lamb = _es.enter_context(_sbt("lamb", [128, 4, 64], F32))
            lams = _es.enter_context(_sbt("lams", [128, 8], F32))
            nrmb = _es.enter_context(_sbt("nrmb", [128, 256], F32))
            pss0 = _es.enter_context(_pst("pss0", [128, 1024], F32))
            pss1 = _es.enter_context(_pst("pss1", [128, 1024], F32))
            accP = _es.enter_context(_pst("accP", [128, 4, 256], F32))
            bk6 = _es.enter_context(_pst("bk6", [128, 512], F32))
            bk7 = _es.enter_context(_pst("bk7", [128, 1024], BF16))
            pp = bk6[:, 0:384]
            pwo = bk6[:, 0:384]
            ptq = bk7[:, 0:256]
            pto = bk7[:, 256:512]
            Rws, Rqk, RVa, Rwou = regs(NB), regs(NB), regs(NB), regs(NB)
            RkTb, RVb, Rlam, Rnrm = Reg(), Reg(), Reg(), Reg()
            Rstq, RpT, Rrt, Rxs, Rsm, Rfo, Rob, RoT, Rtmp = regs(2), regs(3), regs(2), regs(2), regs(4), regs(4), regs(4), regs(4), regs(2)
            RaccS = regs(2)
            Rpss, Racc, Rpp, Rptq, Rpto = regs(2), regs(4), Reg(), Reg(), Reg()
            Rpwo = Rpp
            pss = [pss0, pss1]
            tmpb = [tmpb_[:, 0, :], tmpb_[:, 1, :]]
            for i, nm in enumerate(("l0_lam_q1", "l0_lam_k1", "l0_lam_q2", "l0_lam_k2")):
                dma(SP, lamb[:, i, :], W[nm].partition_broadcast(128), (), [Rlam])
            dma(SP, nrmb[:, 0:128], W["l0_subln"].partition_broadcast(128), (), [Rnrm])
            dma(SP, nrmb[:, 128:192], W["l0_q_norm"].partition_broadcast(128), (), [Rnrm])
            dma(SP, nrmb[:, 192:256], W["l0_k_norm"].partition_broadcast(128), (), [Rnrm])
            lam_init = 0.8 - 0.6 * math.exp(-0.3 * 0)
            ts(DVE, nrmb[:, 0:128], nrmb[:, 0:128], 1.0 - lam_init, ALU.mult, [Rnrm], [Rnrm])
            tt(DVE, lamb[:, 0, :], lamb[:, 0, :], lamb[:, 1, :], ALU.mult, [Rlam], [Rlam])
            tt(DVE, lamb[:, 2, :], lamb[:, 2, :], lamb[:, 3, :], ALU.mult, [Rlam], [Rlam])
            rsum(lams[:, 0:1], lamb[:, 0, :], [Rlam], [Rlam])
            rsum(lams[:, 1:2], lamb[:, 2, :], [Rlam], [Rlam])
            act(lams[:, 2:4], lams[:, 0:2], AF.Exp, [Rlam], [Rlam])
            tt(DVE, lams[:, 4:5], lams[:, 3:4], lams[:, 2:3], ALU.subtract, [Rlam], [Rlam])
            ts(DVE, lams[:, 5:6], lams[:, 4:5], -lam_init, ALU.add, [Rlam], [Rlam])
            for b in range(NB):
                memset(POOL, Va[:, b, :, 128:129], 1.0, [RVa[b]])
            memset(POOL, Vb[:, :, :, 64:65], 1.0, [RVb])
            nlam = lams[:, 5:6]
            subln_b = nrmb[:, 0:128]
            qn_b = nrmb[:, 128:192]
            kn_b = nrmb[:, 192:256]

            def rope(src, H, t, dst, si, rsrc):
                i = t - 2
                cs = bcast_mid(cs_tok[:, i, :], H); sn = bcast_mid(sn_tok[:, i, :], H)
                sv = src.rearrange("p (h i two) -> p h i two", h=H, two=2)
                dv = dst.rearrange("p (h i two) -> p h i two", h=H, two=2)
                x0, x1 = sv[:, :, :, 0], sv[:, :, :, 1]
                tmp = [rt[:, si, j, 0:H * 32].rearrange("p (h i) -> p h i", h=H) for j in range(4)]
                tt(DVE, tmp[0], x0, cs, ALU.mult, rsrc + [Rrope], [Rrt[si]])
                tt(DVE, tmp[1], x1, sn, ALU.mult, rsrc + [Rrope], [Rrt[si]])
                tt(DVE, tmp[2], x0, sn, ALU.mult, rsrc + [Rrope], [Rrt[si]])
                tt(DVE, tmp[3], x1, cs, ALU.mult, rsrc + [Rrope], [Rrt[si]])
                tt(POOL, dv[:, :, :, 0], tmp[0], tmp[1], ALU.subtract, [Rrt[si]], [Rstq[si]])
                tt(POOL, dv[:, :, :, 1], tmp[2], tmp[3], ALU.add, [Rrt[si]], [Rstq[si]])

            def qknorm(src, gain_b, si, Rp_):
                a, b2, c2 = xs[:, si, 0, :], xs[:, si, 1, :], xs[:, si, 2, :]
                cp(DVE, a, src, Rp_, [Rxs[si]])
                tt(POOL, b2, a, a, ALU.mult, [Rxs[si]], [Rxs[si]])
                rsum(sm[:, si, 0:2], b2.rearrange("p (h d) -> p h d", h=2), [Rxs[si]], [Rsm[si]])
                act(sm[:, si, 2:4], sm[:, si, 0:2], AF.Ln, [Rsm[si]], [Rsm[si]], scale=1.0 / 64, bias=cst[:, 0:1])
                act(sm[:, si, 4:6], sm[:, si, 2:4], AF.Exp, [Rsm[si]], [Rsm[si]], scale=-0.5)
                tt(DVE, b2.rearrange("p (h d) -> p h d", h=2), a.rearrange("p (h d) -> p h d", h=2),
                   bcast_last(sm[:, si, 4:6], 64), ALU.mult, [Rxs[si], Rsm[si]], [Rxs[si]])
                tt(POOL, c2.rearrange("p (h d) -> p h d", h=2), b2.rearrange("p (h d) -> p h d", h=2),
                   bcast_mid(gain_b, 2), ALU.mult, [Rxs[si], Rnrm], [Rxs[si]])
                return c2

            ppbuf = [accP[:, 0:2, :].rearrange("p a c -> p (a c)"), accP[:, 2:4, :].rearrange("p a c -> p (a c)")]
            Rppbuf = [[Racc[0], Racc[1]], [Racc[2], Racc[3]]]
            ptqs = [bk7[:, 0:256], bk7[:, 256:512]]
            ptos = [bk7[:, 512:640], bk7[:, 640:768]]
            Rbk7 = Reg()
            Rptqs, Rptos = [Rbk7, Rbk7], [Rbk7, Rbk7]
            Rbk6 = Reg()

            def load_slab(kind, idx, b, sb_):
                if kind == "diff":
                    cols = [(idx * 128, 128), (512 + idx * 128, 128), (1024 + idx * 128, 128)]
                elif kind == "kv":
                    cols = [(2048, 256)]
                else:
                    cols = [(1536 + 64 * idx, 64), (1536 + 64 * (idx + 4), 64)]
                o = 0
                for (c0, cw) in cols:
                    dma(GQ, wslab[:, sb_, :, o:o + cw], w_in[:, :, c0:c0 + cw], (), [Rws[sb_]])
                    o += cw
                if kind == "diff":
                    dma(GQ, wou[:, b, :], w_o[idx * 128:(idx + 1) * 128, :], (), [Rwou[b]])
                elif kind == "gq":
                    dma(GQ, wou[0:64, b, :], w_o[512 + 64 * idx:512 + 64 * idx + 64, :], (), [Rwou[b]])
                    dma(GQ, wou[64:128, b, :], w_o[512 + 64 * (idx + 4):512 + 64 * (idx + 4) + 64, :], (), [Rwou[b]])

            def project(kind, idx, b, sb_):
                if kind == "diff":
                    cols = [(idx * 128, 128), (512 + idx * 128, 128), (1024 + idx * 128, 128)]
                elif kind == "kv":
                    cols = [(2048, 256)]
                else:
                    cols = [(1536 + 64 * idx, 64), (1536 + 64 * (idx + 4), 64)]
                ncol = sum(cw for _, cw in cols)
                pending = []
                for t0_ in range(0, NT, 2):
                    pair = [t0_, t0_ + 1]
                    for t in pair:
                        si = t % 2
                        for k in range(KC):
                            mm(ppbuf[si][:, 0:ncol], hT[:, k, t * 128:(t + 1) * 128], wslab[:, sb_, k, 0:ncol], k == 0, k == KC - 1,
                               [RhT[t], Rws[sb_]], Rppbuf[si])
                    for pfn in pending:
                        pfn()
                    pending = []
                    recs = []
                    for t in pair:
                        si = t % 2
                        pp = ppbuf[si]
                        Rp = Rppbuf[si]
                        lat = t >= 2
                        ptq, Rptq = ptqs[si], Rptqs[si]
                        _rec[0] = []
                        if kind == "diff":
                            if lat:
                                rope(pp[:, 0:256], 4, t, stgq[:, si, 0:256], si, Rp)
                            else:
                                cp(DVE, stgq[:, si, 0:256], pp[:, 0:256], Rp, [Rstq[si]])
                            cp(DVE, Va[:, b, t, 0:128], pp[:, 256:384], Rp, [RVa[b]])

                            def pfn(t=t, si=si, ptq=ptq, Rptq=Rptq):
                                tr(ptq[:, 0:128], stgq[:, si, 0:128], identb[:], [Rstq[si], Rc], [Rptq])
                                tr(ptq[:, 128:256], stgq[:, si, 128:256], identb[:], [Rstq[si], Rc], [Rptq])
                                cp(DVE, qkT[:, b, :, t * 128:(t + 1) * 128], ptq[:, 0:256].rearrange("p (a q) -> p a q", a=2),
                                   [Rptq], [Rqk[b]])
                        elif kind == "kv":
                            xn_ = qknorm(pp[:, 0:128], kn_b, si, Rp)
                            cp(DVE, Vb[:, t, :, 0:64], pp[:, 128:256].rearrange("p (g d) -> p g d", g=2), Rp, [RVb])
                            if lat:
                                rope(xn_, 2, t, stgq[:, si, 0:128], si, [Rxs[si]])
                            else:
                                cp(POOL, stgq[:, si, 0:128], xn_, [Rxs[si]], [Rstq[si]])

                            def pfn(t=t, si=si, ptq=ptq, Rptq=Rptq):
                                tr(ptq[:, 0:128], stgq[:, si, 0:128], identb[:], [Rstq[si], Rc], [Rptq])
                                cp(DVE, kTb[:, t * 128:(t + 1) * 128], ptq[:, 0:128], [Rptq], [RkTb])
                        else:
                            xn_ = qknorm(pp[:, 0:128], qn_b, si, Rp)
                            if lat:
                                rope(xn_, 2, t, stgq[:, si, 0:128], si, [Rxs[si]])
                            else:
                                cp(POOL, stgq[:, si, 0:128], xn_, [Rxs[si]], [Rstq[si]])

                            def pfn(t=t, si=si, ptq=ptq, Rptq=Rptq):
                                tr(ptq[:, 0:128], stgq[:, si, 0:128], identb[:], [Rstq[si], Rc], [Rptq])
                                cp(DVE, qkT[:, b, 0, t * 128:(t + 1) * 128], ptq[:, 0:128], [Rptq], [Rqk[b]])
                        recs.append(_rec[0])
                        _rec[0] = None
                        pending.append(pfn)
                    for k_ in range(max(len(r_) for r_ in recs)):
                        for r_ in recs:
                            if k_ < len(r_):
                                e_, f_, rr_, ww_ = r_[k_]
                                fw.op(e_, f_, reads=rr_, writes=ww_)
                for pfn in pending:
                    pfn()

            def attend(kind, idx, b):
                if kind == "diff":
                    dv1 = 129
                    kT_ap = lambda s, kt: qkT[s * 64:(s + 1) * 64, b, 1, kt * 128:(kt + 1) * 128]
                    v_ap = lambda s, kt: Va[:, b, kt, 0:129]
                    Rk, Rv = Rqk[b], RVa[b]
                else:
                    dv1 = 65
                    kT_ap = lambda s, kt: kTb[s * 64:(s + 1) * 64, kt * 128:(kt + 1) * 128]
                    v_ap = lambda s, kt: Vb[:, kt, s, 0:65]
                    Rk, Rv = RkTb, RVb
                blocks = [(0, [0, 1])] + [(256 + 256 * i, list(range(NT))) for i in range(8)]
                G = []
                for bi_, (q0, ktiles) in enumerate(blocks):
                    for g0 in range(0, len(ktiles), 2):
                        G.append((bi_, q0, ktiles[g0:g0 + 2], g0 == 0, g0 + 2 >= len(ktiles)))
                n_g = len(G)

                def QK(i):
                    bi_, q0, grp, first, last = G[i]
                    pb_, Rp = pss[i % 2], Rpss[i % 2]
                    for gi, kt in enumerate(grp):
                        for s in range(2):
                            o_ = s * 512 + gi * 256
                            mm(pb_[:, o_:o_ + 256], kT_ap(s, kt), qkT[s * 64:(s + 1) * 64, b, 0, q0:q0 + 256], True, True,
                               [Rk, Rqk[b]], [Rp])

                def EXP(i):
                    grp = G[i][2]
                    n = len(grp) * 256
                    src = pss[i % 2][:, :].rearrange("p (s c) -> p s c", s=2)[:, :, 0:n]
                    dst = pTb[:, i % 3, :].rearrange("p (s c) -> p s c", s=2)[:, :, 0:n]
                    act(dst, src, AF.Exp, [Rpss[i % 2]], [RpT[i % 3]], scale=0.125)

                def PV(i):
                    bi_, q0, grp, first, last = G[i]
                    for gi, kt in enumerate(grp):
                        for s in range(2):
                            for qi in range(2):
                                ai = s * 2 + qi
                                o_ = s * 512 + gi * 256 + qi * 128
                                mm(accP[:, ai, 0:dv1], pTb[:, i % 3, o_:o_ + 128], v_ap(s, kt),
                                   first and gi == 0 and qi == 0, last and gi == len(grp) - 1, [RpT[i % 3], Rv], [Racc[ai]])

                def FIN_A(bi_, q0):
                    fb = bi_ % 2
                    fa = 0
                    cp(DVE, accS[:, fb * 0, 0:2, 0:dv1], accP[:, 0:2, 0:dv1], [Racc[0], Racc[1]], [RaccS[0]])
                    cp(DVE, accS[:, 0, 2:4, 0:dv1], accP[:, 2:4, 0:dv1], [Racc[2], Racc[3]], [RaccS[0]])
                    for qi in range(2):
                        sl = fb * 2 + qi
                        if kind == "diff":
                            a0, a1 = accS[:, 0, 0 + qi, :], accS[:, 0, 2 + qi, :]
                            recip(sm[:, sl, 8:9], a0[:, 128:129], [RaccS[0]], [Rsm[sl]])
                            recip(sm[:, sl, 9:10], a1[:, 128:129], [RaccS[0]], [Rsm[sl]])
                            tt(DVE, sm[:, sl, 10:11], sm[:, sl, 9:10], nlam, ALU.mult, [Rsm[sl], Rlam], [Rsm[sl]])
                            ts(DVE, fo[:, sl, 0, :], a0[:, 0:128], sm[:, sl, 8:9], ALU.mult, [RaccS[0], Rsm[sl]], [Rfo[sl]])
                            stt(fo[:, sl, 1, :], a1[:, 0:128], sm[:, sl, 10:11], fo[:, sl, 0, :], ALU.mult, ALU.add,
                                [RaccS[0], Rsm[sl], Rfo[sl]], [Rfo[sl]])
                            tt(POOL, fo[:, sl, 2, :], fo[:, sl, 1, :], fo[:, sl, 1, :], ALU.mult, [Rfo[sl]], [Rfo[sl]])
                            rsum(sm[:, sl, 11:12], fo[:, sl, 2, :], [Rfo[sl]], [Rsm[sl]])
                        else:
                            for s in range(2):
                                a_ = accS[:, 0, s * 2 + qi, :]
                                recip(sm[:, sl, 8 + s:9 + s], a_[:, 64:65], [RaccS[0]], [Rsm[sl]])
                                ts(DVE, ob[:, sl, s * 64:(s + 1) * 64], a_[:, 0:64], sm[:, sl, 8 + s:9 + s], ALU.mult,
                                   [RaccS[0], Rsm[sl]], [Rob[sl]])

                def FIN_A2(bi_, q0):
                    if kind != "diff":
                        return
                    fb = bi_ % 2
                    for qi in range(2):
                        sl = fb * 2 + qi
                        act(sm[:, sl, 12:13], sm[:, sl, 11:12], AF.Ln, [Rsm[sl]], [Rsm[sl]], scale=1.0 / 128, bias=cst[:, 0:1])
                        act(sm[:, sl, 13:14], sm[:, sl, 12:13], AF.Exp, [Rsm[sl]], [Rsm[sl]], scale=-0.5)
                        stt(ob[:, sl, :], fo[:, sl, 1, :], sm[:, sl, 13:14], subln_b, ALU.mult, ALU.mult,
                            [Rfo[sl], Rsm[sl], Rnrm], [Rob[sl]])

                def FIN_B0(bi_, q0):
                    fb = bi_ % 2
                    for qi in range(2):
                        sl = fb * 2 + qi
                        pto, Rpto = ptos[qi], Rptos[qi]
                        tr(pto[:, 0:128], ob[:, sl, :], identb[:], [Rob[sl], Rc], [Rpto])
                    for qi in range(2):
                        sl = fb * 2 + qi
                        pto, Rpto = ptos[qi], Rptos[qi]
                        cp(DVE, oTs[:, sl, :], pto[:, 0:128], [Rpto], [RoT[sl]])

                def FIN_Bk(bi_, q0, kk):
                    fb = bi_ % 2
                    qi, j = kk // 2, kk % 2
                    sl = fb * 2 + qi
                    tq = q0 // 128 + qi
                    v = 1 if tq < 2 else 0
                    h = j % 2
                    c0 = j * 512
                    mm(bk6[:, 0:512], oTs[:, sl, :], wou[:, b, c0:c0 + 512], True, True, [RoT[sl], Rwou[b]], [Rbk6])
                    tt(DVE, tmpb[h][:, 0:512], bk6[:, 0:512], mgb[v][:, c0:c0 + 512], ALU.mult, [Rbk6, Rmgb[v]], [Rtmp[h]])
                    tt(POOL, x_tok[:, tq, c0:c0 + 512], x_tok[:, tq, c0:c0 + 512], tmpb[h][:, 0:512], ALU.add,
                       [Rtmp[h], Rx[tq]], [Rx[tq]])

                def run_stage(st_, b2_, q2_):
                    if st_ == 0:
                        FIN_A2(b2_, q2_)
                    elif st_ == 1:
                        FIN_B0(b2_, q2_)
                    else:
                        FIN_Bk(b2_, q2_, st_ - 2)

                sched = []
                QK(0)
                if n_g > 1:
                    QK(1)
                for i in range(n_g):
                    EXP(i)
                    if i + 2 < n_g:
                        QK(i + 2)
                    PV(i)
                    bi_, q0, grp, first, last = G[i]
                    if last:
                        FIN_A(bi_, q0)
                        sched.append((i + 4, 0, bi_, q0))
                        sched.append((i + 6, 1, bi_, q0))
                        for kk in range(4):
                            sched.append((i + 7 + kk, 2 + kk, bi_, q0))
                        sched.sort()
                    while sched and sched[0][0] <= i:
                        _, st_, b2_, q2_ = sched.pop(0)
                        run_stage(st_, b2_, q2_)
                for (_, st_, b2_, q2_) in sched:
                    run_stage(st_, b2_, q2_)

            units = [("diff", h) for h in range(4)] + [("kv", 0)] + [("gq", j) for j in range(4)]
            plan = []
            bi = 0
            for pos, (kind, idx) in enumerate(units):
                if kind == "kv":
                    plan.append((kind, idx, bi % NB, pos % 2))
                else:
                    plan.append((kind, idx, bi % NB, pos % 2))
                    bi += 1
            load_slab(*plan[0])
            for pos, (kind, idx, b, sb_) in enumerate(plan):
                project(kind, idx, b, sb_)
                if pos + 1 < len(plan):
                    load_slab(*plan[pos + 1])
                if kind != "kv":
                    attend(kind, idx, b)
            fw.barrier()

    def phase_ffn(l, with_ctx):
        w_up = W["l%d_ffn_up" % l].rearrange("(k p) n -> p k n", p=128)
        w_dn = W["l%d_ffn_down" % l]
        cw0 = voff["fcw%d" % l]; cb0 = voff["fcb%d" % l]
        groups = [list(range(0, 4)), list(range(4, 8)), list(range(8, 12)), list(range(12, 16)), list(range(16, 19)), list(range(19, 22))]
        blocks = []
        if with_ctx:
            blocks.append((0, 256, 0, 256, True, True))
        s_ = 256
        for nt_ in (3, 3, 3, 3, 3, 1):
            blocks.append((s_, s_ + nt_ * 128, 256, T))
            s_ += nt_ * 128
        blocks = [(b_[0], b_[1], b_[2], b_[3]) for b_ in blocks]
        with ExitStack() as _es:
            wup = _es.enter_context(_sbt("wup", [128, 2, KC, 4, 256], BF16))
            wdn = _es.enter_context(_sbt("wdn", [128, 2, 4, D], BF16))
            actT = _es.enter_context(_sbt("actT", [128, 2, 4, 384], BF16))
            cvg = _es.enter_context(_sbt("cvg", [128, 3, 2, 384], F32))
            sgt = _es.enter_context(_sbt("sgt", [128, 3, 384], F32))
            tmpf = _es.enter_context(_sbt("tmpf", [128, 2, 512], F32))
            pu0 = _es.enter_context(_pst("pu0", [128, 2, 512], F32))
            pu1 = _es.enter_context(_pst("pu1", [128, 2, 512], F32))
            pu2 = _es.enter_context(_pst("pu2", [128, 2, 512], F32))
            pwof = _es.enter_context(_pst("pwof", [128, 512], F32))
            pwof2 = _es.enter_context(_pst("pwof2", [128, 512], F32))
            Rwup, Rwdn, RaT, Rcv2, Rsg, Rtmp, Rpu2 = regs(2), regs(2), regs(2), [regs(2), regs(2), regs(2)], regs(3), regs(2), [regs(2), regs(2), regs(2)]
            Rpwo = Reg()
            pus = [pu0, pu1, pu2]
            tmpb = [tmpf[:, 0, :], tmpf[:, 1, :]]
            it = 0
            blk_ctr = [0]
            pend_list = []
            pwoH = [pwof[:, 0:512], pwof2[:, 0:512]]
            RpwoH = regs(2)
            def load_group(gi_):
                wb_ = gi_ % 2
                for pi, j in enumerate(groups[gi_]):
                    dma(GQ, wup[:, wb_, :, pi, 0:128], w_up[:, :, j * 128:(j + 1) * 128], (), [Rwup[wb_]])
                    dma(GQ, wup[:, wb_, :, pi, 128:256], w_up[:, :, DFF + j * 128:DFF + (j + 1) * 128], (), [Rwup[wb_]])
                    dma(GQ, wdn[:, wb_, pi, :], w_dn[j * 128:(j + 1) * 128, :], (), [Rwdn[wb_]])

            load_group(0)
            for gi, pairs in enumerate(groups):
                wb = gi % 2
                for bi_, (st, en, seg0, seg1) in enumerate(blocks):
                    in0 = max(seg0, st - 1); in1 = min(seg1, en + 1)
                    n_in = in1 - in0; n = en - st
                    L = st - in0
                    Rr = in1 - en
                    ab = blk_ctr[0] % 2
                    blk_ctr[0] += 1
                    tiles = list(range(st // 128, en // 128))
                    for pi, j in enumerate(pairs):
                        pu, Rp2 = pus[it % 3], Rpu2[it % 3]
                        cb_ = it % 3
                        it += 1
                        for half in range(2):
                            for k in range(KC):
                                mm(pu[:, half, 0:n_in], wup[:, wb, k, pi, half * 128:(half + 1) * 128], hT[:, k, in0:in1],
                                   k == 0, k == KC - 1, [Rwup[wb]] + [RhT[t] for t in range(in0 // 128, (in1 - 1) // 128 + 1)], [Rp2[half]])
                        for half in range(2):
                            fch = j + half * NPAIR
                            w0 = vecT[:, cw0 + fch:cw0 + fch + 1]
                            w1 = vecT[:, cw0 + 44 + fch:cw0 + 44 + fch + 1]
                            w2 = vecT[:, cw0 + 88 + fch:cw0 + 88 + fch + 1]
                            bb = vecT[:, cb0 + fch:cb0 + fch + 1]
                            c_ = cvg[:, cb_, half, :]
                            Rp = Rp2[half]
                            Rcvh = Rcv2[cb_][half]
                            act(c_[:, 0:n], pu[:, half, L:L + n], AF.Identity, [Rp, Rvec], [Rcvh], scale=w1, bias=bb)
                            if L == 1:
                                stt(c_[:, 0:n], pu[:, half, 0:n], w0, c_[:, 0:n], ALU.mult, ALU.add, [Rp, Rvec, Rcvh], [Rcvh])
                            else:
                                stt(c_[:, 1:n], pu[:, half, 0:n - 1], w0, c_[:, 1:n], ALU.mult, ALU.add, [Rp, Rvec, Rcvh], [Rcvh])
                            if Rr == 1:
                                stt(c_[:, 0:n], pu[:, half, L + 1:L + 1 + n], w2, c_[:, 0:n], ALU.mult, ALU.add, [Rp, Rvec, Rcvh], [Rcvh])
                            else:
                                stt(c_[:, 0:n - 1], pu[:, half, L + 1:L + n], w2, c_[:, 0:n - 1], ALU.mult, ALU.add, [Rp, Rvec, Rcvh], [Rcvh])
                        act(sgt[:, cb_, 0:n], cvg[:, cb_, 1, 0:n], AF.Silu, [Rcv2[cb_][1]], [Rsg[cb_]])
                        tt(DVE, actT[:, ab, pi, 0:n], sgt[:, cb_, 0:n], cvg[:, cb_, 0, 0:n], ALU.mult, [Rsg[cb_], Rcv2[cb_][0]], [RaT[ab]])
                        if pend_list:
                            pend_list.pop(0)()
                    def DOWN(ti, tq, ab=ab, wb=wb, npair=len(pairs)):
                        if True:
                            v = 1 if tq < 2 else 0
                            for jj in range(2):
                                h = jj % 2
                                c0 = jj * 512
                                for pi in range(npair):
                                    mm(pwoH[h], actT[:, ab, pi, ti * 128:(ti + 1) * 128], wdn[:, wb, pi, c0:c0 + 512],
                                       pi == 0, pi == npair - 1, [RaT[ab], Rwdn[wb]], [RpwoH[h]])
                                tt(DVE, tmpb[h][:, 0:512], pwoH[h], mgb[v][:, c0:c0 + 512], ALU.mult, [RpwoH[h], Rmgb[v]], [Rtmp[h]])
                                tt(POOL, x_tok[:, tq, c0:c0 + 512], x_tok[:, tq, c0:c0 + 512], tmpb[h][:, 0:512], ALU.add,
                                   [Rtmp[h], Rx[tq]], [Rx[tq]])
                    while pend_list:
                        pend_list.pop(0)()
                    for ti, tq in enumerate(tiles):
                        pend_list.append(lambda ti=ti, tq=tq, DOWN=DOWN: DOWN(ti, tq))
                    if bi_ == 0 and gi + 1 < len(groups):
                        load_group(gi + 1)
            while pend_list:
                pend_list.pop(0)()
            fw.barrier()

    def phase_lru():
        w_in = W["l1_w_in"].rearrange("(k p) n -> p k n", p=128)
        w_o = W["l1_w_o"]
        PIECES = [(0, 256), (256, 1280), (1280, T)]
        with ExitStack() as _es:
            wl = _es.enter_context(_sbt("wl", [128, 2, KC, 256], BF16))
            gw = _es.enter_context(_sbt("gw", [128, 2, 4, 128], BF16))
            woc = _es.enter_context(_sbt("woc", [128, 2, D], BF16))
            upre = _es.enter_context(_sbt("upre", [128, T], F32))
            uu = _es.enter_context(_sbt("uu", [128, T], F32))
            ub = _es.enter_context(_sbt("ub", [128, T], BF16))
            hsf = _es.enter_context(_sbt("hsf", [128, 2048], F32))
            pa = _es.enter_context(_sbt("pa", [128, 2, 1024], F32))
            pq = _es.enter_context(_sbt("pq", [128, 2, 1024], F32))
            hsb = _es.enter_context(_sbt("hsb", [128, 2, 1024], F32))
            gt = _es.enter_context(_sbt("gt", [128, 3, 512], F32))
            gb = _es.enter_context(_sbt("gb", [128, 2048], BF16))
            yb = _es.enter_context(_sbt("yb", [128, 2048], BF16))
            lv = _es.enter_context(_sbt("lv", [128, 64], F32))
            carry = _es.enter_context(_sbt("carry", [128, 4], F32))
            pu = _es.enter_context(_pst("pu", [128, 2, 512], F32))
            pg = _es.enter_context(_pst("pg", [128, 2, 512], F32))
            pwol = _es.enter_context(_pst("pwol", [128, 512], F32))
            pwol2 = _es.enter_context(_pst("pwol2", [128, 512], F32))
            Rwl, Rgw, Rwoc = regs(2), regs(2), regs(2)
            Rupre, Ruu, Rub, Rhsf, Rpa, Rpq, Rhsb, Rgt, Rgb, Ryb, Rlv, Rcar = (Reg() for _ in range(12))
            Rtmp = regs(2); Rpu, Rpg = regs(2), regs(2); Rpwo = Reg()
            Rpa2, Rpq2, Rhsb2 = regs(2), regs(2), regs(2)
            pcnt = [0]
            pr_ = upre
            ap0 = voff["apar"]
            act(lv[:, 0:16], vecT[:, ap0:ap0 + 16], AF.Exp, [Rvec], [Rlv], scale=-1.0)
            act(lv[:, 0:16], lv[:, 0:16], AF.Ln, [Rlv], [Rlv], scale=1.0, bias=cst[:, 1:2])
            ts(DVE, lv[:, 16:32], lv[:, 0:16], -16.0, ALU.mult, [Rlv], [Rlv])
            ts(DVE, lv[:, 0:16], lv[:, 0:16], -8.0, ALU.mult, [Rlv], [Rlv])
            ts(DVE, lv[:, 32:48], vecT[:, voff["gab"]:voff["gab"] + 16], -1.0, ALU.mult, [Rvec], [Rlv])
            ts(DVE, lv[:, 48:64], vecT[:, voff["gxb"]:voff["gxb"] + 16], -1.0, ALU.mult, [Rvec], [Rlv])
            cw0 = voff["l1cw"]; cb0 = voff["l1cb"]
            blocks5 = [(0, 256)] + [(256 + 512 * i, 256 + 512 * (i + 1)) for i in range(4)]
            def load_chunk(c_):
                b_ = c_ % 2
                dma(GQ, wl[:, b_, :, 0:128], w_in[:, :, D + c_ * 128:D + (c_ + 1) * 128], (), [Rwl[b_]])
                dma(GQ, wl[:, b_, :, 128:256], w_in[:, :, c_ * 128:(c_ + 1) * 128], (), [Rwl[b_]])
                for d_ in range(2):
                    dma(GQ, gw[:, b_, d_ * 2 + 0, :], W["l1_gate_a_w"][d_, c_], (), [Rgw[b_]])
                    dma(GQ, gw[:, b_, d_ * 2 + 1, :], W["l1_gate_x_w"][d_, c_], (), [Rgw[b_]])
                dma(GQ, woc[:, b_, :], w_o[c_ * 128:(c_ + 1) * 128, :], (), [Rwoc[b_]])

            Rpw2 = regs(2)
            rpend = []

            def drain(k):
                for _ in range(min(k, len(rpend))):
                    rpend.pop(0)()

            load_chunk(0)
            for c in range(KC):
                b = c % 2
                tt(DVE, woc[:, b, :], woc[:, b, :], mgb[0][:], ALU.mult, [Rwoc[b], Rmgb[0]], [Rwoc[b]])
                for bi_, (s0, s1) in enumerate(blocks5):
                    n = s1 - s0
                    hr = [RhT[t] for t in range(s0 // 128, s1 // 128)]
                    for k in range(KC):
                        mm(pu[:, bi_ % 2, 0:n], wl[:, b, k, 0:128], hT[:, k, s0:s1], k == 0, k == KC - 1, [Rwl[b]] + hr, [Rpu[bi_ % 2]])
                    cp(ACT, upre[:, s0:s1], pu[:, bi_ % 2, 0:n], [Rpu[bi_ % 2]], [Rupre])
                    if s0 >= 256:
                        for k in range(KC):
                            mm(pg[:, bi_ % 2, 0:n], wl[:, b, k, 128:256], hT[:, k, s0:s1], k == 0, k == KC - 1, [Rwl[b]] + hr, [Rpg[bi_ % 2]])
                        xg, t1, t2 = gt[:, 0, 0:n], gt[:, 1, 0:n], gt[:, 2, 0:n]
                        cp(ACT, xg, pg[:, bi_ % 2, 0:n], [Rpg[bi_ % 2]], [Rgt])
                        tt(POOL, t1, xg, xg, ALU.mult, [Rgt], [Rgt])
                        ts(DVE, t1, t1, 0.044715, ALU.mult, [Rgt], [Rgt], s2=1.0, op1=ALU.add)
                        tt(DVE, t1, t1, xg, ALU.mult, [Rgt], [Rgt])
                        act(t2, t1, AF.Sigmoid, [Rgt], [Rgt], scale=1.5957691216057308)
                        tt(POOL, gb[:, s0 - 256:s1 - 256], xg, t2, ALU.mult, [Rgt], [Rgb])
                drain(6)
                wv = [vecT[:, cw0 + j * 8 + c:cw0 + j * 8 + c + 1] for j in range(4)]
                bv = vecT[:, cb0 + c:cb0 + c + 1]
                for (g0, g1) in ((0, 256), (256, T)):
                    act(uu[:, g0:g1], upre[:, g0:g1], AF.Identity, [Rupre, Rvec], [Ruu], scale=wv[2], bias=bv)
                    stt(uu[:, g0 + 2:g1], upre[:, g0:g1 - 2], wv[0], uu[:, g0 + 2:g1], ALU.mult, ALU.add, [Rupre, Rvec, Ruu], [Ruu])
                    stt(uu[:, g0 + 1:g1], upre[:, g0:g1 - 1], wv[1], uu[:, g0 + 1:g1], ALU.mult, ALU.add, [Rupre, Rvec, Ruu], [Ruu])
                    stt(uu[:, g0:g1 - 1], upre[:, g0 + 1:g1], wv[3], uu[:, g0:g1 - 1], ALU.mult, ALU.add, [Rupre, Rvec, Ruu], [Ruu])
                cp(ACT, ub[:], uu[:], [Ruu], [Rub])
                drain(6)
                for d in range(2):
                    order = PIECES if d == 0 else [PIECES[0], PIECES[2], PIECES[1]]
                    nsp = lv[:, d * 8 + c:d * 8 + c + 1]
                    nsp2 = lv[:, 16 + d * 8 + c:16 + d * 8 + c + 1]
                    gab_ = vecT[:, voff["gab"] + d * 8 + c:voff["gab"] + d * 8 + c + 1]
                    gxb_ = vecT[:, voff["gxb"] + d * 8 + c:voff["gxb"] + d * 8 + c + 1]
                    for pi_, (p0, p1) in enumerate(order):
                        np_ = p1 - p0
                        rS, iS = pr_[:, 0:np_], pr_[:, 1024:1024 + np_]
                        for s0 in range(p0, p1, 512):
                            s1 = min(p1, s0 + 512); n = s1 - s0; o0 = s0 - p0
                            mm(pu[:, 0, 0:n], gw[:, b, d * 2 + 0, :], ub[:, s0:s1], True, True, [Rgw[b], Rub], [Rpu[0]])
                            mm(pu[:, 1, 0:n], gw[:, b, d * 2 + 1, :], ub[:, s0:s1], True, True, [Rgw[b], Rub], [Rpu[1]])
                            act(rS[:, o0:o0 + n], pu[:, 0, 0:n], AF.Sigmoid, [Rpu[0], Rvec], [Rupre], scale=1.0, bias=gab_)
                            act(iS[:, o0:o0 + n], pu[:, 1, 0:n], AF.Sigmoid, [Rpu[1], Rvec], [Rupre], scale=1.0, bias=gxb_)
                        pb_ = pcnt[0] % 2
                        pcnt[0] += 1
                        A_, Q_ = pa[:, pb_, 0:np_], pq[:, pb_, 0:np_]
                        Rpa, Rpq, Rhsb = Rpa2[pb_], Rpq2[pb_], Rhsb2[pb_]
                        act(A_, rS, AF.Exp, [Rupre, Rlv], [Rpa], scale=nsp)
                        act(Q_, rS, AF.Exp, [Rupre, Rlv], [Rpq], scale=nsp2)
                        act(Q_, Q_, AF.Ln, [Rpq], [Rpq], scale=-1.0, bias=cst[:, 1:2])
                        act(Q_, Q_, AF.Exp, [Rpq], [Rpq], scale=0.5)
                        tt(DVE, iS, iS, uu[:, p0:p1], ALU.mult, [Rupre, Ruu], [Rupre])
                        tt(DVE, Q_, Q_, iS, ALU.mult, [Rpq, Rupre], [Rpq])
                        if d == 0:
                            if p0 == 0:
                                dst = hsb[:, pb_, 0:np_]; Rd = Rhsb
                            else:
                                dst = hsf[:, p0 - 256:p1 - 256]; Rd = Rhsf
                            init = 0.0 if pi_ == 0 else carry[:, 0:1]
                            op(DVE, lambda: nc.vector.tensor_tensor_scan(out=dst, data0=A_, data1=Q_, initial=init,
                                                                        op0=ALU.mult, op1=ALU.add), [Rpa, Rpq, Rcar], [Rd])
                            cp(DVE, carry[:, 0:1], dst[:, np_ - 1:np_], [Rd], [Rcar])
                        else:
                            dst = hsb[:, pb_, 0:np_]; Rd = Rhsb
                            init = 0.0 if pi_ == 0 else carry[:, 1:2]
                            op(DVE, lambda: nc.vector.tensor_tensor_scan(out=dst[:, ::-1], data0=A_[:, ::-1], data1=Q_[:, ::-1],
                                                                        initial=init, op0=ALU.mult, op1=ALU.add),
                               [Rpa, Rpq, Rcar], [Rd])
                            cp(DVE, carry[:, 1:2], dst[:, 0:1], [Rd], [Rcar])
                            if p0 >= 256:
                                l0_, l1_ = p0 - 256, p1 - 256
                                tt(DVE, dst, dst, hsf[:, l0_:l1_], ALU.add, [Rd, Rhsf], [Rd])
                                tt(POOL, yb[:, l0_:l1_], dst, gb[:, l0_:l1_], ALU.mult, [Rd, Rgb], [Ryb])
                        if d == 0:
                            drain(6)
                        elif pi_ == 0:
                            drain(len(rpend))
                            if c + 1 < KC:
                                load_chunk(c + 1)
                pws, Rpws = [pwol, pwol2], Rpw2

                def _rstep(tl, half, b=b):
                    tq = 2 + tl
                    c0 = half * 512
                    mm(pws[half][:, 0:512], yb[:, tl * 128:(tl + 1) * 128], woc[:, b, c0:c0 + 512], True, True,
                       [Ryb, Rwoc[b]], [Rpws[half]])
                    tt(DVE, x_tok[:, tq, c0:c0 + 512], pws[half][:, 0:512], x_tok[:, tq, c0:c0 + 512], ALU.add,
                       [Rpws[half], Rx[tq]], [Rx[tq]])
                for tl in range(16):
                    for half in range(2):
                        rpend.append(lambda tl=tl, half=half, f=_rstep: f(tl, half))
            drain(len(rpend))
            fw.barrier()

    def phase_final(tiles_src=None):
        with ExitStack() as _es:
            fnb = _es.enter_context(_sbt("fnb", [128, D], F32))
            osb = _es.enter_context(_sbt("osb", [128, 2, D], F32))
            junk = _es.enter_context(_sbt("junk", [128, D], BF16))
            Rfn, Ros = Reg(), regs(2)
            dma(SP, fnb[:], W["final_norm"].partition_broadcast(128), (), [Rfn])
            for t in range(2, NT):
                act(junk[:], x_tok[:, t, :], AF.Square, [Rx[t]], [Rjunk, Rst], accum_out=ss[:, t:t + 1])
            act(lnv[:, 2:NT], ss[:, 2:NT], AF.Ln, [Rst], [Rst], scale=1.0 / D, bias=cst[:, 0:1])
            act(rstd[:, 2:NT], lnv[:, 2:NT], AF.Exp, [Rst], [Rst], scale=-0.5)
            for t in range(2, NT):
                o = osb[:, t % 2, :]
                stt(o, x_tok[:, t, :], rstd[:, t:t + 1], fnb[:], ALU.mult, ALU.mult, [Rx[t], Rst, Rfn], [Ros[t % 2]])
                dma(SP, out_d[(t - 2) * 128:(t - 1) * 128, :], o, [Ros[t % 2]], ())
            fw.barrier()

    def dump_x():
        print("COUNTS", {e.name: e.count for e in fw.all}, fw.ninst)
        for t in range(2, NT):
            dma(SP, out_d[(t - 2) * 128:(t - 1) * 128, :], x_tok[:, t, :], [Rx[t]], ())
        fw.barrier()

    phase_hT(0, 0, G_ALL)
    if stop == 1:
        dump_x(); return nc
    phase_attn()
    if stop == 2:
        dump_x(); return nc
    phase_hT(0, 1, G_ALL)
    phase_ffn(0, True)
    if stop == 3:
        dump_x(); return nc
    phase_hT(1, 0, G_ALL)
    phase_lru()
    if stop == 4:
        dump_x(); return nc
    phase_hT(1, 1, G_LAT)
    phase_ffn(1, False)
    if stop == 5:
        dump_x(); return nc
    phase_final()
    return nc


def _rope_tables():
    pairs = 16
    inv = (10000.0 ** (-np.arange(pairs, dtype=np.float32) / pairs)).astype(np.float32)
    row = np.repeat(np.arange(32, dtype=np.float32), 64)
    col = np.tile(np.arange(64, dtype=np.float32), 32)
    ang = np.concatenate([row[:, None] * inv, col[:, None] * inv], axis=-1).astype(np.float32)
    return np.cos(ang).astype(np.float32), np.sin(ang).astype(np.float32)


def kernel(_stop=99, **inputs):
    nc = build(_stop)
    cos, sin = _rope_tables()
    shared = {n: np.ascontiguousarray(np.asarray(inputs[n], dtype=np.float32)) for n, _ in WEIGHT_SPECS}
    shared["c_ctx"] = np.ascontiguousarray(np.asarray(inputs["c_ctx"], dtype=np.float32))
    shared["rope_cos"] = cos
    shared["rope_sin"] = sin
    x = np.asarray(inputs["x"], dtype=np.float32)
    c = np.asarray(inputs["c"], dtype=np.float32)
    ctx = np.asarray(inputs["ctx"], dtype=np.float32)
    in_maps = []
    for b in range(8):
        m = dict(shared)
        m["x"] = np.ascontiguousarray(x[b]); m["ctx"] = np.ascontiguousarray(ctx[b]); m["c"] = np.ascontiguousarray(c[b])
        in_maps.append(m)
    res = run_bass_kernel_spmd(nc, in_maps, core_ids=list(range(8)))
    return np.stack([np.asarray(r["out"], dtype=np.float32) for r in res.results], axis=0)
```

```python
import math
import os
from contextlib import ExitStack
import numpy as np
import concourse.bass as bass
import concourse.mybir as mybir
from concourse.bass_utils import run_bass_kernel_spmd

F32 = mybir.dt.float32
BF16 = mybir.dt.bfloat16
AF = mybir.ActivationFunctionType
ALU = mybir.AluOpType
AX = mybir.AxisListType

D = 1024
KC = 8
NT = 18
T = 2304
DFF = 2816
NPAIR = 22
EPS = 1e-6


class Reg:
    __slots__ = ("w", "r")

    def __init__(self):
        self.w = None
        self.r = []


def regs(n):
    return [Reg() for _ in range(n)]


class Eng:
    def __init__(self, fw, e, name, is_dma=False):
        self.fw, self.e, self.name, self.is_dma = fw, e, name, is_dma
        self.count = 0
        self.waited = {}
        if not is_dma:
            self.sem = fw.nc.alloc_semaphore("s_" + name)
        else:
            self.nsem = 6
            self.sems = [fw.nc.alloc_semaphore("d_%s_%d" % (name, i)) for i in range(self.nsem)]

    def sem_val(self, idx):
        if not self.is_dma:
            return self.sem, idx
        i = idx - 1
        return self.sems[i % self.nsem], 16 * (i // self.nsem + 1)


class FW:
    def __init__(self, nc):
        self.nc = nc
        self.pe = Eng(self, nc.tensor, "pe")
        self.act = Eng(self, nc.scalar, "act")
        self.dve = Eng(self, nc.vector, "dve")
        self.pool = Eng(self, nc.gpsimd, "pool")
        self.sp = Eng(self, nc.sync, "sp", True)
        self.gq = Eng(self, nc.gpsimd, "gq", True)
        self.all = [self.pe, self.act, self.dve, self.pool, self.sp, self.gq]
        self.host = {"pe": self.pe, "act": self.act, "dve": self.dve, "pool": self.pool,
                     "sp": self.sp, "gq": self.pool}
        self.ninst = 0

    def _wait(self, eng, dep):
        de, di = dep
        host = self.host[eng.name]
        if de.is_dma:
            sem, val = de.sem_val(di)
            key = (de.name, (di - 1) % de.nsem)
        else:
            sem, val = de.sem, di
            key = de.name
        if host.waited.get(key, 0) >= val:
            return
        host.waited[key] = val
        eng.e.wait_ge(sem, val)

    def op(self, eng, fn, reads=(), writes=()):
        deps = []
        for r in reads:
            if r.w is not None:
                deps.append(r.w)
        for w in writes:
            if w.w is not None:
                deps.append(w.w)
            deps.extend(w.r)
        if eng.is_dma and eng.count >= eng.nsem:
            self._wait(eng, (eng, eng.count - eng.nsem + 1))
        seen = set()
        for d in deps:
            de, di = d
            k = (de.name, di)
            if k in seen:
                continue
            seen.add(k)
            if de is eng and not eng.is_dma:
                if eng is self.pe:
                    continue
                if not any((r.w is not None and r.w[0] is eng and r.w[1] == di) for r in reads):
                    continue
            self._wait(eng, d)
        inst = fn()
        eng.count += 1
        idx = eng.count
        sem, _ = eng.sem_val(idx)
        inst.then_inc(sem, 16 if eng.is_dma else 1)
        self.ninst += 1
        me = (eng, idx)
        for r in reads:
            r.r.append(me)
            if len(r.r) > 64:
                last = {}
                for (e2, i2) in r.r:
                    if e2.name not in last or last[e2.name][1] < i2:
                        last[e2.name] = (e2, i2)
                r.r = list(last.values())
        for w in writes:
            w.w = me
            w.r = []
        return inst

    def barrier(self):
        hosts = [self.pe, self.act, self.dve, self.pool, self.sp]
        for h in hosts:
            for e in self.all:
                if e.count == 0:
                    continue
                if e.is_dma:
                    for di in range(max(1, e.count - e.nsem + 1), e.count + 1):
                        self._wait(h, (e, di))
                elif e is not h:
                    self._wait(h, (e, e.count))


WEIGHT_SPECS = [
    ("l0_ada_w", (1024, 6144)), ("l0_ada_b", (6144,)), ("l0_norm_mix", (1024,)), ("l0_norm_ffn", (1024,)),
    ("l0_w_in", (1024, 2304)), ("l0_lam_q1", (64,)), ("l0_lam_k1", (64,)), ("l0_lam_q2", (64,)),
    ("l0_lam_k2", (64,)), ("l0_subln", (128,)), ("l0_q_norm", (64,)), ("l0_k_norm", (64,)),
    ("l0_w_o", (1024, 1024)), ("l0_ffn_up", (1024, 5632)), ("l0_ffn_conv_w", (3, 5632)),
    ("l0_ffn_conv_b", (5632,)), ("l0_ffn_down", (2816, 1024)),
    ("l1_ada_w", (1024, 6144)), ("l1_ada_b", (6144,)), ("l1_norm_mix", (1024,)), ("l1_norm_ffn", (1024,)),
    ("l1_w_in", (1024, 2048)), ("l1_conv_w", (4, 1024)), ("l1_conv_b", (1024,)),
    ("l1_gate_a_w", (2, 8, 128, 128)), ("l1_gate_a_b", (2, 1024)), ("l1_gate_x_w", (2, 8, 128, 128)),
    ("l1_gate_x_b", (2, 1024)), ("l1_a_param", (2, 1024)), ("l1_w_o", (1024, 1024)),
    ("l1_ffn_up", (1024, 5632)), ("l1_ffn_conv_w", (3, 5632)), ("l1_ffn_conv_b", (5632,)),
    ("l1_ffn_down", (2816, 1024)), ("final_norm", (1024,)),
]


def build(stop=99):
    nc = bass.Bass("TRN2", target_bir_lowering=False)
    fw = FW(nc)
    PE, ACT, DVE, POOL, SP, GQ = fw.pe, fw.act, fw.dve, fw.pool, fw.sp, fw.gq
    _uid = [0]

    def _sbt(name, shape, dt):
        _uid[0] += 1
        return nc.sbuf_tensor("%s_%d" % (name, _uid[0]), shape, dt)

    def _pst(name, shape, dt):
        _uid[0] += 1
        return nc.psum_tensor("%s_%d" % (name, _uid[0]), shape, dt)

    def din(name, shape):
        return nc.dram_tensor(name, list(shape), F32, kind="ExternalInput").ap()

    x_d = din("x", (2048, 1024)); ctx_d = din("ctx", (256, 1024)); c_d = din("c", (1024,)); cctx_d = din("c_ctx", (1024,))
    W = {n: din(n, s) for n, s in WEIGHT_SPECS}
    cos_d = din("rope_cos", (2048, 32)); sin_d = din("rope_sin", (2048, 32))
    out_d = nc.dram_tensor("out", [2048, 1024], F32, kind="ExternalOutput").ap()

    _rec = [None]

    def op(eng, f, r=(), w=()):
        if _rec[0] is not None:
            _rec[0].append((eng, f, list(r), list(w)))
            return None
        return fw.op(eng, f, reads=r, writes=w)

    def ve(eng):
        return nc.vector if eng is DVE else nc.gpsimd

    def mm(out, lhsT, rhs, st, sp_, r, w):
        op(PE, lambda: nc.tensor.matmul(out, lhsT, rhs, start=st, stop=sp_), r, w)

    def tr(out, in_, idt, r, w):
        op(PE, lambda: nc.tensor.transpose(out, in_, idt), r, w)

    def act(out, in_, func, r, w, **kw):
        op(ACT, lambda: nc.scalar.activation(out=out, in_=in_, func=func, **kw), r, w)

    def tt(eng, out, in0, in1, alu, r, w):
        op(eng, lambda: ve(eng).tensor_tensor(out=out, in0=in0, in1=in1, op=alu), r, w)

    def ts(eng, out, in0, s1, op0, r, w, s2=None, op1=None):
        if op1 is None:
            op(eng, lambda: ve(eng).tensor_scalar(out=out, in0=in0, scalar1=s1, scalar2=None, op0=op0), r, w)
        else:
            op(eng, lambda: ve(eng).tensor_scalar(out=out, in0=in0, scalar1=s1, scalar2=s2, op0=op0, op1=op1), r, w)

    def stt(out, in0, scalar, in1, op0, op1, r, w):
        op(DVE, lambda: nc.vector.scalar_tensor_tensor(out=out, in0=in0, scalar=scalar, in1=in1, op0=op0, op1=op1), r, w)

    def cp(eng, out, in_, r, w):
        if eng is ACT:
            op(ACT, lambda: nc.scalar.copy(out, in_), r, w)
        else:
            op(eng, lambda: ve(eng).tensor_copy(out=out, in_=in_), r, w)

    def recip(out, in_, r, w):
        op(DVE, lambda: nc.vector.reciprocal(out=out, in_=in_), r, w)

    def rsum(out, in_, r, w):
        op(DVE, lambda: nc.vector.reduce_sum(out=out, in_=in_, axis=AX.X), r, w)

    def memset(eng, ap, val, w):
        op(eng, lambda: ve(eng).memset(ap, val), (), w)

    def dma(q, out, in_, r, w):
        e = nc.sync if q is SP else nc.gpsimd
        op(q, lambda: e.dma_start(out=out, in_=in_), r, w)

    def bcast_mid(a, n):
        return bass.AP(a.tensor, a.offset, [a.ap[0], [0, n], *a.ap[1:]])

    def bcast_last(a, n):
        return bass.AP(a.tensor, a.offset, [*a.ap, [0, n]])

    sbt = nc.alloc_sbuf_tensor

    x_tok = sbt("x_tok", [128, NT, D], F32); Rx = regs(NT)
    hT = sbt("hT", [128, KC, T], BF16); RhT = regs(NT)
    identf = sbt("identf", [128, 128], F32); identb = sbt("identb", [128, 128], BF16); onesf = sbt("onesf", [128, 128], F32)
    Rc = Reg()
    NV = 640
    vecT = sbt("vecT", [128, NV], F32); Rvec = Reg()
    modT = sbt("modT", [128, 2, 48, 2], F32); Rmod = Reg()
    mgb = [sbt("mgb%d" % i, [128, D], F32) for i in range(2)]; Rmgb = regs(2)
    cs_tok = sbt("cs_tok", [128, 16, 32], F32); sn_tok = sbt("sn_tok", [128, 16, 32], F32); Rrope = Reg()
    Rjunk = Reg()
    ss = sbt("ss", [128, NT], F32); lnv = sbt("lnv", [128, NT], F32); rstd = sbt("rstd", [128, NT], F32); Rst = Reg()
    cst = sbt("cst", [128, 4], F32)
    scl = [sbt("scl%d" % i, [128, KC], F32) for i in range(2)]
    sft = [sbt("sft%d" % i, [128, KC], F32) for i in range(2)]
    Rsc = Reg()
    scb = sbt("scb", [128, KC, 2], BF16); Rscb = Reg()

    vec_list = [("c", c_d, 1024), ("c_ctx", cctx_d, 1024)]
    for l in (0, 1):
        vec_list += [("ada_b%d" % l, W["l%d_ada_b" % l], 6144), ("nmix%d" % l, W["l%d_norm_mix" % l], 1024),
                     ("nffn%d" % l, W["l%d_norm_ffn" % l], 1024),
                     ("fcw%d" % l, W["l%d_ffn_conv_w" % l].rearrange("a b -> (a b)"), 3 * 5632),
                     ("fcb%d" % l, W["l%d_ffn_conv_b" % l], 5632)]
    vec_list += [("l1cw", W["l1_conv_w"].rearrange("a b -> (a b)"), 4096), ("l1cb", W["l1_conv_b"], 1024),
                 ("gab", W["l1_gate_a_b"].rearrange("a b -> (a b)"), 2048),
                 ("gxb", W["l1_gate_x_b"].rearrange("a b -> (a b)"), 2048),
                 ("apar", W["l1_a_param"].rearrange("a b -> (a b)"), 2048)]
    voff = {}
    r0 = 0
    for name, ap_, n in vec_list:
        voff[name] = r0
        r0 += n // 128
    assert r0 <= NV

    with ExitStack() as _es:
        stg = _es.enter_context(_sbt("stg", [128, 5, 128], F32))
        adaw0 = _es.enter_context(_sbt("adaw0", [128, KC, 512], BF16))
        adaw1 = _es.enter_context(_sbt("adaw1", [128, KC, 512], BF16))
        sct = _es.enter_context(_sbt("sct", [128, 16], F32))
        pT0 = _es.enter_context(_pst("pT0", [128, 128], F32))
        modps = _es.enter_context(_pst("modps", [128, 96], F32))
        Rstg, Rsct, RpT0, Rmp = Reg(), Reg(), Reg(), Reg()
        adaw = [adaw0, adaw1]; Radaw = regs(2)
        for t in range(NT):
            src = ctx_d[t * 128:(t + 1) * 128, :] if t < 2 else x_d[(t - 2) * 128:(t - 1) * 128, :]
            dma(SP, x_tok[:, t, :], src, (), [Rx[t]])
        memset(POOL, identf[:], 0.0, [Rc])
        op(POOL, lambda: nc.gpsimd.affine_select(out=identf[:], in_=identf[:], compare_op=ALU.not_equal, fill=1.0,
                                                 base=0, pattern=[[-1, 128]], channel_multiplier=1), [Rc], [Rc])
        cp(POOL, identb[:], identf[:], [Rc], [Rc])
        memset(POOL, onesf[:], 1.0, [Rc])
        memset(POOL, cst[:, 0:1], EPS, [Rc])
        memset(POOL, cst[:, 1:2], 1.0, [Rc])
        memset(POOL, stg[:], 0.0, [Rstg])
        dma(SP, cs_tok[:], cos_d.rearrange("(i p) f -> p i f", p=128), (), [Rrope])
        dma(SP, sn_tok[:], sin_d.rearrange("(i p) f -> p i f", p=128), (), [Rrope])
        for name, ap_, n in vec_list:
            ra, rb = voff[name], voff[name] + n // 128
            r = ra
            while r < rb:
                s = r // 128
                e = min(rb, (s + 1) * 128)
                dma(SP, stg[r - s * 128:e - s * 128, s, :],
                    ap_[(r - ra) * 128:(e - ra) * 128].rearrange("(r p) -> r p", p=128), (), [Rstg])
                r = e
        for s in range(5):
            tr(pT0[:], stg[:, s, :], identf[:], [Rstg, Rc], [RpT0])
            cp(DVE, vecT[:, s * 128:(s + 1) * 128], pT0[:], [RpT0], [Rvec])
        act(sct[:], vecT[:, 0:16], AF.Exp, [Rvec], [Rsct], scale=-1.0)
        ts(DVE, sct[:], sct[:], 1.0, ALU.add, [Rsct], [Rsct])
        recip(sct[:], sct[:], [Rsct], [Rsct])
        tt(DVE, scb[:].rearrange("p k v -> p v k"), vecT[:, 0:16].rearrange("p (v k) -> p v k", v=2),
           sct[:].rearrange("p (v k) -> p v k", v=2), ALU.mult, [Rsct, Rvec], [Rscb])
        for l in (0,):
            wsrc = W["l%d_ada_w" % l].rearrange("(k p) n -> p k n", p=128)
            for blk in range(12):
                wb, Rw = adaw[blk % 2], Radaw[blk % 2]
                dma(GQ, wb[:], wsrc[:, :, blk * 512:(blk + 1) * 512], (), [Rw])
                for j in range(4):
                    col = (blk * 4 + j) * 2
                    for k in range(KC):
                        mm(modps[:, col:col + 2], wb[:, k, j * 128:(j + 1) * 128], scb[:, k, :], k == 0, k == KC - 1,
                           [Rw, Rscb], [Rmp])
            ab = vecT[:, voff["ada_b%d" % l]:voff["ada_b%d" % l] + 48]
            tt(DVE, modT[:, l, :, :], modps[:, 0:96].rearrange("p (j v) -> p j v", v=2), bcast_last(ab, 2), ALU.add,
               [Rmp, Rvec], [Rmod])
        fw.barrier()

    def prep_and_build(l, which, groups, pb, pst):
        gname = ("nmix%d" if which == 0 else "nffn%d") % l
        g = vecT[:, voff[gname]:voff[gname] + 8]
        vs = sorted(set(v for v, _ in groups))
        Rpb, Rdg = Reg(), regs(2)
        with ExitStack() as _es:
            dg0 = _es.enter_context(_sbt("dg0", [128, 128], F32))
            dg1 = _es.enter_context(_sbt("dg1", [128, 128], F32))
            xnb0 = _es.enter_context(_sbt("xnb0", [128, 4, D], BF16))
            xnb1 = _es.enter_context(_sbt("xnb1", [128, 4, D], BF16))
            junk = _es.enter_context(_sbt("junk", [128, D], BF16))
            dgs = [dg0, dg1]
            xnbs = [xnb0, xnb1]; Rxn = regs(2)
            for v in vs:
                s_scale = (3 * which + 1) * 8
                s_shift = (3 * which) * 8
                s_gate = (3 * which + 2) * 8
                stt(scl[v][:], modT[:, l, s_scale:s_scale + 8, v], 1.0, g, ALU.add, ALU.mult, [Rmod, Rvec], [Rsc])
                cp(DVE, sft[v][:], modT[:, l, s_shift:s_shift + 8, v], [Rmod], [Rsc])
                for j in range(8):
                    ts(DVE, dgs[j % 2][:], identf[:], modT[:, l, s_gate + j, v:v + 1], ALU.mult, [Rc, Rmod], [Rdg[j % 2]])
                    mm(pb[:, j * 128:(j + 1) * 128], onesf[:], dgs[j % 2][:], True, True, [Rc, Rdg[j % 2]], [Rpb])
                cp(DVE, mgb[v][:], pb[:, 0:D], [Rpb], [Rmgb[v]])
            tiles_all = [t for _, ts_ in groups for t in ts_]
            for t in tiles_all:
                act(junk[:], x_tok[:, t, :], AF.Square, [Rx[t]], [Rjunk, Rst], accum_out=ss[:, t:t + 1])
            t0, t1 = min(tiles_all), max(tiles_all) + 1
            act(lnv[:, t0:t1], ss[:, t0:t1], AF.Ln, [Rst], [Rst], scale=1.0 / D, bias=cst[:, 0:1])
            act(rstd[:, t0:t1], lnv[:, t0:t1], AF.Exp, [Rst], [Rst], scale=-0.5)
            Rps = regs(2)
            for gi, (v, tiles) in enumerate(groups):
                xn, Rn = xnbs[gi % 2], Rxn[gi % 2]
                for i, t in enumerate(tiles):
                    if i % 2:
                        act(xn[:, i, :], x_tok[:, t, :], AF.Identity, [Rx[t], Rst], [Rn], scale=rstd[:, t:t + 1])
                    else:
                        ts(DVE, xn[:, i, :], x_tok[:, t, :], rstd[:, t:t + 1], ALU.mult, [Rx[t], Rst], [Rn])
                n = len(tiles) * 128
                tok0 = tiles[0] * 128
                for half in range(2):
                    ps_, Rp = pst[half], Rps[half]
                    for ci in range(4):
                        c = half * 4 + ci
                        for i, t in enumerate(tiles):
                            tr(ps_[:, ci, i * 128:(i + 1) * 128], xn[:, i, c * 128:(c + 1) * 128], identb[:], [Rn, Rc], [Rp])
                        act(hT[:, c, tok0:tok0 + n], ps_[:, ci, 0:n], AF.Identity, [Rp, Rsc], [RhT[t] for t in tiles],
                            scale=scl[v][:, c:c + 1], bias=sft[v][:, c:c + 1])
            fw.barrier()

    def phase_hT(l, which, groups):
        with ExitStack() as _es:
            pb = _es.enter_context(_pst("pb", [128, D], F32))
            pst0 = _es.enter_context(_pst("pst0", [128, 4, 512], BF16))
            pst1 = _es.enter_context(_pst("pst1", [128, 4, 512], BF16))
            prep_and_build(l, which, groups, pb, [pst0, pst1])

    G_ALL = [(1, [0, 1])] + [(0, list(range(2 + 4 * i, 6 + 4 * i))) for i in range(4)]
    G_LAT = [(0, list(range(2 + 4 * i, 6 + 4 * i))) for i in range(4)]

    def resid_add(tq, v, lhsT, wmat, pwo, Rpwo, tmpb, Rtmp, rdeps):
        for ci, (c0, cw) in enumerate(((0, 384), (384, 384), (768, 256))):
            mm(pwo[:, 0:cw], lhsT, wmat[:, c0:c0 + cw], True, True, rdeps, [Rpwo])
            tb, Rt = tmpb[ci % 2], Rtmp[ci % 2]
            tt(DVE, tb[:, 0:cw], pwo[:, 0:cw], mgb[v][:, c0:c0 + cw], ALU.mult, [Rpwo, Rmgb[v]], [Rt])
            tt(POOL, x_tok[:, tq, c0:c0 + cw], x_tok[:, tq, c0:c0 + cw], tb[:, 0:cw], ALU.add, [Rt, Rx[tq]], [Rx[tq]])

    def phase_attn():
        w_in = W["l0_w_in"].rearrange("(k p) n -> p k n", p=128)
        w_o = W["l0_w_o"]
        NB = 2
        with ExitStack() as _es:
            wslab = _es.enter_context(_sbt("wslab", [128, NB, KC, 384], BF16))
            qkT = _es.enter_context(_sbt("qkT", [128, NB, 2, T], BF16))
            Va = _es.enter_context(_sbt("Va", [128, NB, NT, 129], BF16))
            kTb = _es.enter_context(_sbt("kTb", [128, T], BF16))
            Vb = _es.enter_context(_sbt("Vb", [128, NT, 2, 65], BF16))
            wou = _es.enter_context(_sbt("wou", [128, NB, D], BF16))
            stgq = _es.enter_context(_sbt("stgq", [128, 2, 256], BF16))
            pTb = _es.enter_context(_sbt("pTb", [128, 3, 1024], BF16))
            rt = _es.enter_context(_sbt("rt", [128, 2, 4, 128], F32))
            xs = _es.enter_context(_sbt("xs", [128, 2, 3, 128], F32))
            sm = _es.enter_context(_sbt("sm", [128, 4, 16], F32))
            fo = _es.enter_context(_sbt("fo", [128, 4, 3, 128], F32))
            ob = _es.enter_context(_sbt("ob", [128, 4, 128], BF16))
            oTs = _es.enter_context(_sbt("oTs", [128, 4, 128], BF16))
            accS = _es.enter_context(_sbt("accS", [128, 1, 4, 132], F32))
            tmpb_ = _es.enter_context(_sbt("tmpb", [128, 2, 512], F32))
            lamb = _es.enter_context(_sbt("lamb", [128, 4, 64], F32))
            lams = _es.enter_context(_sbt("lams", [128, 8], F32))
            nrmb = _es.enter_context(_sbt("nrmb", [128, 256], F32))
            pss0 = _es.enter_context(_pst("pss0", [128, 1024], F32))
            pss1 = _es.enter_context(_pst("pss1", [128, 1024], F32))
            accP = _es.enter_context(_pst("accP", [128, 4, 256], F32))
            bk6 = _es.enter_context(_pst("bk6", [128, 512], F32))
            bk7 = _es.enter_context(_pst("bk7", [128, 1024], BF16))
            pp = bk6[:, 0:384]
            pwo = bk6[:, 0:384]
            ptq = bk7[:, 0:256]
            pto = bk7[:, 256:512]
            Rws, Rqk, RVa, Rwou = regs(NB), regs(NB), regs(NB), regs(NB)
            RkTb, RVb, Rlam, Rnrm = Reg(), Reg(), Reg(), Reg()
            Rstq, RpT, Rrt, Rxs, Rsm, Rfo, Rob, RoT, Rtmp = regs(2), regs(3), regs(2), regs(2), regs(4), regs(4), regs(4), regs(4), regs(2)
            RaccS = regs(2)
            Rpss, Racc, Rpp, Rptq, Rpto = regs(2), regs(4), Reg(), Reg(), Reg()
            Rpwo = Rpp
            pss = [pss0, pss1]
            tmpb = [tmpb_[:, 0, :], tmpb_[:, 1, :]]
            for i, nm in enumerate(("l0_lam_q1", "l0_lam_k1", "l0_lam_q2", "l0_lam_k2")):
                dma(SP, lamb[:, i, :], W[nm].partition_broadcast(128), (), [Rlam])
            dma(SP, nrmb[:, 0:128], W["l0_subln"].partition_broadcast(128), (), [Rnrm])
            dma(SP, nrmb[:, 128:192], W["l0_q_norm"].partition_broadcast(128), (), [Rnrm])
            dma(SP, nrmb[:, 192:256], W["l0_k_norm"].partition_broadcast(128), (), [Rnrm])
            lam_init = 0.8 - 0.6 * math.exp(-0.3 * 0)
            ts(DVE, nrmb[:, 0:128], nrmb[:, 0:128], 1.0 - lam_init, ALU.mult, [Rnrm], [Rnrm])
            tt(DVE, lamb[:, 0, :], lamb[:, 0, :], lamb[:, 1, :], ALU.mult, [Rlam], [Rlam])
            tt(DVE, lamb[:, 2, :], lamb[:, 2, :], lamb[:, 3, :], ALU.mult, [Rlam], [Rlam])
            rsum(lams[:, 0:1], lamb[:, 0, :], [Rlam], [Rlam])
            rsum(lams[:, 1:2], lamb[:, 2, :], [Rlam], [Rlam])
            act(lams[:, 2:4], lams[:, 0:2], AF.Exp, [Rlam], [Rlam])
            tt(DVE, lams[:, 4:5], lams[:, 3:4], lams[:, 2:3], ALU.subtract, [Rlam], [Rlam])
            ts(DVE, lams[:, 5:6], lams[:, 4:5], -lam_init, ALU.add, [Rlam], [Rlam])
            for b in range(NB):
                memset(POOL, Va[:, b, :, 128:129], 1.0, [RVa[b]])
            memset(POOL, Vb[:, :, :, 64:65], 1.0, [RVb])
            nlam = lams[:, 5:6]
            subln_b = nrmb[:, 0:128]
            qn_b = nrmb[:, 128:192]
            kn_b = nrmb[:, 192:256]

            def rope(src, H, t, dst, si, rsrc):
                i = t - 2
                cs = bcast_mid(cs_tok[:, i, :], H); sn = bcast_mid(sn_tok[:, i, :], H)
                sv = src.rearrange("p (h i two) -> p h i two", h=H, two=2)
                dv = dst.rearrange("p (h i two) -> p h i two", h=H, two=2)
                x0, x1 = sv[:, :, :, 0], sv[:, :, :, 1]
                tmp = [rt[:, si, j, 0:H * 32].rearrange("p (h i) -> p h i", h=H) for j in range(4)]
                tt(DVE, tmp[0], x0, cs, ALU.mult, rsrc + [Rrope], [Rrt[si]])
                tt(DVE, tmp[1], x1, sn, ALU.mult, rsrc + [Rrope], [Rrt[si]])
                tt(DVE, tmp[2], x0, sn, ALU.mult, rsrc + [Rrope], [Rrt[si]])
                tt(DVE, tmp[3], x1, cs, ALU.mult, rsrc + [Rrope], [Rrt[si]])
                tt(POOL, dv[:, :, :, 0], tmp[0], tmp[1], ALU.subtract, [Rrt[si]], [Rstq[si]])
                tt(POOL, dv[:, :, :, 1], tmp[2], tmp[3], ALU.add, [Rrt[si]], [Rstq[si]])

            def qknorm(src, gain_b, si, Rp_):
                a, b2, c2 = xs[:, si, 0, :], xs[:, si, 1, :], xs[:, si, 2, :]
                cp(DVE, a, src, Rp_, [Rxs[si]])
                tt(POOL, b2, a, a, ALU.mult, [Rxs[si]], [Rxs[si]])
                rsum(sm[:, si, 0:2], b2.rearrange("p (h d) -> p h d", h=2), [Rxs[si]], [Rsm[si]])
                act(sm[:, si, 2:4], sm[:, si, 0:2], AF.Ln, [Rsm[si]], [Rsm[si]], scale=1.0 / 64, bias=cst[:, 0:1])
                act(sm[:, si, 4:6], sm[:, si, 2:4], AF.Exp, [Rsm[si]], [Rsm[si]], scale=-0.5)
                tt(DVE, b2.rearrange("p (h d) -> p h d", h=2), a.rearrange("p (h d) -> p h d", h=2),
                   bcast_last(sm[:, si, 4:6], 64), ALU.mult, [Rxs[si], Rsm[si]], [Rxs[si]])
                tt(POOL, c2.rearrange("p (h d) -> p h d", h=2), b2.rearrange("p (h d) -> p h d", h=2),
                   bcast_mid(gain_b, 2), ALU.mult, [Rxs[si], Rnrm], [Rxs[si]])
                return c2

            ppbuf = [accP[:, 0:2, :].rearrange("p a c -> p (a c)"), accP[:, 2:4, :].rearrange("p a c -> p (a c)")]
            Rppbuf = [[Racc[0], Racc[1]], [Racc[2], Racc[3]]]
            ptqs = [bk7[:, 0:256], bk7[:, 256:512]]
            ptos = [bk7[:, 512:640], bk7[:, 640:768]]
            Rbk7 = Reg()
            Rptqs, Rptos = [Rbk7, Rbk7], [Rbk7, Rbk7]
            Rbk6 = Reg()

            def load_slab(kind, idx, b, sb_):
                if kind == "diff":
                    cols = [(idx * 128, 128), (512 + idx * 128, 128), (1024 + idx * 128, 128)]
                elif kind == "kv":
                    cols = [(2048, 256)]
                else:
                    cols = [(1536 + 64 * idx, 64), (1536 + 64 * (idx + 4), 64)]
                o = 0
                for (c0, cw) in cols:
                    dma(GQ, wslab[:, sb_, :, o:o + cw], w_in[:, :, c0:c0 + cw], (), [Rws[sb_]])
                    o += cw
                if kind == "diff":
                    dma(GQ, wou[:, b, :], w_o[idx * 128:(idx + 1) * 128, :], (), [Rwou[b]])
                elif kind == "gq":
                    dma(GQ, wou[0:64, b, :], w_o[512 + 64 * idx:512 + 64 * idx + 64, :], (), [Rwou[b]])
                    dma(GQ, wou[64:128, b, :], w_o[512 + 64 * (idx + 4):512 + 64 * (idx + 4) + 64, :], (), [Rwou[b]])

            def project(kind, idx, b, sb_):
                if kind == "diff":
                    cols = [(idx * 128, 128), (512 + idx * 128, 128), (1024 + idx * 128, 128)]
                elif kind == "kv":
                    cols = [(2048, 256)]
                else:
                    cols = [(1536 + 64 * idx, 64), (1536 + 64 * (idx + 4), 64)]
                ncol = sum(cw for _, cw in cols)
                pending = []
                for t0_ in range(0, NT, 2):
                    pair = [t0_, t0_ + 1]
                    for t in pair:
                        si = t % 2
                        for k in range(KC):
                            mm(ppbuf[si][:, 0:ncol], hT[:, k, t * 128:(t + 1) * 128], wslab[:, sb_, k, 0:ncol], k == 0, k == KC - 1,
                               [RhT[t], Rws[sb_]], Rppbuf[si])
                    for pfn in pending:
                        pfn()
                    pending = []
                    recs = []
                    for t in pair:
                        si = t % 2
                        pp = ppbuf[si]
                        Rp = Rppbuf[si]
                        lat = t >= 2
                        ptq, Rptq = ptqs[si], Rptqs[si]
                        _rec[0] = []
                        if kind == "diff":
                            if lat:
                                rope(pp[:, 0:256], 4, t, stgq[:, si, 0:256], si, Rp)
                            else:
                                cp(DVE, stgq[:, si, 0:256], pp[:, 0:256], Rp, [Rstq[si]])
                            cp(DVE, Va[:, b, t, 0:128], pp[:, 256:384], Rp, [RVa[b]])

                            def pfn(t=t, si=si, ptq=ptq, Rptq=Rptq):
                                tr(ptq[:, 0:128], stgq[:, si, 0:128], identb[:], [Rstq[si], Rc], [Rptq])
                                tr(ptq[:, 128:256], stgq[:, si, 128:256], identb[:], [Rstq[si], Rc], [Rptq])
                                cp(DVE, qkT[:, b, :, t * 128:(t + 1) * 128], ptq[:, 0:256].rearrange("p (a q) -> p a q", a=2),
                                   [Rptq], [Rqk[b]])
                        elif kind == "kv":
                            xn_ = qknorm(pp[:, 0:128], kn_b, si, Rp)
                            cp(DVE, Vb[:, t, :, 0:64], pp[:, 128:256].rearrange("p (g d) -> p g d", g=2), Rp, [RVb])
                            if lat:
                                rope(xn_, 2, t, stgq[:, si, 0:128], si, [Rxs[si]])
                            else:
                                cp(POOL, stgq[:, si, 0:128], xn_, [Rxs[si]], [Rstq[si]])

                            def pfn(t=t, si=si, ptq=ptq, Rptq=Rptq):
                                tr(ptq[:, 0:128], stgq[:, si, 0:128], identb[:], [Rstq[si], Rc], [Rptq])
                                cp(DVE, kTb[:, t * 128:(t + 1) * 128], ptq[:, 0:128], [Rptq], [RkTb])
                        else:
                            xn_ = qknorm(pp[:, 0:128], qn_b, si, Rp)
                            if lat:
                                rope(xn_, 2, t, stgq[:, si, 0:128], si, [Rxs[si]])
                            else:
                                cp(POOL, stgq[:, si, 0:128], xn_, [Rxs[si]], [Rstq[si]])

                            def pfn(t=t, si=si, ptq=ptq, Rptq=Rptq):
                                tr(ptq[:, 0:128], stgq[:, si, 0:128], identb[:], [Rstq[si], Rc], [Rptq])
                                cp(DVE, qkT[:, b, 0, t * 128:(t + 1) * 128], ptq[:, 0:128], [Rptq], [Rqk[b]])
                        recs.append(_rec[0])
                        _rec[0] = None
                        pending.append(pfn)
                    for k_ in range(max(len(r_) for r_ in recs)):
                        for r_ in recs:
                            if k_ < len(r_):
                                e_, f_, rr_, ww_ = r_[k_]
                                fw.op(e_, f_, reads=rr_, writes=ww_)
                for pfn in pending:
                    pfn()

            def attend(kind, idx, b):
                if kind == "diff":
                    dv1 = 129
                    kT_ap = lambda s, kt: qkT[s * 64:(s + 1) * 64, b, 1, kt * 128:(kt + 1) * 128]
                    v_ap = lambda s, kt: Va[:, b, kt, 0:129]
                    Rk, Rv = Rqk[b], RVa[b]
                else:
                    dv1 = 65
                    kT_ap = lambda s, kt: kTb[s * 64:(s + 1) * 64, kt * 128:(kt + 1) * 128]
                    v_ap = lambda s, kt: Vb[:, kt, s, 0:65]
                    Rk, Rv = RkTb, RVb
                blocks = [(0, [0, 1])] + [(256 + 256 * i, list(range(NT))) for i in range(8)]
                G = []
                for bi_, (q0, ktiles) in enumerate(blocks):
                    for g0 in range(0, len(ktiles), 2):
                        G.append((bi_, q0, ktiles[g0:g0 + 2], g0 == 0, g0 + 2 >= len(ktiles)))
                n_g = len(G)

                def QK(i):
                    bi_, q0, grp, first, last = G[i]
                    pb_, Rp = pss[i % 2], Rpss[i % 2]
                    for gi, kt in enumerate(grp):
                        for s in range(2):
                            o_ = s * 512 + gi * 256
                            mm(pb_[:, o_:o_ + 256], kT_ap(s, kt), qkT[s * 64:(s + 1) * 64, b, 0, q0:q0 + 256], True, True,
                               [Rk, Rqk[b]], [Rp])

                def EXP(i):
                    grp = G[i][2]
                    n = len(grp) * 256
                    src = pss[i % 2][:, :].rearrange("p (s c) -> p s c", s=2)[:, :, 0:n]
                    dst = pTb[:, i % 3, :].rearrange("p (s c) -> p s c", s=2)[:, :, 0:n]
                    act(dst, src, AF.Exp, [Rpss[i % 2]], [RpT[i % 3]], scale=0.125)

                def PV(i):
                    bi_, q0, grp, first, last = G[i]
                    for gi, kt in enumerate(grp):
                        for s in range(2):
                            for qi in range(2):
                                ai = s * 2 + qi
                                o_ = s * 512 + gi * 256 + qi * 128
                                mm(accP[:, ai, 0:dv1], pTb[:, i % 3, o_:o_ + 128], v_ap(s, kt),
                                   first and gi == 0 and qi == 0, last and gi == len(grp) - 1, [RpT[i % 3], Rv], [Racc[ai]])

                def FIN_A(bi_, q0):
                    fb = bi_ % 2
                    fa = 0
                    cp(DVE, accS[:, fb * 0, 0:2, 0:dv1], accP[:, 0:2, 0:dv1], [Racc[0], Racc[1]], [RaccS[0]])
                    cp(DVE, accS[:, 0, 2:4, 0:dv1], accP[:, 2:4, 0:dv1], [Racc[2], Racc[3]], [RaccS[0]])
                    for qi in range(2):
                        sl = fb * 2 + qi
                        if kind == "diff":
                            a0, a1 = accS[:, 0, 0 + qi, :], accS[:, 0, 2 + qi, :]
                            recip(sm[:, sl, 8:9], a0[:, 128:129], [RaccS[0]], [Rsm[sl]])
                            recip(sm[:, sl, 9:10], a1[:, 128:129], [RaccS[0]], [Rsm[sl]])
                            tt(DVE, sm[:, sl, 10:11], sm[:, sl, 9:10], nlam, ALU.mult, [Rsm[sl], Rlam], [Rsm[sl]])
                            ts(DVE, fo[:, sl, 0, :], a0[:, 0:128], sm[:, sl, 8:9], ALU.mult, [RaccS[0], Rsm[sl]], [Rfo[sl]])
                            stt(fo[:, sl, 1, :], a1[:, 0:128], sm[:, sl, 10:11], fo[:, sl, 0, :], ALU.mult, ALU.add,
                                [RaccS[0], Rsm[sl], Rfo[sl]], [Rfo[sl]])
                            tt(POOL, fo[:, sl, 2, :], fo[:, sl, 1, :], fo[:, sl, 1, :], ALU.mult, [Rfo[sl]], [Rfo[sl]])
                            rsum(sm[:, sl, 11:12], fo[:, sl, 2, :], [Rfo[sl]], [Rsm[sl]])
                        else:
                            for s in range(2):
                                a_ = accS[:, 0, s * 2 + qi, :]
                                recip(sm[:, sl, 8 + s:9 + s], a_[:, 64:65], [RaccS[0]], [Rsm[sl]])
                                ts(DVE, ob[:, sl, s * 64:(s + 1) * 64], a_[:, 0:64], sm[:, sl, 8 + s:9 + s], ALU.mult,
                                   [RaccS[0], Rsm[sl]], [Rob[sl]])

                def FIN_A2(bi_, q0):
                    if kind != "diff":
                        return
                    fb = bi_ % 2
                    for qi in range(2):
                        sl = fb * 2 + qi
                        act(sm[:, sl, 12:13], sm[:, sl, 11:12], AF.Ln, [Rsm[sl]], [Rsm[sl]], scale=1.0 / 128, bias=cst[:, 0:1])
                        act(sm[:, sl, 13:14], sm[:, sl, 12:13], AF.Exp, [Rsm[sl]], [Rsm[sl]], scale=-0.5)
                        stt(ob[:, sl, :], fo[:, sl, 1, :], sm[:, sl, 13:14], subln_b, ALU.mult, ALU.mult,
                            [Rfo[sl], Rsm[sl], Rnrm], [Rob[sl]])

                def FIN_B0(bi_, q0):
                    fb = bi_ % 2
                    for qi in range(2):
                        sl = fb * 2 + qi
                        pto, Rpto = ptos[qi], Rptos[qi]
                        tr(pto[:, 0:128], ob[:, sl, :], identb[:], [Rob[sl], Rc], [Rpto])
                    for qi in range(2):
                        sl = fb * 2 + qi
                        pto, Rpto = ptos[qi], Rptos[qi]
                        cp(DVE, oTs[:, sl, :], pto[:, 0:128], [Rpto], [RoT[sl]])

                def FIN_Bk(bi_, q0, kk):
                    fb = bi_ % 2
                    qi, j = kk // 2, kk % 2
                    sl = fb * 2 + qi
                    tq = q0 // 128 + qi
                    v = 1 if tq < 2 else 0
                    h = j % 2
                    c0 = j * 512
                    mm(bk6[:, 0:512], oTs[:, sl, :], wou[:, b, c0:c0 + 512], True, True, [RoT[sl], Rwou[b]], [Rbk6])
                    tt(DVE, tmpb[h][:, 0:512], bk6[:, 0:512], mgb[v][:, c0:c0 + 512], ALU.mult, [Rbk6, Rmgb[v]], [Rtmp[h]])
                    tt(POOL, x_tok[:, tq, c0:c0 + 512], x_tok[:, tq, c0:c0 + 512], tmpb[h][:, 0:512], ALU.add,
                       [Rtmp[h], Rx[tq]], [Rx[tq]])

                def run_stage(st_, b2_, q2_):
                    if st_ == 0:
                        FIN_A2(b2_, q2_)
                    elif st_ == 1:
                        FIN_B0(b2_, q2_)
                    else:
                        FIN_Bk(b2_, q2_, st_ - 2)

                sched = []
                QK(0)
                if n_g > 1:
                    QK(1)
                for i in range(n_g):
                    EXP(i)
                    if i + 2 < n_g:
                        QK(i + 2)
                    PV(i)
                    bi_, q0, grp, first, last = G[i]
                    if last:
                        FIN_A(bi_, q0)
                        sched.append((i + 4, 0, bi_, q0))
                        sched.append((i + 6, 1, bi_, q0))
                        for kk in range(4):
                            sched.append((i + 7 + kk, 2 + kk, bi_, q0))
                        sched.sort()
                    while sched and sched[0][0] <= i:
                        _, st_, b2_, q2_ = sched.pop(0)
                        run_stage(st_, b2_, q2_)
                for (_, st_, b2_, q2_) in sched:
                    run_stage(st_, b2_, q2_)

            units = [("diff", h) for h in range(4)] + [("kv", 0)] + [("gq", j) for j in range(4)]
            plan = []
            bi = 0
            for pos, (kind, idx) in enumerate(units):
                if kind == "kv":
                    plan.append((kind, idx, bi % NB, pos % 2))
                else:
                    plan.append((kind, idx, bi % NB, pos % 2))
                    bi += 1
            load_slab(*plan[0])
            for pos, (kind, idx, b, sb_) in enumerate(plan):
                project(kind, idx, b, sb_)
                if pos + 1 < len(plan):
                    load_slab(*plan[pos + 1])
                if kind != "kv":
                    attend(kind, idx, b)
            fw.barrier()

    def phase_ffn(l, with_ctx):
        w_up = W["l%d_ffn_up" % l].rearrange("(k p) n -> p k n", p=128)
        w_dn = W["l%d_ffn_down" % l]
        cw0 = voff["fcw%d" % l]; cb0 = voff["fcb%d" % l]
        groups = [list(range(0, 4)), list(range(4, 8)), list(range(8, 12)), list(range(12, 16)), list(range(16, 19)), list(range(19, 22))]
        blocks = []
        if with_ctx:
            blocks.append((0, 256, 0, 256, True, True))
        s_ = 256
        for nt_ in (3, 3, 3, 3, 3, 1):
            blocks.append((s_, s_ + nt_ * 128, 256, T))
            s_ += nt_ * 128
        blocks = [(b_[0], b_[1], b_[2], b_[3]) for b_ in blocks]
        with ExitStack() as _es:
            wup = _es.enter_context(_sbt("wup", [128, 2, KC, 4, 256], BF16))
            wdn = _es.enter_context(_sbt("wdn", [128, 2, 4, D], BF16))
            actT = _es.enter_context(_sbt("actT", [128, 2, 4, 384], BF16))
            cvg = _es.enter_context(_sbt("cvg", [128, 2, 2, 384], F32))
            sgt = _es.enter_context(_sbt("sgt", [128, 2, 384], F32))
            tmpf = _es.enter_context(_sbt("tmpf", [128, 2, 512], F32))
            pu0 = _es.enter_context(_pst("pu0", [128, 2, 512], F32))
            pu1 = _es.enter_context(_pst("pu1", [128, 2, 512], F32))
            pwof = _es.enter_context(_pst("pwof", [128, 512], F32))
            pwof2 = _es.enter_context(_pst("pwof2", [128, 512], F32))
            Rwup, Rwdn, RaT, Rcv2, Rsg, Rtmp, Rpu2 = regs(2), regs(2), regs(2), [regs(2), regs(2)], regs(2), regs(2), [regs(2), regs(2)]
            Rpwo = Reg()
            pus = [pu0, pu1]
            tmpb = [tmpf[:, 0, :], tmpf[:, 1, :]]
            it = 0
            blk_ctr = [0]
            pend_list = []
            mod_steps = []
            if l == 0:
                adawA = _es.enter_context(_sbt("adawA", [128, KC, 256], BF16))
                adawB = _es.enter_context(_sbt("adawB", [128, KC, 256], BF16))
                modps1 = _es.enter_context(_pst("modps1", [128, 96], F32))
                adaw_, Radaw_, Rmp1 = [adawA, adawB], regs(2), Reg()
                wsrc1 = W["l1_ada_w"].rearrange("(k p) n -> p k n", p=128)

                def _mload(blk):
                    dma(GQ, adaw_[blk % 2][:], wsrc1[:, :, blk * 256:(blk + 1) * 256], (), [Radaw_[blk % 2]])

                def _mstep(blk):
                    if blk + 1 < 24:
                        _mload(blk + 1)
                    wb_, Rw_ = adaw_[blk % 2], Radaw_[blk % 2]
                    for j4 in range(2):
                        col = (blk * 2 + j4) * 2
                        for k in range(KC):
                            mm(modps1[:, col:col + 2], wb_[:, k, j4 * 128:(j4 + 1) * 128], scb[:, k, :], k == 0, k == KC - 1,
                               [Rw_, Rscb], [Rmp1])
                    if blk == 23:
                        ab1 = vecT[:, voff["ada_b1"]:voff["ada_b1"] + 48]
                        tt(DVE, modT[:, 1, :, :], modps1[:, 0:96].rearrange("p (j v) -> p j v", v=2), bcast_last(ab1, 2), ALU.add,
                           [Rmp1, Rvec], [Rmod])
                _mload(0)
                mod_steps = [(lambda blk=blk: _mstep(blk)) for blk in range(24)]
            pwoH = [pwof[:, 0:512], pwof2[:, 0:512]]
            RpwoH = regs(2)
            def load_group(gi_):
                wb_ = gi_ % 2
                for pi, j in enumerate(groups[gi_]):
                    dma(GQ, wup[:, wb_, :, pi, 0:128], w_up[:, :, j * 128:(j + 1) * 128], (), [Rwup[wb_]])
                    dma(GQ, wup[:, wb_, :, pi, 128:256], w_up[:, :, DFF + j * 128:DFF + (j + 1) * 128], (), [Rwup[wb_]])
                    dma(GQ, wdn[:, wb_, pi, :], w_dn[j * 128:(j + 1) * 128, :], (), [Rwdn[wb_]])

            load_group(0)
            for gi, pairs in enumerate(groups):
                wb = gi % 2
                for bi_, (st, en, seg0, seg1) in enumerate(blocks):
                    in0 = max(seg0, st - 1); in1 = min(seg1, en + 1)
                    n_in = in1 - in0; n = en - st
                    L = st - in0
                    Rr = in1 - en
                    ab = blk_ctr[0] % 2
                    blk_ctr[0] += 1
                    tiles = list(range(st // 128, en // 128))
                    for pi, j in enumerate(pairs):
                        pu, Rp2 = pus[it % 2], Rpu2[it % 2]
                        cb_ = it % 2
                        it += 1
                        for half in range(2):
                            for k in range(KC):
                                mm(pu[:, half, 0:n_in], wup[:, wb, k, pi, half * 128:(half + 1) * 128], hT[:, k, in0:in1],
                                   k == 0, k == KC - 1, [Rwup[wb]] + [RhT[t] for t in range(in0 // 128, (in1 - 1) // 128 + 1)], [Rp2[half]])
                        for half in range(2):
                            fch = j + half * NPAIR
                            w0 = vecT[:, cw0 + fch:cw0 + fch + 1]
                            w1 = vecT[:, cw0 + 44 + fch:cw0 + 44 + fch + 1]
                            w2 = vecT[:, cw0 + 88 + fch:cw0 + 88 + fch + 1]
                            bb = vecT[:, cb0 + fch:cb0 + fch + 1]
                            c_ = cvg[:, cb_, half, :]
                            Rp = Rp2[half]
                            Rcvh = Rcv2[cb_][half]
                            act(c_[:, 0:n], pu[:, half, L:L + n], AF.Identity, [Rp, Rvec], [Rcvh], scale=w1, bias=bb)
                            if L == 1:
                                stt(c_[:, 0:n], pu[:, half, 0:n], w0, c_[:, 0:n], ALU.mult, ALU.add, [Rp, Rvec, Rcvh], [Rcvh])
                            else:
                                stt(c_[:, 1:n], pu[:, half, 0:n - 1], w0, c_[:, 1:n], ALU.mult, ALU.add, [Rp, Rvec, Rcvh], [Rcvh])
                            if Rr == 1:
                                stt(c_[:, 0:n], pu[:, half, L + 1:L + 1 + n], w2, c_[:, 0:n], ALU.mult, ALU.add, [Rp, Rvec, Rcvh], [Rcvh])
                            else:
                                stt(c_[:, 0:n - 1], pu[:, half, L + 1:L + n], w2, c_[:, 0:n - 1], ALU.mult, ALU.add, [Rp, Rvec, Rcvh], [Rcvh])
                        act(sgt[:, cb_, 0:n], cvg[:, cb_, 1, 0:n], AF.Silu, [Rcv2[cb_][1]], [Rsg[cb_]])
                        tt(DVE, actT[:, ab, pi, 0:n], sgt[:, cb_, 0:n], cvg[:, cb_, 0, 0:n], ALU.mult, [Rsg[cb_], Rcv2[cb_][0]], [RaT[ab]])
                        if pend_list:
                            pend_list.pop(0)()
                    def DOWN(ti, tq, ab=ab, wb=wb, npair=len(pairs)):
                        if True:
                            v = 1 if tq < 2 else 0
                            for jj in range(2):
                                h = jj % 2
                                c0 = jj * 512
                                for pi in range(npair):
                                    mm(pwoH[h], actT[:, ab, pi, ti * 128:(ti + 1) * 128], wdn[:, wb, pi, c0:c0 + 512],
                                       pi == 0, pi == npair - 1, [RaT[ab], Rwdn[wb]], [RpwoH[h]])
                                tt(DVE, tmpb[h][:, 0:512], pwoH[h], mgb[v][:, c0:c0 + 512], ALU.mult, [RpwoH[h], Rmgb[v]], [Rtmp[h]])
                                tt(POOL, x_tok[:, tq, c0:c0 + 512], x_tok[:, tq, c0:c0 + 512], tmpb[h][:, 0:512], ALU.add,
                                   [Rtmp[h], Rx[tq]], [Rx[tq]])
                    while pend_list:
                        pend_list.pop(0)()
                    for ti, tq in enumerate(tiles):
                        pend_list.append(lambda ti=ti, tq=tq, DOWN=DOWN: DOWN(ti, tq))
                    if bi_ == 0 and gi + 1 < len(groups):
                        load_group(gi + 1)
                    if mod_steps and bi_ >= 1:
                        mod_steps.pop(0)()
            while pend_list:
                pend_list.pop(0)()
            while mod_steps:
                mod_steps.pop(0)()
            fw.barrier()

    def phase_lru():
        w_in = W["l1_w_in"].rearrange("(k p) n -> p k n", p=128)
        w_o = W["l1_w_o"]
        PIECES = [(0, 256), (256, 1280), (1280, T)]
        with ExitStack() as _es:
            wl = _es.enter_context(_sbt("wl", [128, 2, KC, 256], BF16))
            gw = _es.enter_context(_sbt("gw", [128, 2, 4, 128], BF16))
            woc = _es.enter_context(_sbt("woc", [128, 2, D], BF16))
            upre = _es.enter_context(_sbt("upre", [128, T], F32))
            uu = _es.enter_context(_sbt("uu", [128, T], F32))
            ub = _es.enter_context(_sbt("ub", [128, T], BF16))
            hsf = _es.enter_context(_sbt("hsf", [128, 2048], F32))
            pa = _es.enter_context(_sbt("pa", [128, 2, 1024], F32))
            pq = _es.enter_context(_sbt("pq", [128, 2, 1024], F32))
            hsb = _es.enter_context(_sbt("hsb", [128, 2, 1024], F32))
            gt = _es.enter_context(_sbt("gt", [128, 3, 512], F32))
            gb = _es.enter_context(_sbt("gb", [128, 2048], BF16))
            yb = _es.enter_context(_sbt("yb", [128, 2048], BF16))
            lv = _es.enter_context(_sbt("lv", [128, 64], F32))
            carry = _es.enter_context(_sbt("carry", [128, 4], F32))
            pu = _es.enter_context(_pst("pu", [128, 2, 512], F32))
            pg = _es.enter_context(_pst("pg", [128, 2, 512], F32))
            pwol = _es.enter_context(_pst("pwol", [128, 512], F32))
            pwol2 = _es.enter_context(_pst("pwol2", [128, 512], F32))
            Rwl, Rgw, Rwoc = regs(2), regs(2), regs(2)
            Rupre, Ruu, Rub, Rhsf, Rpa, Rpq, Rhsb, Rgt, Rgb, Ryb, Rlv, Rcar = (Reg() for _ in range(12))
            Rtmp = regs(2); Rpu, Rpg = regs(2), regs(2); Rpwo = Reg()
            Rpa2, Rpq2, Rhsb2 = regs(2), regs(2), regs(2)
            pcnt = [0]
            pr_ = upre
            ap0 = voff["apar"]
            act(lv[:, 0:16], vecT[:, ap0:ap0 + 16], AF.Exp, [Rvec], [Rlv], scale=-1.0)
            act(lv[:, 0:16], lv[:, 0:16], AF.Ln, [Rlv], [Rlv], scale=1.0, bias=cst[:, 1:2])
            ts(DVE, lv[:, 16:32], lv[:, 0:16], -16.0, ALU.mult, [Rlv], [Rlv])
            ts(DVE, lv[:, 0:16], lv[:, 0:16], -8.0, ALU.mult, [Rlv], [Rlv])
            ts(DVE, lv[:, 32:48], vecT[:, voff["gab"]:voff["gab"] + 16], -1.0, ALU.mult, [Rvec], [Rlv])
            ts(DVE, lv[:, 48:64], vecT[:, voff["gxb"]:voff["gxb"] + 16], -1.0, ALU.mult, [Rvec], [Rlv])
            cw0 = voff["l1cw"]; cb0 = voff["l1cb"]
            blocks5 = [(0, 256)] + [(256 + 512 * i, 256 + 512 * (i + 1)) for i in range(4)]
            def load_chunk(c_):
                b_ = c_ % 2
                dma(GQ, wl[:, b_, :, 0:128], w_in[:, :, D + c_ * 128:D + (c_ + 1) * 128], (), [Rwl[b_]])
                dma(GQ, wl[:, b_, :, 128:256], w_in[:, :, c_ * 128:(c_ + 1) * 128], (), [Rwl[b_]])
                for d_ in range(2):
                    dma(GQ, gw[:, b_, d_ * 2 + 0, :], W["l1_gate_a_w"][d_, c_], (), [Rgw[b_]])
                    dma(GQ, gw[:, b_, d_ * 2 + 1, :], W["l1_gate_x_w"][d_, c_], (), [Rgw[b_]])
                dma(GQ, woc[:, b_, :], w_o[c_ * 128:(c_ + 1) * 128, :], (), [Rwoc[b_]])

            Rpw2 = regs(2)
            rpend = []

            def drain(k):
                for _ in range(min(k, len(rpend))):
                    rpend.pop(0)()

            load_chunk(0)
            for c in range(KC):
                b = c % 2
                tt(DVE, woc[:, b, :], woc[:, b, :], mgb[0][:], ALU.mult, [Rwoc[b], Rmgb[0]], [Rwoc[b]])
                for bi_, (s0, s1) in enumerate(blocks5):
                    n = s1 - s0
                    hr = [RhT[t] for t in range(s0 // 128, s1 // 128)]
                    for k in range(KC):
                        mm(pu[:, bi_ % 2, 0:n], wl[:, b, k, 0:128], hT[:, k, s0:s1], k == 0, k == KC - 1, [Rwl[b]] + hr, [Rpu[bi_ % 2]])
                    cp(ACT, upre[:, s0:s1], pu[:, bi_ % 2, 0:n], [Rpu[bi_ % 2]], [Rupre])
                    if s0 >= 256:
                        for k in range(KC):
                            mm(pg[:, bi_ % 2, 0:n], wl[:, b, k, 128:256], hT[:, k, s0:s1], k == 0, k == KC - 1, [Rwl[b]] + hr, [Rpg[bi_ % 2]])
                        xg, t1, t2 = gt[:, 0, 0:n], gt[:, 1, 0:n], gt[:, 2, 0:n]
                        cp(ACT, xg, pg[:, bi_ % 2, 0:n], [Rpg[bi_ % 2]], [Rgt])
                        tt(POOL, t1, xg, xg, ALU.mult, [Rgt], [Rgt])
                        ts(DVE, t1, t1, 0.044715, ALU.mult, [Rgt], [Rgt], s2=1.0, op1=ALU.add)
                        tt(DVE, t1, t1, xg, ALU.mult, [Rgt], [Rgt])
                        act(t2, t1, AF.Sigmoid, [Rgt], [Rgt], scale=1.5957691216057308)
                        tt(POOL, gb[:, s0 - 256:s1 - 256], xg, t2, ALU.mult, [Rgt], [Rgb])
                drain(6)
                wv = [vecT[:, cw0 + j * 8 + c:cw0 + j * 8 + c + 1] for j in range(4)]
                bv = vecT[:, cb0 + c:cb0 + c + 1]
                for (g0, g1) in ((0, 256), (256, T)):
                    act(uu[:, g0:g1], upre[:, g0:g1], AF.Identity, [Rupre, Rvec], [Ruu], scale=wv[2], bias=bv)
                    stt(uu[:, g0 + 2:g1], upre[:, g0:g1 - 2], wv[0], uu[:, g0 + 2:g1], ALU.mult, ALU.add, [Rupre, Rvec, Ruu], [Ruu])
                    stt(uu[:, g0 + 1:g1], upre[:, g0:g1 - 1], wv[1], uu[:, g0 + 1:g1], ALU.mult, ALU.add, [Rupre, Rvec, Ruu], [Ruu])
                    stt(uu[:, g0:g1 - 1], upre[:, g0 + 1:g1], wv[3], uu[:, g0:g1 - 1], ALU.mult, ALU.add, [Rupre, Rvec, Ruu], [Ruu])
                cp(ACT, ub[:], uu[:], [Ruu], [Rub])
                drain(6)
                for d in range(2):
                    order = PIECES if d == 0 else [PIECES[0], PIECES[2], PIECES[1]]
                    nsp = lv[:, d * 8 + c:d * 8 + c + 1]
                    nsp2 = lv[:, 16 + d * 8 + c:16 + d * 8 + c + 1]
                    gab_ = vecT[:, voff["gab"] + d * 8 + c:voff["gab"] + d * 8 + c + 1]
                    gxb_ = vecT[:, voff["gxb"] + d * 8 + c:voff["gxb"] + d * 8 + c + 1]
                    for pi_, (p0, p1) in enumerate(order):
                        np_ = p1 - p0
                        rS, iS = pr_[:, 0:np_], pr_[:, 1024:1024 + np_]
                        for s0 in range(p0, p1, 512):
                            s1 = min(p1, s0 + 512); n = s1 - s0; o0 = s0 - p0
                            mm(pu[:, 0, 0:n], gw[:, b, d * 2 + 0, :], ub[:, s0:s1], True, True, [Rgw[b], Rub], [Rpu[0]])
                            mm(pu[:, 1, 0:n], gw[:, b, d * 2 + 1, :], ub[:, s0:s1], True, True, [Rgw[b], Rub], [Rpu[1]])
                            act(rS[:, o0:o0 + n], pu[:, 0, 0:n], AF.Sigmoid, [Rpu[0], Rvec], [Rupre], scale=1.0, bias=gab_)
                            act(iS[:, o0:o0 + n], pu[:, 1, 0:n], AF.Sigmoid, [Rpu[1], Rvec], [Rupre], scale=1.0, bias=gxb_)
                        pb_ = pcnt[0] % 2
                        pcnt[0] += 1
                        A_, Q_ = pa[:, pb_, 0:np_], pq[:, pb_, 0:np_]
                        Rpa, Rpq, Rhsb = Rpa2[pb_], Rpq2[pb_], Rhsb2[pb_]
                        act(A_, rS, AF.Exp, [Rupre, Rlv], [Rpa], scale=nsp)
                        act(Q_, rS, AF.Exp, [Rupre, Rlv], [Rpq], scale=nsp2)
                        act(Q_, Q_, AF.Ln, [Rpq], [Rpq], scale=-1.0, bias=cst[:, 1:2])
                        act(Q_, Q_, AF.Exp, [Rpq], [Rpq], scale=0.5)
                        tt(DVE, iS, iS, uu[:, p0:p1], ALU.mult, [Rupre, Ruu], [Rupre])
                        tt(DVE, Q_, Q_, iS, ALU.mult, [Rpq, Rupre], [Rpq])
                        if d == 0:
                            if p0 == 0:
                                dst = hsb[:, pb_, 0:np_]; Rd = Rhsb
                            else:
                                dst = hsf[:, p0 - 256:p1 - 256]; Rd = Rhsf
                            init = 0.0 if pi_ == 0 else carry[:, 0:1]
                            op(DVE, lambda: nc.vector.tensor_tensor_scan(out=dst, data0=A_, data1=Q_, initial=init,
                                                                        op0=ALU.mult, op1=ALU.add), [Rpa, Rpq, Rcar], [Rd])
                            cp(DVE, carry[:, 0:1], dst[:, np_ - 1:np_], [Rd], [Rcar])
                        else:
                            dst = hsb[:, pb_, 0:np_]; Rd = Rhsb
                            init = 0.0 if pi_ == 0 else carry[:, 1:2]
                            op(DVE, lambda: nc.vector.tensor_tensor_scan(out=dst[:, ::-1], data0=A_[:, ::-1], data1=Q_[:, ::-1],
                                                                        initial=init, op0=ALU.mult, op1=ALU.add),
                               [Rpa, Rpq, Rcar], [Rd])
                            cp(DVE, carry[:, 1:2], dst[:, 0:1], [Rd], [Rcar])
                            if p0 >= 256:
                                l0_, l1_ = p0 - 256, p1 - 256
                                tt(DVE, dst, dst, hsf[:, l0_:l1_], ALU.add, [Rd, Rhsf], [Rd])
                                tt(POOL, yb[:, l0_:l1_], dst, gb[:, l0_:l1_], ALU.mult, [Rd, Rgb], [Ryb])
                        if d == 0:
                            drain(6)
                        elif pi_ == 0:
                            drain(len(rpend))
                            if c + 1 < KC:
                                load_chunk(c + 1)
                pws, Rpws = [pwol, pwol2], Rpw2

                def _rstep(tl, half, b=b):
                    tq = 2 + tl
                    c0 = half * 512
                    mm(pws[half][:, 0:512], yb[:, tl * 128:(tl + 1) * 128], woc[:, b, c0:c0 + 512], True, True,
                       [Ryb, Rwoc[b]], [Rpws[half]])
                    tt(DVE, x_tok[:, tq, c0:c0 + 512], pws[half][:, 0:512], x_tok[:, tq, c0:c0 + 512], ALU.add,
                       [Rpws[half], Rx[tq]], [Rx[tq]])
                for tl in range(16):
                    for half in range(2):
                        rpend.append(lambda tl=tl, half=half, f=_rstep: f(tl, half))
            drain(len(rpend))
            fw.barrier()

    def phase_final(tiles_src=None):
        with ExitStack() as _es:
            fnb = _es.enter_context(_sbt("fnb", [128, D], F32))
            osb = _es.enter_context(_sbt("osb", [128, 2, D], F32))
            junk = _es.enter_context(_sbt("junk", [128, D], BF16))
            Rfn, Ros = Reg(), regs(2)
            dma(SP, fnb[:], W["final_norm"].partition_broadcast(128), (), [Rfn])
            for t in range(2, NT):
                act(junk[:], x_tok[:, t, :], AF.Square, [Rx[t]], [Rjunk, Rst], accum_out=ss[:, t:t + 1])
            act(lnv[:, 2:NT], ss[:, 2:NT], AF.Ln, [Rst], [Rst], scale=1.0 / D, bias=cst[:, 0:1])
            act(rstd[:, 2:NT], lnv[:, 2:NT], AF.Exp, [Rst], [Rst], scale=-0.5)
            for t in range(2, NT):
                o = osb[:, t % 2, :]
                stt(o, x_tok[:, t, :], rstd[:, t:t + 1], fnb[:], ALU.mult, ALU.mult, [Rx[t], Rst, Rfn], [Ros[t % 2]])
                dma(SP, out_d[(t - 2) * 128:(t - 1) * 128, :], o, [Ros[t % 2]], ())
            fw.barrier()

    def dump_x():
        print("COUNTS", {e.name: e.count for e in fw.all}, fw.ninst)
        for t in range(2, NT):
            dma(SP, out_d[(t - 2) * 128:(t - 1) * 128, :], x_tok[:, t, :], [Rx[t]], ())
        fw.barrier()

    phase_hT(0, 0, G_ALL)
    if stop == 1:
        dump_x(); return nc
    phase_attn()
    if stop == 2:
        dump_x(); return nc
    phase_hT(0, 1, G_ALL)
    phase_ffn(0, True)
    if stop == 3:
        dump_x(); return nc
    phase_hT(1, 0, G_ALL)
    phase_lru()
    if stop == 4:
        dump_x(); return nc
    phase_hT(1, 1, G_LAT)
    phase_ffn(1, False)
    if stop == 5:
        dump_x(); return nc
    phase_final()
    return nc


def _rope_tables():
    pairs = 16
    inv = (10000.0 ** (-np.arange(pairs, dtype=np.float32) / pairs)).astype(np.float32)
    row = np.repeat(np.arange(32, dtype=np.float32), 64)
    col = np.tile(np.arange(64, dtype=np.float32), 32)
    ang = np.concatenate([row[:, None] * inv, col[:, None] * inv], axis=-1).astype(np.float32)
    return np.cos(ang).astype(np.float32), np.sin(ang).astype(np.float32)


def kernel(_stop=99, **inputs):
    nc = build(_stop)
    cos, sin = _rope_tables()
    shared = {n: np.ascontiguousarray(np.asarray(inputs[n], dtype=np.float32)) for n, _ in WEIGHT_SPECS}
    shared["c_ctx"] = np.ascontiguousarray(np.asarray(inputs["c_ctx"], dtype=np.float32))
    shared["rope_cos"] = cos
    shared["rope_sin"] = sin
    x = np.asarray(inputs["x"], dtype=np.float32)
    c = np.asarray(inputs["c"], dtype=np.float32)
    ctx = np.asarray(inputs["ctx"], dtype=np.float32)
    in_maps = []
    for b in range(8):
        m = dict(shared)
        m["x"] = np.ascontiguousarray(x[b]); m["ctx"] = np.ascontiguousarray(ctx[b]); m["c"] = np.ascontiguousarray(c[b])
        in_maps.append(m)
    res = run_bass_kernel_spmd(nc, in_maps, core_ids=list(range(8)))
    return np.stack([np.asarray(r["out"], dtype=np.float32) for r in res.results], axis=0)
```
